# Optimizing a Trainium2 kernel written in Bass

```python
import math
import jax
import jax.numpy as jnp
from jax import lax
import numpy as np

D_MODEL = 1024
BATCH = 2
SEQ = 8192
DEPTH = 4
DEC_BATCH = 128
DEC_SEQ = 1
PAST_LEN = 2048
PAGE_SIZE = 128

D_MIX = D_MODEL
N_MIXERS = 4
W_GROUP = D_MIX // N_MIXERS
N_PROJ_BLOCKS = 10
D_IN = N_PROJ_BLOCKS * W_GROUP

POOL_WINDOWS = (2, 4, 8, 16)
POOL_GROUPS = len(POOL_WINDOWS)
POOL_CH = W_GROUP // POOL_GROUPS
POOL_BUF = max(POOL_WINDOWS) - 1

HG_HEADS = 4
HG_DK = W_GROUP // HG_HEADS
HG_CHUNK = 64

DA_HEADS = 4
DA_DV = W_GROUP // DA_HEADS
DA_HALF = DA_DV // 2
DA_DK = 2 * DA_HALF
Q_BLOCK = 128
REL_BUCKETS = 32
REL_MAX_DIST = 128

SG_HEADS = 4
SG_CH = W_GROUP // SG_HEADS
SG_CHUNK = 128

D_FF = 2816
ALPHA = (2.0 * DEPTH) ** 0.25
BETA = (8.0 * DEPTH) ** -0.25
EPS = 1e-5

kernel_name = 'hybrid_pool_hgrn2_diffattn_sgu_decoder_step'


def layer_norm(x, g, b):
    xf = x.astype(jnp.float32)
    mu = jnp.mean(xf, -1, keepdims=True)
    var = jnp.mean(jnp.square(xf - mu), -1, keepdims=True)
    return ((xf - mu) * lax.rsqrt(var + EPS)).astype(x.dtype) * g + b


def rms_norm(x, g):
    xf = x.astype(jnp.float32)
    return (xf * lax.rsqrt(jnp.mean(xf * xf, -1, keepdims=True) + EPS)).astype(x.dtype) * g


def swiglu(x, w_gu, w_dn):
    h = jnp.einsum('ntd,df->ntf', x, w_gu)
    gate, up = jnp.split(h, 2, axis=-1)
    return jnp.einsum('ntf,fd->ntd', jax.nn.silu(gate) * up, w_dn)


def pool_mix(a_ext, pos, w, scale):
    n, t = a_ext.shape[0], pos.shape[0]
    cs = jnp.cumsum(a_ext.astype(jnp.float32), axis=1)
    cs = jnp.pad(cs, ((0, 0), (1, 0), (0, 0)))
    end = cs[:, POOL_BUF + 1:]
    x_self = a_ext[:, POOL_BUF:].astype(jnp.float32)
    parts = []
    for g, win in enumerate(POOL_WINDOWS):
        ch = slice(g * POOL_CH, (g + 1) * POOL_CH)
        start = cs[:, POOL_BUF + 1 - win:POOL_BUF + 1 - win + t, ch]
        cnt = jnp.minimum(pos + 1, win).astype(jnp.float32)[None, :, None]
        parts.append((end[..., ch] - start) / cnt - x_self[..., ch])
    d = jnp.stack(parts, axis=2).astype(w.dtype)
    y = jnp.einsum('ntgc,gce->ntge', d, w).reshape(n, t, W_GROUP)
    return y * scale


def hgrn_chunked(q, k, v, logf, s0):
    n, t, h, _ = q.shape
    dv = v.shape[-1]
    c = min(HG_CHUNK, t)
    n_chunks = -(-t // c)
    pad = n_chunks * c - t

    def chunks(z):
        z = jnp.pad(z.astype(jnp.float32), ((0, 0), (0, pad), (0, 0), (0, 0)))
        return z.reshape(n, n_chunks, c, h, z.shape[-1]).transpose(1, 0, 3, 2, 4)

    causal = jnp.tril(jnp.ones((c, c), dtype=bool))[:, :, None]

    def step(s, inp):
        qc, kc, vc, lc = inp
        b = jnp.cumsum(lc, axis=2)
        rel = jnp.exp(jnp.where(causal, b[:, :, :, None, :] - b[:, :, None, :, :], -jnp.inf))
        att = jnp.einsum('nhtd,nhsd,nhtsd->nhts', qc, kc, rel)
        o = jnp.einsum('nhts,nhsv->nhtv', att, vc) + jnp.einsum('nhtd,nhdv->nhtv', qc * jnp.exp(b), s)
        b_last = b[:, :, -1:, :]
        s = jnp.exp(b_last[:, :, 0, :])[..., None] * s + jnp.einsum('nhsd,nhsv->nhdv', kc * jnp.exp(b_last - b), vc)
        return s, o

    s_fin, o = lax.scan(step, s0.astype(jnp.float32), (chunks(q), chunks(k), chunks(v), chunks(logf)))
    o = o.transpose(1, 0, 3, 2, 4).reshape(n, n_chunks * c, h, dv)[:, :t]
    return o, s_fin


def rel_bucket(dist):
    n = jnp.maximum(dist, 0)
    max_exact = REL_BUCKETS // 2
    large = max_exact + (jnp.log(jnp.maximum(n, 1).astype(jnp.float32) / max_exact)
                         / math.log(REL_MAX_DIST / max_exact) * (REL_BUCKETS - max_exact)).astype(jnp.int32)
    large = jnp.minimum(large, REL_BUCKETS - 1)
    return jnp.where(n < max_exact, n, large)


def diff_attend(q, k, v, q_pos, k_pos, rel_bias, lam):
    s = jnp.einsum('nqhmd,nkhmd->nhmqk', q, k).astype(jnp.float32) * (DA_HALF ** -0.5)
    dist = q_pos[:, None] - k_pos[None, :]
    bias = jnp.transpose(rel_bias[rel_bucket(dist)], (2, 0, 1)).astype(jnp.float32)
    s = jnp.where((dist >= 0)[None, None, None], s + bias[None, :, None], -jnp.inf)
    p = jax.nn.softmax(s, axis=-1)
    a = p[:, :, 0] - lam * p[:, :, 1]
    return jnp.einsum('nhqk,nkhv->nqhv', a.astype(v.dtype), v)


def diff_attn_prompt(q, k, v, rel_bias, lam):
    n, t = q.shape[:2]
    qb_len = min(Q_BLOCK, t)
    nb = t // qb_len
    qb = q.reshape(n, nb, qb_len, DA_HEADS, 2, DA_HALF).swapaxes(0, 1)
    k_pos = jnp.arange(t, dtype=jnp.int32)

    def one(args):
        q_blk, bi = args
        q_pos = bi * qb_len + jnp.arange(qb_len, dtype=jnp.int32)
        return diff_attend(q_blk, k, v, q_pos, k_pos, rel_bias, lam)

    o = lax.map(one, (qb, jnp.arange(nb, dtype=jnp.int32)))
    return o.swapaxes(0, 1).reshape(n, t, DA_HEADS, DA_DV)


def sgu(u, v, ln_gain, ln_bias, w_s, b_s):
    n, t, c = v.shape
    vn = layer_norm(v, ln_gain, ln_bias)
    L = min(t, SG_CHUNK)
    pad = (-t) % L
    n_chunks = (t + pad) // L
    vc = jnp.pad(vn, ((0, 0), (0, pad), (0, 0))).reshape(n, n_chunks, L, SG_HEADS, SG_CH)
    w = w_s[:, :L, :L] * jnp.tril(jnp.ones((L, L), w_s.dtype))
    s = jnp.einsum('gts,ncsgd->nctgd', w, vc) + b_s[:, :L].T[None, None, :, :, None]
    s = s.reshape(n, n_chunks * L, c)[:, :t]
    return u * s, vn


def mix_sublayer(x, pos, pool_prefix, hg_s0, kv_past, lp):
    n, t, _ = x.shape
    p = jnp.einsum('ntd,de->nte', x, lp['w_in'])
    a, hq, hf, hi, hg, dq, dk, dv, su, sv = jnp.split(p, N_PROJ_BLOCKS, axis=-1)
    a_ext = jnp.concatenate([pool_prefix.astype(a.dtype), a], axis=1)
    y_a = pool_mix(a_ext, pos, lp['pool_w'], lp['pool_scale'])
    new_pool = a_ext[:, -POOL_BUF:]
    lb = lp['lb']
    z = hf.astype(jnp.float32)
    logf = jnp.logaddexp(jnp.log(lb), jnp.log1p(-lb) + jax.nn.log_sigmoid(z))
    kin = (1.0 - lb) * jax.nn.sigmoid(-z)
    heads = lambda z_: z_.reshape(n, t, HG_HEADS, HG_DK)
    o_b, new_s = hgrn_chunked(heads(hq), heads(kin), heads(hi), heads(logf), hg_s0)
    y_b = rms_norm(o_b.astype(x.dtype), lp['hg_g']).reshape(n, t, W_GROUP) * jax.nn.silu(hg)
    q = dq.reshape(n, t, DA_HEADS, 2, DA_HALF)
    k = dk.reshape(n, t, DA_HEADS, 2, DA_HALF)
    v = dv.reshape(n, t, DA_HEADS, DA_DV)
    if kv_past is None:
        o_c = diff_attn_prompt(q, k, v, lp['rel_bias'], lp['lam'])
    else:
        k_past, v_past = kv_past
        n_past = k_past.shape[1]
        k_all = jnp.concatenate([k_past.reshape(n, n_past, DA_HEADS, 2, DA_HALF).astype(k.dtype), k], axis=1)
        v_all = jnp.concatenate([v_past.astype(v.dtype), v], axis=1)
        k_pos = jnp.arange(n_past + t, dtype=jnp.int32)
        o_c = diff_attend(q, k_all, v_all, pos, k_pos, lp['rel_bias'], lp['lam'])
    y_c = (rms_norm(o_c, lp['sub_g']) * (1.0 - lp['lam_init'])).reshape(n, t, W_GROUP)
    y_d, vn = sgu(su, sv, lp['sg_ln_g'], lp['sg_ln_b'], lp['sg_w'], lp['sg_b'])
    n_open = (t - 1) % SG_CHUNK + 1
    y = jnp.einsum('nte,ed->ntd', jnp.concatenate([y_a, y_b, y_c, y_d], axis=-1), lp['w_out'])
    state = (new_pool, new_s.astype(x.dtype), dk.reshape(n, t, DA_HEADS, DA_DK), v, vn[:, t - n_open:])
    return y, state


def trunk_layer(x, pos, pool_prefix, hg_s0, kv_past, lp):
    x = layer_norm(ALPHA * x + 0.5 * swiglu(x, lp['ffn1_w_gu'], lp['ffn1_w_dn']), lp['ln_g'][0], lp['ln_b'][0])
    y, state = mix_sublayer(x, pos, pool_prefix, hg_s0, kv_past, lp)
    x = layer_norm(ALPHA * x + y, lp['ln_g'][1], lp['ln_b'][1])
    x = layer_norm(ALPHA * x + 0.5 * swiglu(x, lp['ffn2_w_gu'], lp['ffn2_w_dn']), lp['ln_g'][2], lp['ln_b'][2])
    return x, state


def setup_inputs(seed: int = 0) -> dict:
    key = jax.random.key(seed)
    ks = jax.random.split(key, 32)
    n_pages = PAST_LEN // PAGE_SIZE
    n_pool = (DEC_BATCH * n_pages * 5) // 4
    f32 = jnp.float32
    nrm = lambda k, shape, s: jax.random.normal(k, shape, f32) * s
    page_table = jax.random.permutation(ks[6], n_pool)[:DEC_BATCH * n_pages].reshape(DEC_BATCH, n_pages).astype(jnp.int32)
    return {
        'x_prompt': nrm(ks[0], (BATCH, SEQ, D_MODEL), 1.0),
        'x_sample': nrm(ks[1], (DEC_BATCH, DEC_SEQ, D_MODEL), 1.0),
        'state_pool': nrm(ks[2], (DEPTH, DEC_BATCH, POOL_BUF, W_GROUP), 1.0),
        'state_hgrn': nrm(ks[3], (DEPTH, DEC_BATCH, HG_HEADS, HG_DK, HG_DK), 0.5),
        'cache_k': nrm(ks[4], (DEPTH, n_pool, PAGE_SIZE, DA_HEADS, DA_DK), 1.0),
        'cache_v': nrm(ks[5], (DEPTH, n_pool, PAGE_SIZE, DA_HEADS, DA_DV), 1.0),
        'page_table': page_table,
        'rel_bias': nrm(ks[7], (REL_BUCKETS, DA_HEADS), 0.5),
        'ln_g': 1.0 + nrm(ks[8], (DEPTH, 3, D_MODEL), 0.02),
        'ln_b': nrm(ks[9], (DEPTH, 3, D_MODEL), 0.02),
        'ffn1_w_gu': nrm(ks[10], (DEPTH, D_MODEL, 2 * D_FF), D_MODEL ** -0.5),
        'ffn1_w_dn': nrm(ks[11], (DEPTH, D_FF, D_MODEL), BETA * D_FF ** -0.5),
        'ffn2_w_gu': nrm(ks[12], (DEPTH, D_MODEL, 2 * D_FF), D_MODEL ** -0.5),
        'ffn2_w_dn': nrm(ks[13], (DEPTH, D_FF, D_MODEL), BETA * D_FF ** -0.5),
        'w_in': nrm(ks[14], (DEPTH, D_MODEL, D_IN), D_MODEL ** -0.5),
        'w_out': nrm(ks[15], (DEPTH, D_MIX, D_MODEL), BETA * D_MIX ** -0.5),
        'pool_w': nrm(ks[16], (DEPTH, POOL_GROUPS, POOL_CH, POOL_CH), POOL_CH ** -0.5),
        'pool_scale': 1.0 + nrm(ks[17], (DEPTH, W_GROUP), 0.1),
        'hgrn_lb': nrm(ks[18], (DEPTH, W_GROUP), 0.1),
        'hgrn_norm_g': 1.0 + nrm(ks[19], (DEPTH, HG_DK), 0.02),
        'diff_lam_q1': nrm(ks[20], (DEPTH, DA_HALF), 0.1),
        'diff_lam_k1': nrm(ks[21], (DEPTH, DA_HALF), 0.1),
        'diff_lam_q2': nrm(ks[22], (DEPTH, DA_HALF), 0.1),
        'diff_lam_k2': nrm(ks[23], (DEPTH, DA_HALF), 0.1),
        'diff_subln_g': 1.0 + nrm(ks[24], (DEPTH, DA_DV), 0.02),
        'sgu_ln_g': 1.0 + nrm(ks[25], (DEPTH, W_GROUP), 0.02),
        'sgu_ln_b': nrm(ks[26], (DEPTH, W_GROUP), 0.02),
        'sgu_w': nrm(ks[27], (DEPTH, SG_HEADS, SG_CHUNK, SG_CHUNK), SG_CHUNK ** -0.5),
        'sgu_b': 1.0 + nrm(ks[28], (DEPTH, SG_HEADS, SG_CHUNK), 0.1),
    }


def reference(x_prompt, x_sample, state_pool, state_hgrn, cache_k, cache_v, page_table,
              rel_bias, ln_g, ln_b, ffn1_w_gu, ffn1_w_dn, ffn2_w_gu, ffn2_w_dn, w_in, w_out,
              pool_w, pool_scale, hgrn_lb, hgrn_norm_g, diff_lam_q1, diff_lam_k1,
              diff_lam_q2, diff_lam_k2, diff_subln_g, sgu_ln_g, sgu_ln_b, sgu_w, sgu_b):
    f32 = jnp.float32
    bp, tp, _ = x_prompt.shape
    bs, ts, _ = x_sample.shape
    pos_p = jnp.arange(tp, dtype=jnp.int32)
    pos_s = PAST_LEN + jnp.arange(ts, dtype=jnp.int32)
    lb_cum = jnp.cumsum(jax.nn.softmax(hgrn_lb.astype(f32), axis=0), axis=0)
    lb_all = jnp.maximum(lb_cum - lb_cum[:1], 0.0)
    xp, xs = x_prompt, x_sample
    st_p = [[], [], [], [], []]
    st_s = [[], [], [], [], []]
    for l in range(DEPTH):
        lam_init = 0.8 - 0.6 * math.exp(-0.3 * l)
        lam = (jnp.exp(jnp.sum(diff_lam_q1[l].astype(f32) * diff_lam_k1[l].astype(f32)))
               - jnp.exp(jnp.sum(diff_lam_q2[l].astype(f32) * diff_lam_k2[l].astype(f32))) + lam_init)
        lp = {
            'w_in': w_in[l], 'w_out': w_out[l], 'ln_g': ln_g[l], 'ln_b': ln_b[l],
            'ffn1_w_gu': ffn1_w_gu[l], 'ffn1_w_dn': ffn1_w_dn[l],
            'ffn2_w_gu': ffn2_w_gu[l], 'ffn2_w_dn': ffn2_w_dn[l],
            'pool_w': pool_w[l], 'pool_scale': pool_scale[l],
            'lb': lb_all[l], 'hg_g': hgrn_norm_g[l],
            'rel_bias': rel_bias, 'lam': lam, 'lam_init': lam_init, 'sub_g': diff_subln_g[l],
            'sg_ln_g': sgu_ln_g[l], 'sg_ln_b': sgu_ln_b[l], 'sg_w': sgu_w[l], 'sg_b': sgu_b[l],
        }
        xp, sp = trunk_layer(xp, pos_p, jnp.zeros((bp, POOL_BUF, W_GROUP), xp.dtype),
                             jnp.zeros((bp, HG_HEADS, HG_DK, HG_DK), f32), None, lp)
        k_past = cache_k[l][page_table].reshape(bs, -1, DA_HEADS, DA_DK)
        v_past = cache_v[l][page_table].reshape(bs, -1, DA_HEADS, DA_DV)
        xs, ss = trunk_layer(xs, pos_s, state_pool[l], state_hgrn[l].astype(f32), (k_past, v_past), lp)
        for i in range(5):
            st_p[i].append(sp[i])
            st_s[i].append(ss[i])
    pool_p, hgrn_p, k_p, v_p, sgv_p = [jnp.stack(z, axis=0) for z in st_p]
    pool_s, hgrn_s, k_s, v_s, sgv_s = [jnp.stack(z, axis=0) for z in st_s]
    return (xp, xs, pool_p, pool_s, hgrn_p, hgrn_s, k_p, k_s, v_p, v_s, sgv_p, sgv_s)
```

```python
import bisect
import math
import numpy as np
import concourse.bass as bass
import concourse.mybir as mybir
from concourse.bass_utils import run_bass_kernel_spmd

F32 = mybir.dt.float32
BF16 = mybir.dt.bfloat16
I32 = mybir.dt.int32
AF = mybir.ActivationFunctionType
ALU = mybir.AluOpType
AX = mybir.AxisListType

D_MODEL = 1024
KC = 8
D_FF = 2816
FC = 22
FH = 11
D_IN = 2560
NS = 16
PAGE = 128
NPG = 16
PAST = 2048
EPS = 1e-5
NEG = -30000.0
QSCALE = 32 ** -0.5


class Ev:
    __slots__ = ("eng", "seq", "instr", "sem", "val", "dma")

    def __init__(self, eng, seq, instr, dma=False):
        self.eng = eng
        self.seq = seq
        self.instr = instr
        self.sem = None
        self.val = None
        self.dma = dma


class Builder:
    COMPUTE = ("pe", "dve", "act", "pool")

    def __init__(self, nc, n_dma_sems=48):
        self.nc = nc
        self.E = {"pe": nc.tensor, "dve": nc.vector, "act": nc.scalar, "pool": nc.gpsimd, "sp": nc.sync}
        self.csem = {e: nc.semaphore("c_" + e).__enter__() for e in self.COMPUTE}
        self.ccount = {e: 0 for e in self.COMPUTE}
        self.marked = {e: [] for e in self.COMPUTE}
        self.seq = {e: 0 for e in self.E}
        self.last = {}
        self.dsems = [nc.semaphore("d%d" % i).__enter__() for i in range(n_dma_sems)]
        self.dval = [0] * n_dma_sems
        self.dnext = 0
        self.dnext_sw = 0
        self.ccsem = nc.semaphore("ccsem").__enter__()
        self.ccval = 0
        self.waited = {}
        self.state = {}
        self.n_wait = 0
        self.n_ins = 0

    def _resolve(self, ev):
        if ev.val is not None:
            return ev.sem, ev.val
        e = ev.eng
        lst = self.marked[e]
        i = bisect.bisect_left(lst, (ev.seq, -1))
        if i < len(lst):
            return self.csem[e], lst[i][1]
        self.ccount[e] += 1
        ev.instr.then_inc(self.csem[e], 1)
        ev.sem = self.csem[e]
        ev.val = self.ccount[e]
        lst.append((ev.seq, ev.val))
        return ev.sem, ev.val

    def _wait(self, eng, sem, val):
        key = (eng, id(sem))
        if self.waited.get(key, 0) >= val:
            return
        self.E[eng].wait_ge(sem, val)
        self.waited[key] = val
        self.n_wait += 1

    def _deps(self, eng, reads, writes):
        deps = []
        for k in reads:
            st = self.state.get(k)
            if st:
                deps.extend(st[0])
                if k.startswith("ps"):
                    deps.extend(o for o in st[1] if o.eng != eng)
        for k in writes:
            st = self.state.get(k)
            if st:
                deps.extend(st[0])
                deps.extend(st[1])
        seen = set()
        for ev in deps:
            if id(ev) in seen:
                continue
            seen.add(id(ev))
            if ev.eng == eng and eng == "pe" and not ev.dma:
                continue
            sem, val = self._resolve(ev)
            self._wait(eng, sem, val)

    def _update(self, ev, reads, writes):
        for k in writes:
            self.state[k] = ([ev], [])
        for k in reads:
            st = self.state.get(k)
            if st is None:
                st = ([], [])
                self.state[k] = st
            rl = st[1]
            if not ev.dma:
                for i, o in enumerate(rl):
                    if (not o.dma) and o.eng == ev.eng:
                        rl[i] = ev
                        break
                else:
                    rl.append(ev)
            else:
                rl.append(ev)

    def op(self, eng, fn, reads=(), writes=()):
        self._deps(eng, reads, writes)
        instr = fn()
        ev = Ev(eng, self.seq[eng], instr)
        self.seq[eng] += 1
        self.last[eng] = ev
        self._update(ev, reads, writes)
        self.n_ins += 1
        return ev

    def dma(self, eng, fn, reads=(), writes=()):
        self._deps(eng, reads, writes)
        half = len(self.dsems) // 2
        if eng == "pool":
            i = half + self.dnext_sw
            self.dnext_sw = (self.dnext_sw + 1) % (len(self.dsems) - half)
        else:
            i = self.dnext
            self.dnext = (self.dnext + 1) % half
        self._wait(eng, self.dsems[i], self.dval[i])
        instr = fn()
        self.dval[i] += 16
        instr.then_inc(self.dsems[i], 16)
        ev = Ev(eng, self.seq[eng], instr, dma=True)
        self.seq[eng] += 1
        ev.sem = self.dsems[i]
        ev.val = self.dval[i]
        self._update(ev, reads, writes)
        self.n_ins += 1
        return ev

    def cc(self, fn, reads=(), writes=()):
        eng = "pool"
        self._deps(eng, reads, writes)
        instr = fn()
        self.ccval += 1
        instr.then_inc(self.ccsem)
        ev = Ev(eng, self.seq[eng], instr, dma=True)
        self.seq[eng] += 1
        ev.sem = self.ccsem
        ev.val = self.ccval
        self._update(ev, reads, writes)
        return ev

    def barrier(self):
        evs = [self.last[e] for e in self.COMPUTE if e in self.last]
        res = [self._resolve(ev) for ev in evs]
        for y in self.E:
            for (sem, val) in res:
                self._wait(y, sem, val)
            for i, s in enumerate(self.dsems):
                if self.dval[i]:
                    self._wait(y, s, self.dval[i])
            if self.ccval:
                self._wait(y, self.ccsem, self.ccval)
        self.state = {}

    def finish(self):
        self.barrier()


class Arena:
    def __init__(self, nc, words):
        self.t = nc.sbuf_tensor("arena", [128, words], F32).__enter__()
        self.words = words
        self.top = 0
        self.peak = 0

    def alloc(self, shape, dtype=F32):
        n = 1
        for s in shape:
            n *= s
        w = n if dtype in (F32, I32) else (n + 1) // 2
        a = self.top
        self.top += w
        self.peak = max(self.peak, self.top)
        assert self.top <= self.words, ("arena overflow", self.top, self.words)
        v = self.t[:, a:a + w]
        if dtype == BF16:
            v = v.bitcast(BF16)
            if 2 * w != n:
                v = v[:, :n]
        elif dtype == I32:
            v = v.bitcast(I32)
        if len(shape) == 2:
            v = v.rearrange("p (a b) -> p a b", a=shape[0], b=shape[1])
        elif len(shape) == 3:
            v = v.rearrange("p (a b c) -> p a b c", a=shape[0], b=shape[1], c=shape[2])
        elif len(shape) == 4:
            v = v.rearrange("p (a b c d) -> p a b c d", a=shape[0], b=shape[1], c=shape[2], d=shape[3])
        return v

    def mark(self):
        return self.top

    def release(self, m):
        self.top = m


def rel_bucket_np(dist):
    n = np.maximum(dist, 0)
    max_exact = 16
    large = max_exact + (np.log(np.maximum(n, 1).astype(np.float32) / np.float32(max_exact))
                         / np.float32(math.log(128 / max_exact)) * np.float32(32 - max_exact)).astype(np.int32)
    large = np.minimum(large, 31)
    return np.where(n < max_exact, n, large)


class StopBuild(Exception):
    pass


class Prog:
    def __init__(self, TL, DEPTH, n_pool_pages, stages=99, dbg=False):
        self.TL = TL
        self.TT = TL + NS
        self.DEPTH = DEPTH
        self.NB = TL // 128
        self.NT = TL // 512
        self.NCH = TL // 64
        self.n_pool_pages = n_pool_pages
        self.stages = stages
        self.dbg = dbg
        self.tiles = [(i * 512, 512) for i in range(self.NT)] + [(TL, NS)]
        self.ptiles = self.tiles[:-1]
        self.ALPHA = (2.0 * DEPTH) ** 0.25
        nc = bass.Bass("TRN2", target_bir_lowering=False)
        self.nc = nc
        self.B = Builder(nc)
        self.ar = Arena(nc, 52000)
        self.PS = [nc.psum_tensor("ps%d" % i, [128, 512], F32).__enter__() for i in range(8)]
        self.outs = []
        self.declare_io()
        self.build()

    def din(self, name, shape, dt=F32):
        return self.nc.dram_tensor(name, list(shape), dt, kind="ExternalInput").ap()

    def dout(self, name, shape, dt=F32):
        self.outs.append(name)
        return self.nc.dram_tensor(name, list(shape), dt, kind="ExternalOutput").ap()

    def declare_io(self):
        TL, L = self.TL, self.DEPTH
        self.i_xp = self.din("xp", [TL, D_MODEL])
        self.i_xs = self.din("xs", [NS, D_MODEL])
        self.i_spool = self.din("spool", [L, NS, 15 * 256])
        self.i_shgrn = self.din("shgrn", [L, NS, 4, 4096])
        self.i_ck = [self.din("cache_k%d" % l, [self.n_pool_pages * 8, 4096]) for l in range(L)]
        self.i_cv = [self.din("cache_v%d" % l, [self.n_pool_pages * 8, 4096]) for l in range(L)]
        self.i_pt = self.din("pt", [NS, NPG], I32)
        self.i_relb = self.din("rel_bias", [1, 128])
        self.i_lng = self.din("ln_g", [L * 3 * KC, 128])
        self.i_lnb = self.din("ln_b", [L * 3 * KC, 128])
        self.i_wgu = [self.din("ffn1_w_gu", [L, D_MODEL, 2 * D_FF]), self.din("ffn2_w_gu", [L, D_MODEL, 2 * D_FF])]
        self.i_wdn = [self.din("ffn1_w_dn", [L, D_FF, D_MODEL]), self.din("ffn2_w_dn", [L, D_FF, D_MODEL])]
        self.i_win = self.din("w_in", [L, D_MODEL, D_IN])
        self.i_wout = self.din("w_out", [L, D_MODEL, D_MODEL])
        self.i_poolw = self.din("pool_w", [L, 4, 64, 64])
        self.i_pscale = self.din("pool_scale", [L, 256])
        self.i_lb = self.din("hgrn_lb", [1, L * 256])
        self.i_hgg = self.din("hgrn_norm_g", [L, 64])
        self.i_lam = [self.din(n, [L, 32]) for n in ("diff_lam_q1", "diff_lam_k1", "diff_lam_q2", "diff_lam_k2")]
        self.i_subg = self.din("diff_subln_g", [L, 64])
        self.i_sglg = self.din("sgu_ln_g", [L, 256])
        self.i_sglb = self.din("sgu_ln_b", [L, 256])
        self.i_sgw = self.din("sgu_w", [L, 4, 128, 128])
        self.i_sgb = self.din("sgu_b", [L, 4, 128])
        self.i_ident = self.din("c_ident", [128, 128])
        self.i_maskc = self.din("c_maskc", [128, 128])
        self.i_idx01 = self.din("c_idx01", [128, 256])
        self.i_idxs = self.din("c_idxs", [128, 16])
        self.i_segm = self.din("c_segm", [128, 512])
        self.i_mask2 = self.din("c_mask2", [128, 128])
        self.i_ccore = self.din("c_core", [1, 32])
        self.i_invc = self.din("c_invc", [128, 32])
        self.i_sel16 = self.din("c_sel16", [16, 16 * 128])
        self.i_misc = self.din("c_misc", [128, 64])
        self.i_c8 = self.din("c_c8", [8, 514])
        self.o_y = self.dout("o_y", [TL, D_MODEL])
        self.o_ys = self.dout("o_ys", [NS, D_MODEL])
        self.o_poolp = self.dout("o_poolp", [L, 15, 256])
        self.o_pools = self.dout("o_pools", [L, NS, 15 * 256])
        self.o_hgp = self.dout("o_hgp", [L, 4, 64, 64])
        self.o_hgs = self.dout("o_hgs", [L, NS, 4, 4096])
        self.o_kp = self.dout("o_kp", [L, TL, 256])
        self.o_ks = self.dout("o_ks", [L, NS, 256])
        self.o_vp = self.dout("o_vp", [L, TL, 256])
        self.o_vs = self.dout("o_vs", [L, NS, 256])
        self.o_sgp = self.dout("o_sgp", [L, 128, 256])
        self.o_sgs = self.dout("o_sgs", [L, NS, 256])
        nq = 128 * TL
        self.nq = nq
        self.g_kv_in = [[self.nc.dram_tensor("gkvi%d_%d" % (l, q), [nq // 512, 512], BF16) for q in range(4)] for l in range(L)]
        self.g_kv_out = [[self.nc.dram_tensor("gkvo%d_%d" % (l, q), [4 * nq // 512, 512], BF16) for q in range(4)] for l in range(L)]
        self.g_sm_in = [[self.nc.dram_tensor("gsmi%d_%d" % (l, g), [128, 161], F32) for g in range(2)] for l in range(L)]
        self.g_sm_out = [[self.nc.dram_tensor("gsmo%d_%d" % (l, g), [512, 161], F32) for g in range(2)] for l in range(L)]

    def mm(self, out, lhsT, rhs, start, stop, reads, writes):
        nc = self.nc
        return self.B.op("pe", lambda: nc.tensor.matmul(out, lhsT=lhsT, rhs=rhs, start=start, stop=stop),
                         reads=reads, writes=writes)

    def tr(self, out, in_, ident, reads, writes):
        nc = self.nc
        return self.B.op("pe", lambda: nc.tensor.transpose(out, in_, ident), reads=reads, writes=writes)

    def act(self, out, in_, func, reads, writes, scale=1.0, bias=0.0):
        nc = self.nc
        return self.B.op("act", lambda: nc.scalar.activation(out=out, in_=in_, func=func, scale=scale, bias=bias),
                         reads=reads, writes=writes)

    def tt(self, eng, out, in0, in1, op, reads, writes):
        e = self.B.E[eng]
        return self.B.op(eng, lambda: e.tensor_tensor(out=out, in0=in0, in1=in1, op=op), reads=reads, writes=writes)

    def ts(self, eng, out, in0, s1, op0, reads, writes, s2=None, op1=None):
        e = self.B.E[eng]
        if op1 is None:
            return self.B.op(eng, lambda: e.tensor_scalar(out=out, in0=in0, scalar1=s1, scalar2=None, op0=op0),
                             reads=reads, writes=writes)
        return self.B.op(eng, lambda: e.tensor_scalar(out=out, in0=in0, scalar1=s1, scalar2=s2, op0=op0, op1=op1),
                         reads=reads, writes=writes)

    def stt(self, out, in0, scalar, in1, op0, op1, reads, writes):
        nc = self.nc
        return self.B.op("dve", lambda: nc.vector.scalar_tensor_tensor(out=out, in0=in0, scalar=scalar, in1=in1,
                                                                       op0=op0, op1=op1), reads=reads, writes=writes)

    def cp(self, eng, out, in_, reads, writes):
        e = self.B.E[eng]
        if eng == "act":
            return self.act(out, in_, AF.Copy, reads, writes)
        return self.B.op(eng, lambda: e.tensor_copy(out=out, in_=in_), reads=reads, writes=writes)

    def memset(self, eng, out, val, writes):
        e = self.B.E[eng]
        return self.B.op(eng, lambda: e.memset(out, val), writes=writes)

    def ld(self, out, in_, writes, reads=(), eng="sp", **kw):
        e = self.B.E[eng]
        return self.B.dma(eng, lambda: e.dma_start(out=out, in_=in_, **kw), reads=reads, writes=writes)

    def st(self, out, in_, reads, writes=(), eng="sp", **kw):
        e = self.B.E[eng]
        return self.B.dma(eng, lambda: e.dma_start(out=out, in_=in_, **kw), reads=reads, writes=writes)

    def ws_init(self):
        self.ws_nbuf = 4
        self.ws_depth = 3
        self.ws_bufs = [self.ar.alloc([2048], BF16) for _ in range(self.ws_nbuf)]
        self.ws_specs = []
        self.ws_issued = 0
        self.ws_cur = 0
        self.ws_rel = 0

    def ws_issue_upto(self, n):
        n = min(n, len(self.ws_specs))
        while self.ws_issued < n:
            i = self.ws_issued
            tag, parts = self.ws_specs[i]
            b = i % self.ws_nbuf
            for (off, shape, src) in parts:
                nel = 1
                for s in shape[1:]:
                    nel *= s
                dst = self.ws_bufs[b][:shape[0], off:off + nel]
                if len(shape) == 3:
                    dst = dst.rearrange("p (a b) -> p a b", a=shape[1], b=shape[2])
                elif len(shape) == 4:
                    dst = dst.rearrange("p (a b c) -> p a b c", a=shape[1], b=shape[2], c=shape[3])
                self.ld(dst, src, writes=["ws%d" % b], eng="pool")
            self.ws_issued += 1

    def ws_next(self, tag, hold=False):
        i = self.ws_cur
        t, parts = self.ws_specs[i]
        assert t == tag, (t, tag, i)
        if not hold:
            self.ws_rel = i
        self.ws_issue_upto(self.ws_rel + self.ws_nbuf)
        assert self.ws_issued > i, "holding more slabs than ring slots"
        self.ws_cur += 1
        b = i % self.ws_nbuf
        return self.ws_bufs[b], "ws%d" % b

    def ws_plan(self):
        specs = []
        for l in range(self.DEPTH):
            for which in (0, 1):
                if which == 1 and self.stages >= 2:
                    wv = self.i_win[l].rearrange("(k p) e -> p k e", p=128)
                    order = []
                    if self.stages >= 4:
                        order += list(range(20))
                    order += [4, 2, 8, 6, 18, 19, 16, 17, 0, 1, 5, 3, 9, 7, 12, 13, 14, 15, 10, 11]
                    for e in order:
                        specs.append((("win", l, e), [(0, [128, KC, 128], wv[:, :, e * 128:(e + 1) * 128])]))
                    wo = self.i_wout[l].rearrange("(k p) d -> p k d", p=128)
                    for dc in range(KC):
                        specs.append((("wout", l, dc), [(0, [128, KC, 128], wo[:, :, dc * 128:(dc + 1) * 128])]))
                wgu = self.i_wgu[which][l].rearrange("(k p) f -> p k f", p=128)
                wdn = self.i_wdn[which][l].rearrange("(f p) d -> p f d", p=128)
                for fh in range(2):
                    for fc in range(FH):
                        f = fh * FH + fc
                        specs.append((("gu", l, which, f),
                                      [(0, [128, KC, 128], wgu[:, :, f * 128:(f + 1) * 128]),
                                       (1024, [128, KC, 128], wgu[:, :, D_FF + f * 128:D_FF + (f + 1) * 128])]))
                    for dc in range(KC):
                        specs.append((("dn", l, which, fh, dc),
                                      [(0, [128, FH, 128], wdn[:, fh * FH:(fh + 1) * FH, dc * 128:(dc + 1) * 128])]))
        self.ws_specs = specs

    def build(self):
        import os
        self.const_setup()
        self.ws_init()
        self.ws_plan()
        if os.environ.get("KSKIP", "") == "load":
            self.st(self.o_y[0:128, 0:128], self.identF, reads=["identF"])
            self.B.finish()
            return
        self.load_x()
        if os.environ.get("KSKIP", "") == "store":
            self.st(self.o_y[0:128, :], self.xres[:, 0, 0:1024] if self.TL >= 1024 else self.xres[:, 0:2, 0:512], reads=[])
            self.B.finish()
            return
        for l in range(self.DEPTH):
            if self.stages < 1:
                break
            self.ffn(l, 0, last=(self.stages < 2))
            if self.stages >= 2:
                try:
                    self.mixer(l)
                except StopBuild:
                    self.B.barrier()
                    break
                self.ffn(l, 1, last=(l == self.DEPTH - 1))
        self.store_y()
        self.B.finish()

    def const_setup(self):
        ar, L = self.ar, self.DEPTH
        self.identF = ar.alloc([128])
        self.ld(self.identF, self.i_ident[:, :], writes=["identF"])
        self.identB = ar.alloc([128], BF16)
        self.cp("dve", self.identB, self.identF, ["identF"], ["identB"])
        self.onesF = ar.alloc([128])
        self.memset("dve", self.onesF, 1.0 / D_MODEL, ["onesF"])
        self.xres = ar.alloc([KC, self.TT + 112])
        a0 = ar.top
        self.xbf = ar.alloc([KC, self.TT + 112], BF16)
        self.xbf_words = (a0, ar.top)
        self.memset("dve", self.xres, 0.0, ["xres_init"])
        self.memset("pool", self.xbf, 0.0, ["xbf_init"])
        n = L * 3 * KC
        self.lng = ar.alloc([n])
        self.lnb = ar.alloc([n])
        self.lnag = ar.alloc([n])
        self.lnab = ar.alloc([n])
        m = ar.mark()
        tmp = ar.alloc([128])
        import os
        for (src, dst, key) in (() if os.environ.get("SKIPLN") else ((self.i_lng, self.lng, "lng"), (self.i_lnb, self.lnb, "lnb"))):
            self.memset("dve", tmp, 0.0, ["ptmp"])
            self.ld(tmp[:n, :], src[:, :], writes=["ptmp"])
            self.tr(self.PS[0][:, :128], tmp, self.identF, ["ptmp", "identF"], ["ps0"])
            self.cp("dve", dst, self.PS[0][:, :n], ["ps0"], [key])
        self.ts("dve", self.lnag, self.lng, self.ALPHA, ALU.mult, ["lng"], ["lnag"])
        self.ts("dve", self.lnab, self.lnb, self.ALPHA, ALU.mult, ["lnb"], ["lnab"])
        self.B.barrier()
        ar.release(m)
        self.B.barrier()
        self.mask2 = ar.alloc([128])
        self.ld(self.mask2, self.i_mask2[:, :], writes=["mask2"])
        self.attn_consts()
        self.sample_consts()

    def lncol(self, l, i, k):
        return (l * 3 + i) * KC + k

    def load_x(self):
        ar = self.ar
        m = ar.mark()
        xin = [ar.alloc([D_MODEL]) for _ in range(2)]
        import os
        for blk in range(self.NB + (0 if os.environ.get("NOSAMPLE") else 1)):
            b = blk % 2
            if blk < self.NB:
                rows, t0 = 128, blk * 128
                self.ld(xin[b], self.i_xp[t0:t0 + 128, :], writes=["xin%d" % b])
            else:
                rows, t0 = NS, self.TL
                self.memset("dve", xin[b], 0.0, ["xin%d" % b])
                self.ld(xin[b][:NS, :], self.i_xs[:, :], writes=["xin%d" % b])
            tt = min(t0 // 512, self.NT)
            for half in range(2):
                ps = self.PS[half]
                psf = ps[:, :].rearrange("p (k t) -> p k t", k=4)
                psv = psf[:, :, :rows]
                for kk in range(4):
                    k = half * 4 + kk
                    self.tr(psf[:, kk, :], xin[b][:, k * 128:(k + 1) * 128], self.identF,
                            ["xin%d" % b, "identF"], ["ps%d" % half])
                ks = ["x.%d.%d" % (half * 4 + kk, tt) for kk in range(4)]
                if os.environ.get("EV2D"):
                    for kk in range(4):
                        k = half * 4 + kk
                        self.act(self.xres[:, k, t0:t0 + rows], psv[:, kk, :], AF.Identity, ["ps%d" % half],
                                 ["r" + ks[kk]], scale=self.ALPHA)
                        self.cp("dve", self.xbf[:, k, t0:t0 + rows], psv[:, kk, :], ["ps%d" % half], ["b" + ks[kk]])
                    continue
                self.act(self.xres[:, half * 4:half * 4 + 4, t0:t0 + rows], psv, AF.Identity, ["ps%d" % half],
                         ["r" + s for s in ks], scale=self.ALPHA)
                self.cp("dve", self.xbf[:, half * 4:half * 4 + 4, t0:t0 + rows], psv, ["ps%d" % half],
                        ["b" + s for s in ks])
        self.B.barrier()
        self.B.barrier()
        ar.release(m)

    def store_y(self):
        ar = self.ar
        m = ar.mark()
        yt = [ar.alloc([D_MODEL]) for _ in range(2)]
        for blk in range(self.NB + 1):
            b = blk % 2
            rows, t0 = (128, blk * 128) if blk < self.NB else (NS, self.TL)
            tt = min(t0 // 512, self.NT)
            for half in range(2):
                ps = self.PS[half]
                psv = ps[:, :].rearrange("p (k c) -> p k c", k=4)
                for kk in range(4):
                    k = half * 4 + kk
                    self.tr(psv[:, kk, :], self.xres[:, k, t0:t0 + 128], self.identF[:, :],
                            ["rx.%d.%d" % (k, tt), "identF"], ["ps%d" % half])
                eng = "act" if half == 0 else "dve"
                r0 = 0
                self.cp(eng, yt[b][r0:r0 + rows, half * 512:(half + 1) * 512], ps[r0:r0 + rows, :], ["ps%d" % half],
                        ["yt%d.%d" % (b, half)])
            dst = self.o_y[t0:t0 + 128, :] if blk < self.NB else self.o_ys[:, :]
            r0 = 0
            self.st(dst, yt[b][r0:r0 + rows, :], reads=["yt%d.0" % b, "yt%d.1" % b])
        self.B.barrier()
        self.B.barrier()
        ar.release(m)

    def ffn(self, l, which, last=False):
        ar, B = self.ar, self.B
        m = ar.mark()
        G = ar.alloc([FH, self.TT], BF16)
        sg = [ar.alloc([512], BF16) for _ in range(2)]
        cnt = 0
        cy = 0
        for fh in range(2):
            for fc in range(FH):
                f = fh * FH + fc
                buf, wk = self.ws_next(("gu", l, which, f))
                slab = buf[:, :2048].rearrange("p (u k c) -> p k u c", k=KC, u=2)
                for ti, (t0, tn) in enumerate(self.tiles):
                    pb = 2 * (cnt % 2)
                    psg, psu = self.PS[pb], self.PS[pb + 1]
                    for u, ps in ((0, psg), (1, psu)):
                        for k in range(KC):
                            self.mm(ps[:, :tn], slab[:, k, u, :], self.xbf[:, k, t0:t0 + tn], k == 0, k == KC - 1,
                                    [wk, "bx.%d.%d" % (k, ti)], ["ps%d" % (pb + u)])
                    s = sg[cnt % 2]
                    self.act(s[:, :tn], psg[:, :tn], AF.Silu, ["ps%d" % pb], ["sg%d" % (cnt % 2)])
                    self.tt("dve", G[:, fc, t0:t0 + tn], s[:, :tn], psu[:, :tn], ALU.mult,
                            ["sg%d" % (cnt % 2), "ps%d" % (pb + 1)], ["G.%d.%d" % (fc, ti)])
                    cnt += 1
            for dc in range(KC):
                buf, wk = self.ws_next(("dn", l, which, fh, dc))
                slab = buf[:, :FH * 128].rearrange("p (f c) -> p f c", f=FH)
                for ti, (t0, tn) in enumerate(self.tiles):
                    ps = self.PS[4 + cy % 2]
                    pk = "ps%d" % (4 + cy % 2)
                    for fc in range(FH):
                        self.mm(ps[:, :tn], slab[:, fc, :], G[:, fc, t0:t0 + tn], fc == 0, fc == FH - 1,
                                [wk, "G.%d.%d" % (fc, ti)], [pk])
                    xr = self.xres[:, dc, t0:t0 + tn]
                    self.stt(xr, ps[:, :tn], 0.5, xr, ALU.mult, ALU.add, [pk, "rx.%d.%d" % (dc, ti)],
                             ["rx.%d.%d" % (dc, ti)])
                    cy += 1
        self.B.barrier()
        ar.release(m)
        self.layernorm(l, 0 if which == 0 else 2, last)

    def layernorm(self, l, i, last=False):
        ar = self.ar
        m = ar.mark()
        SQ = ar.alloc([KC, 512])
        MEAN = ar.alloc([512])
        T1 = ar.alloc([512])
        RSTD = ar.alloc([512])
        NBv = ar.alloc([512])
        for ti, (t0, tn) in enumerate(self.tiles):
            xk = ["rx.%d.%d" % (k, ti) for k in range(KC)]
            xv = self.xres[:, :, t0:t0 + tn]
            self.act(SQ[:, :, :tn], xv, AF.Square, xk, ["SQ"])
            for k in range(KC):
                self.mm(self.PS[6][:, :tn], self.onesF, self.xres[:, k, t0:t0 + tn], k == 0, k == KC - 1,
                        ["onesF", xk[k]], ["ps6"])
            for k in range(KC):
                self.mm(self.PS[7][:, :tn], self.onesF, SQ[:, k, :tn], k == 0, k == KC - 1, ["onesF", "SQ"], ["ps7"])
            self.act(MEAN[:, :tn], self.PS[6][:, :tn], AF.Copy, ["ps6"], ["MEAN"])
            self.tt("dve", T1[:, :tn], MEAN[:, :tn], MEAN[:, :tn], ALU.mult, ["MEAN"], ["T1"])
            self.tt("dve", T1[:, :tn], self.PS[7][:, :tn], T1[:, :tn], ALU.subtract, ["ps7", "T1"], ["T1"])
            self.act(T1[:, :tn], T1[:, :tn], AF.Sqrt, ["T1"], ["T1"], bias=EPS)
            nc = self.nc
            self.B.op("dve", lambda: nc.vector.reciprocal(out=RSTD[:, :tn], in_=T1[:, :tn]), ["T1"], ["RSTD"])
            self.stt(NBv[:, :tn], MEAN[:, :tn], -1.0, RSTD[:, :tn], ALU.mult, ALU.mult, ["MEAN", "RSTD"], ["NB"])
            rb = RSTD[:, :tn].unsqueeze(1).to_broadcast([128, KC, tn])
            nb = NBv[:, :tn].unsqueeze(1).to_broadcast([128, KC, tn])
            self.tt("dve", SQ[:, :, :tn], xv, rb, ALU.mult, xk + ["RSTD"], ["SQ"])
            self.tt("dve", SQ[:, :, :tn], SQ[:, :, :tn], nb, ALU.add, ["SQ", "NB"], ["SQ"])
            for k in range(KC):
                c = self.lncol(l, i, k)
                gs, bs = (self.lng, self.lnb) if last else (self.lnag, self.lnab)
                self.act(self.xres[:, k, t0:t0 + tn], SQ[:, k, :tn], AF.Identity, ["SQ", "lnag", "lnab", "lng", "lnb"],
                         ["rx.%d.%d" % (k, ti)], scale=gs[:, c:c + 1], bias=bs[:, c:c + 1])
                self.ts("pool", self.xbf[:, k, t0:t0 + tn], SQ[:, k, :tn], self.lng[:, c:c + 1], ALU.mult,
                        ["SQ", "lng", "lnb"], ["bx.%d.%d" % (k, ti)], s2=self.lnb[:, c:c + 1], op1=ALU.add)
        self.B.barrier()
        ar.release(m)

    def attn_consts(self):
        ar, L = self.ar, self.DEPTH
        RB = ar.alloc([128])
        self.ld(RB, self.i_relb.partition_broadcast(128), writes=["RB"])
        CC = ar.alloc([32])
        self.ld(CC, self.i_ccore.partition_broadcast(128), writes=["CC"])
        self.CC, self.RB = CC, RB
        self.CH = RB[:, 124:128]
        self.maskc = ar.alloc([128])
        self.ld(self.maskc, self.i_maskc[:, :], writes=["maskc"])
        self.segm = ar.alloc([512])
        self.ld(self.segm, self.i_segm[:, :], writes=["segm"])
        self.T01 = ar.alloc([4, 256])
        self.BSh = ar.alloc([4, 16])
        self.BC = ar.alloc([3, 4])
        self.COLV = ar.alloc([3, 4])
        self.blk1 = ar.alloc([128])
        self.memset("dve", self.blk1, 0.0, ["blk1"])
        self.memset("dve", self.blk1[0:64, 0:64], 1.0 / 64, ["blk1"])
        self.memset("dve", self.blk1[64:128, 64:128], 1.0 / 64, ["blk1"])
        self.sh64 = ar.alloc([2, 128], BF16)
        self.memset("dve", self.sh64, 0.0, ["sh64"])
        self.cp("dve", self.sh64[0:64, 0, 0:64], self.identF[0:64, 0:64], ["identF", "sh64"], ["sh64"])
        self.cp("dve", self.sh64[0:64, 1, 64:128], self.identF[0:64, 0:64], ["identF", "sh64"], ["sh64"])
        self.e65 = ar.alloc([64])
        self.memset("dve", self.e65, 0.0, ["e65"])
        self.memset("dve", self.e65[64:65, :], 1.0, ["e65"])
        self.ones64 = ar.alloc([64])
        self.memset("dve", self.ones64, 1.0 / 64, ["ones64"])
        self.mapm = ar.alloc([2])
        self.memset("dve", self.mapm, 0.0, ["mapm"])
        for hh in range(2):
            for mm_ in range(2):
                p0 = hh * 64 + mm_ * 32
                self.memset("dve", self.mapm[p0:p0 + 32, mm_:mm_ + 1], 1.0, ["mapm"])
        m = ar.mark()
        IDX = ar.alloc([256])
        IDS = ar.alloc([16])
        EQ = ar.alloc([272])
        MK = ar.alloc([128])
        self.ld(IDX, self.i_idx01[:, :], writes=["IDX"])
        self.ld(IDS, self.i_idxs[:, :], writes=["IDS"])
        self.memset("dve", self.T01, 0.0, ["T01"])
        self.memset("dve", self.BSh, 0.0, ["BSh"])
        for b in range(32):
            self.ts("dve", EQ[:, 0:256], IDX, float(b), ALU.is_equal, ["IDX"], ["EQ"])
            self.ts("dve", EQ[:, 256:272], IDS, float(b), ALU.is_equal, ["IDS"], ["EQ"])
            for h in range(4):
                col = RB[:, b * 4 + h:b * 4 + h + 1]
                self.stt(self.T01[:, h, :], EQ[:, 0:256], col, self.T01[:, h, :], ALU.mult, ALU.add,
                         ["EQ", "RB", "T01"], ["T01"])
                self.stt(self.BSh[:, h, :], EQ[:, 256:272], col, self.BSh[:, h, :], ALU.mult, ALU.add,
                         ["EQ", "RB", "BSh"], ["BSh"])
        self.ts("dve", MK, IDX[:, 0:128], 0.0, ALU.is_lt, ["IDX"], ["MK"], s2=NEG, op1=ALU.mult)
        for h in range(4):
            self.tt("dve", self.T01[:, h, 0:128], self.T01[:, h, 0:128], MK, ALU.add, ["T01", "MK"], ["T01"])
        TMPc = ar.alloc([8])
        for i in range(3):
            vis = CC[:, 8 + i:9 + i]
            near = CC[:, 4 + i:5 + i]
            far = CC[:, 12 + i:13 + i]
            self.ts("dve", TMPc[:, 0:4], self.CH, -NEG, ALU.add, ["RB", "CC"], ["TMPc"], s2=vis, op1=ALU.mult)
            self.ts("dve", self.BC[:, i, :], TMPc[:, 0:4], NEG, ALU.add, ["TMPc"], ["BC"])
            self.tt("dve", TMPc[:, 4:5], near, far, ALU.add, ["CC"], ["TMPc"])
            self.ts("dve", TMPc[:, 4:5], TMPc[:, 4:5], -NEG, ALU.mult, ["TMPc"], ["TMPc"], s2=NEG, op1=ALU.add)
            self.ts("dve", TMPc[:, 0:4], self.CH, far, ALU.mult, ["RB", "CC"], ["TMPc"], s2=TMPc[:, 4:5], op1=ALU.add)
            self.cp("dve", self.COLV[:, i, :], TMPc[:, 0:4], ["TMPc"], ["COLV"])
        self.B.barrier()
        self.B.barrier()
        ar.release(m)
        self.LBR = ar.alloc([L, 256])
        self.LBC = ar.alloc([L, 2])
        self.OMC = ar.alloc([L, 2])
        m = ar.mark()
        E = ar.alloc([L, 256])
        SUM = ar.alloc([256])
        self.ld(E.rearrange("p l c -> p (l c)"), self.i_lb.partition_broadcast(128), writes=["LBE"])
        self.act(E, E, AF.Exp, ["LBE"], ["LBE"])
        self.cp("dve", SUM, E[:, 0, :], ["LBE"], ["LBS"])
        for l in range(1, L):
            self.tt("dve", SUM, SUM, E[:, l, :], ALU.add, ["LBS", "LBE"], ["LBS"])
        nc = self.nc
        self.B.op("dve", lambda: nc.vector.reciprocal(out=SUM, in_=SUM), ["LBS"], ["LBS"])
        self.memset("dve", self.LBR[:, 0, :], 0.0, ["LBR"])
        for l in range(1, L):
            self.tt("dve", E[:, l, :], E[:, l, :], SUM, ALU.mult, ["LBE", "LBS"], ["LBE"])
            self.tt("dve", self.LBR[:, l, :], self.LBR[:, l - 1, :], E[:, l, :], ALU.add, ["LBR", "LBE"], ["LBR"])
        self.ts("dve", self.LBR, self.LBR, 0.0, ALU.max, ["LBR"], ["LBR"])
        scr = self.nc.dram_tensor("lbscr", [L, 256], F32)
        self.st(scr[:, :], self.LBR[0:1, :, :], reads=["LBR"], writes=["lbscr"])
        self.ld(self.LBC, scr.ap().rearrange("l (c p) -> p l c", p=128), writes=["LBC"], reads=["lbscr"],
                allow_slow_non_contiguous=True)
        self.ts("dve", self.OMC, self.LBC, -1.0, ALU.mult, ["LBC"], ["OMC"], s2=1.0, op1=ALU.add)
        self.B.barrier()
        self.B.barrier()
        ar.release(m)

    def layer_params(self, l):
        ar = self.ar
        P = {}
        P["psc"] = ar.alloc([2])
        self.ld(P["psc"], self.i_pscale[l].rearrange("(c p) -> p c", p=128), writes=["psc"], allow_slow_non_contiguous=True)
        P["hgg"] = ar.alloc([1])
        for hh in range(2):
            self.ld(P["hgg"][hh * 64:(hh + 1) * 64, :], self.i_hgg[l].rearrange("(p o) -> p o", o=1), writes=["hgg"],
                    allow_slow_non_contiguous=True)
        lam_init = 0.8 - 0.6 * math.exp(-0.3 * l)
        P["subg"] = ar.alloc([1])
        self.ld(P["subg"][0:64, :], self.i_subg[l].rearrange("(p o) -> p o", o=1), writes=["subg"], allow_slow_non_contiguous=True)
        self.ts("dve", P["subg"][0:64, :], P["subg"][0:64, :], 1.0 - lam_init, ALU.mult, ["subg"], ["subg"])
        LQ = ar.alloc([4, 32])
        for i in range(4):
            self.ld(LQ[:, i, :], self.i_lam[i][l:l + 1, :].partition_broadcast(128), writes=["LQ%d" % i])
        S2 = ar.alloc([4])
        nc = self.nc
        self.tt("dve", LQ[:, 0, :], LQ[:, 0, :], LQ[:, 1, :], ALU.mult, ["LQ0", "LQ1"], ["LQ0"])
        self.tt("dve", LQ[:, 2, :], LQ[:, 2, :], LQ[:, 3, :], ALU.mult, ["LQ2", "LQ3"], ["LQ2"])
        self.B.op("dve", lambda: nc.vector.reduce_sum(out=S2[:, 0:1], in_=LQ[:, 0, :], axis=AX.X), ["LQ0"], ["S2"])
        self.B.op("dve", lambda: nc.vector.reduce_sum(out=S2[:, 1:2], in_=LQ[:, 2, :], axis=AX.X), ["LQ2", "S2"], ["S2"])
        self.act(S2[:, 0:2], S2[:, 0:2], AF.Exp, ["S2"], ["S2"])
        self.tt("dve", S2[:, 2:3], S2[:, 0:1], S2[:, 1:2], ALU.subtract, ["S2"], ["S2"])
        P["nlam"] = ar.alloc([1])
        self.ts("dve", P["nlam"], S2[:, 2:3], lam_init, ALU.add, ["S2"], ["nlam"], s2=-1.0, op1=ALU.mult)
        P["pw"] = ar.alloc([2, 128], BF16)
        self.memset("pool", P["pw"], 0.0, ["pw"])
        for g in range(4):
            c, hh = g // 2, g % 2
            self.ld(P["pw"][hh * 64:(hh + 1) * 64, c, hh * 64:(hh + 1) * 64], self.i_poolw[l, g], writes=["pw"], eng="pool")
        P["sglg"] = ar.alloc([256])
        P["sglb"] = ar.alloc([256])
        self.ld(P["sglg"], self.i_sglg[l:l + 1, :].partition_broadcast(128), writes=["sglg"])
        self.ld(P["sglb"], self.i_sglb[l:l + 1, :].partition_broadcast(128), writes=["sglb"])
        P["sgb"] = ar.alloc([2, 128])
        for g in range(4):
            c, hh = g // 2, g % 2
            self.ld(P["sgb"][hh * 64:(hh + 1) * 64, c, :], self.i_sgb[l, g:g + 1, :].partition_broadcast(64), writes=["sgb"])
        P["wt"] = ar.alloc([4, 128], BF16)
        P["w00"] = ar.alloc([256])
        P["b00"] = ar.alloc([256])
        WC = ar.alloc([8])
        for g in range(4):
            self.ld(WC[:, g:g + 1], self.i_sgw[l, g, 0, 0:1].partition_broadcast(128), writes=["WC"])
            self.ld(WC[:, 4 + g:5 + g], self.i_sgb[l, g, 0:1].partition_broadcast(128), writes=["WC"])
        for g in range(4):
            self.cp("dve", P["w00"][:, g * 64:(g + 1) * 64], WC[:, g:g + 1].to_broadcast([128, 64]), ["WC"], ["w00"])
            self.cp("dve", P["b00"][:, g * 64:(g + 1) * 64], WC[:, 4 + g:5 + g].to_broadcast([128, 64]), ["WC"], ["b00"])
        P["hggrow"] = ar.alloc([64])
        self.ld(P["hggrow"], self.i_hgg[l:l + 1, :].partition_broadcast(128), writes=["hggrow"])
        P["subgrow"] = ar.alloc([64])
        self.ld(P["subgrow"], self.i_subg[l:l + 1, :].partition_broadcast(128), writes=["subgrow"])
        self.ts("dve", P["subgrow"], P["subgrow"], 1.0 - lam_init, ALU.mult, ["subgrow"], ["subgrow"])
        m = ar.mark()
        WL = ar.alloc([128])
        for g in range(4):
            self.ld(WL, self.i_sgw[l, g], writes=["WL"])
            self.tr(self.PS[0][:, 0:128], WL, self.identF, ["WL", "identF"], ["ps0"])
            self.tt("dve", P["wt"][:, g, :], self.PS[0][:, 0:128], self.maskc, ALU.mult, ["ps0", "maskc"], ["wt"])
        self.B.barrier()
        self.B.barrier()
        ar.release(m)
        return P

    def tok2fm(self, src, skeys, dst, dkeys, post=None):
        self.mm(self.PS[6][:, 0:16], src, self.identF[0:16, 0:16], True, True, list(skeys) + ["identF"], ["ps6"])
        if post is None:
            self.cp("act", dst, self.PS[6][:, 0:16], ["ps6"], dkeys)
        else:
            post(self.PS[6][:, 0:16])

    def sample_mixer(self, l, P):
        ar, nc, TL = self.ar, self.nc, self.TL
        m0 = ar.mark()
        R = slice(0, NS)
        PSs = ar.alloc([D_IN])
        for e in range(20):
            buf, wk = self.ws_next(("win", l, e))
            slab = self.slabv(buf)
            ps = self.PS[e % 2]
            pk = "ps%d" % (e % 2)
            for k in range(KC):
                self.mm(ps[R, 0:128], self.xbf[:, k, TL:TL + NS], slab[:, k, :], k == 0, k == KC - 1,
                        [wk, "bx.%d.%d" % (k, self.NT)], [pk])
            self.cp("act" if e % 2 else "dve", PSs[R, e * 128:(e + 1) * 128], ps[R, 0:128], [pk], ["PSs"])
        col = lambda i: PSs[R, i * 256:(i + 1) * 256]
        a_s, hq, hf, hi, hg, dq, dk, dv, su, sv = [col(i) for i in range(10)]
        self.st(self.o_ks[l], dk, reads=["PSs"])
        self.st(self.o_vs[l], dv, reads=["PSs"])
        Y = ar.alloc([256])
        m = ar.mark()
        AE = ar.alloc([16, 256])
        self.ld(AE[R, 0:15, :].rearrange("p r c -> p (r c)"), self.i_spool[l], writes=["AE"])
        self.cp("dve", AE[R, 15, :], a_s, ["PSs", "AE"], ["AE"])
        self.st(self.o_pools[l], AE[R, 1:16, :].rearrange("p r c -> p (r c)"), reads=["AE"])
        for g in range(4):
            win = 2 ** (g + 1)
            gs = slice(g * 64, (g + 1) * 64)
            self.B.op("dve", lambda: nc.vector.tensor_reduce(out=Y[R, gs], in_=AE[R, 16 - win:16, gs].rearrange("p r c -> p c r"),
                                                             axis=AX.X, op=ALU.add), ["AE", "sY"], ["sY"])
            self.stt(Y[R, gs], Y[R, gs], 1.0 / win, a_s[:, gs], ALU.mult, ALU.subtract, ["sY", "PSs"], ["sY"])
        Dbf = ar.alloc([2, NS], BF16)
        for c in range(2):
            self.tok2fm(Y[R, c * 128:(c + 1) * 128], ["sY"], Dbf[:, c, :], ["sDbf"])
            self.mm(self.PS[7][:, 0:NS], P["pw"][:, c, :], Dbf[:, c, :], True, True, ["pw", "sDbf"], ["ps7"])
            self.ts("dve", self.cats[:, c, :], self.PS[7][:, 0:NS], P["psc"][:, c:c + 1], ALU.mult, ["ps7", "psc", "cats"], ["cats"])
        self.B.barrier()
        ar.release(m)
        m = ar.mark()
        ST = ar.alloc([8])
        VN = ar.alloc([256])
        self.B.op("dve", lambda: nc.vector.bn_stats(out=ST[R, 0:6], in_=sv), ["PSs"], ["sST"])
        self.B.op("dve", lambda: nc.vector.bn_aggr(out=ST[R, 6:8], in_=ST[R, 0:6]), ["sST"], ["sST"])
        self.act(ST[R, 7:8], ST[R, 7:8], AF.Sqrt, ["sST"], ["sST"], bias=EPS)
        self.B.op("dve", lambda: nc.vector.reciprocal(out=ST[R, 7:8], in_=ST[R, 7:8]), ["sST"], ["sST"])
        self.ts("dve", VN[R, :], sv, ST[R, 6:7], ALU.subtract, ["PSs", "sST"], ["sVN"], s2=ST[R, 7:8], op1=ALU.mult)
        self.tt("dve", VN[R, :], VN[R, :], P["sglg"][R, :], ALU.mult, ["sVN", "sglg"], ["sVN"])
        self.tt("dve", VN[R, :], VN[R, :], P["sglb"][R, :], ALU.add, ["sVN", "sglb"], ["sVN"])
        self.st(self.o_sgs[l], VN[R, :], reads=["sVN"])
        self.tt("dve", Y[R, :], VN[R, :], P["w00"][R, :], ALU.mult, ["sVN", "w00", "sY"], ["sY"])
        self.tt("dve", Y[R, :], Y[R, :], P["b00"][R, :], ALU.add, ["sY", "b00"], ["sY"])
        self.tt("dve", Y[R, :], Y[R, :], su, ALU.mult, ["sY", "PSs"], ["sY"])
        for c in range(2):
            self.tok2fm(Y[R, c * 128:(c + 1) * 128], ["sY"], self.cats[:, 6 + c, :], ["cats"])
        self.B.barrier()
        ar.release(m)
        m = ar.mark()
        Fs = ar.alloc([256])
        Ks = ar.alloc([256])
        O = ar.alloc([256])
        SS = ar.alloc([64, 64])
        TM = ar.alloc([32, 64])
        self.act(Fs[R, :], hf, AF.Sigmoid, ["PSs"], ["sF"])
        OMR = Ks
        self.ts("dve", OMR[R, :], self.LBR[R, l, :], -1.0, ALU.mult, ["LBR"], ["sK"], s2=1.0, op1=ALU.add)
        self.tt("dve", Fs[R, :], Fs[R, :], OMR[R, :], ALU.mult, ["sF", "sK"], ["sF"])
        self.tt("dve", Fs[R, :], Fs[R, :], self.LBR[R, l, :], ALU.add, ["sF", "LBR"], ["sF"])
        self.ts("dve", Ks[R, :], Fs[R, :], -1.0, ALU.mult, ["sF", "sK"], ["sK"], s2=1.0, op1=ALU.add)
        O2 = ar.alloc([64])
        for h in range(4):
            hs = slice(h * 64, (h + 1) * 64)
            self.ld(SS[R, :, :].rearrange("p k v -> p (k v)"), self.i_shgrn[l, :, h, :], writes=["sSS"])
            for kh in range(2):
                ks_ = slice(h * 64 + kh * 32, h * 64 + kh * 32 + 32)
                kr = slice(kh * 32, kh * 32 + 32)
                kb = Ks[R, ks_].unsqueeze(2).to_broadcast([NS, 32, 64])
                fb = Fs[R, ks_].unsqueeze(2).to_broadcast([NS, 32, 64])
                qb = hq[:, ks_].unsqueeze(2).to_broadcast([NS, 32, 64])
                vb = hi[:, hs].unsqueeze(1).to_broadcast([NS, 32, 64])
                self.tt("dve", TM[R, :, :], kb, vb, ALU.mult, ["sK", "PSs", "sTM"], ["sTM"])
                self.tt("dve", SS[R, kr, :], SS[R, kr, :], fb, ALU.mult, ["sSS", "sF"], ["sSS"])
                self.tt("dve", SS[R, kr, :], SS[R, kr, :], TM[R, :, :], ALU.add, ["sSS", "sTM"], ["sSS"])
                self.tt("dve", TM[R, :, :], SS[R, kr, :], qb, ALU.mult, ["sSS", "PSs", "sTM"], ["sTM"])
                dst = O[R, hs] if kh == 0 else O2[R, :]
                self.B.op("dve", lambda: nc.vector.tensor_reduce(out=dst, in_=TM[R, :, :].rearrange("p k v -> p v k"),
                                                                 axis=AX.X, op=ALU.add), ["sTM", "sO"], ["sO"])
            self.tt("dve", O[R, hs], O[R, hs], O2[R, :], ALU.add, ["sO"], ["sO"])
            self.st(self.o_hgs[l, :, h, :], SS[R, :, :].rearrange("p k v -> p (k v)"), reads=["sSS"])
        SQ = Fs
        self.tt("dve", SQ[R, :], O[R, :], O[R, :], ALU.mult, ["sO", "sF"], ["sF"])
        self.B.op("dve", lambda: nc.vector.tensor_reduce(out=ST[R, 0:4], in_=SQ[R, :].rearrange("p (h d) -> p h d", h=4),
                                                         axis=AX.X, op=ALU.add), ["sF", "sST"], ["sST"])
        self.act(ST[R, 0:4], ST[R, 0:4], AF.Sqrt, ["sST"], ["sST"], scale=1.0 / 64, bias=EPS)
        self.B.op("dve", lambda: nc.vector.reciprocal(out=ST[R, 0:4], in_=ST[R, 0:4]), ["sST"], ["sST"])
        self.tt("dve", O[R, :].rearrange("p (h d) -> p h d", h=4), O[R, :].rearrange("p (h d) -> p h d", h=4),
                ST[R, 0:4].unsqueeze(2).to_broadcast([NS, 4, 64]), ALU.mult, ["sO", "sST"], ["sO"])
        self.tt("dve", O[R, :].rearrange("p (h d) -> p h d", h=4), O[R, :].rearrange("p (h d) -> p h d", h=4),
                P["hggrow"][R, :].unsqueeze(1).to_broadcast([NS, 4, 64]), ALU.mult, ["sO", "hggrow"], ["sO"])
        self.act(Y[R, :], hg, AF.Silu, ["PSs", "sY"], ["sY"])
        self.tt("dve", Y[R, :], Y[R, :], O[R, :], ALU.mult, ["sY", "sO"], ["sY"])
        for c in range(2):
            self.tok2fm(Y[R, c * 128:(c + 1) * 128], ["sY"], self.cats[:, 2 + c, :], ["cats"])
        self.B.barrier()
        ar.release(m)
        m = ar.mark()
        KG = ar.alloc([16, 256])
        VG = ar.alloc([16, 256])
        S8 = ar.alloc([16, 8])
        Pm = ar.alloc([16, 8])
        PRd = ar.alloc([8])
        A8 = ar.alloc([256])
        CL = ar.alloc([4])
        QS = ar.alloc([256])
        PSF = ar.alloc([16, 8])
        IDL = ar.alloc([NS], I32)
        IDF = ar.alloc([NS])
        self.cp("dve", IDL, self.IDXF, ["IDXF"], ["sIDL"])
        W8 = CL[0:8, 0:1]
        self.stt(W8, self.c8[0:8, 1:2], P["nlam"][0:8, 0:1], self.c8[0:8, 0:1], ALU.mult, ALU.add, ["c8", "nlam"], ["sCL"])
        self.tt("dve", QS[R, :], dq, dk, ALU.mult, ["PSs", "sQS"], ["sQS"])
        self.B.op("dve", lambda: nc.vector.tensor_reduce(out=S8[R, 0, :], in_=QS[R, :].rearrange("p (j d) -> p j d", d=32),
                                                         axis=AX.X, op=ALU.add), ["sQS", "sS8"], ["sS8"])
        self.stt(S8[R, 0, :], S8[R, 0, :], QSCALE, self.B0[R, :], ALU.mult, ALU.add, ["sS8", "B0"], ["sS8"])
        self.act(S8[R, 0, :], S8[R, 0, :], AF.Exp, ["sS8"], ["sS8"])
        self.tt("dve", PSF[R, :, :], S8[R, 0, :].unsqueeze(1).to_broadcast([NS, NS, 8]),
                self.identF[R, 0:NS].unsqueeze(2).to_broadcast([NS, NS, 8]), ALU.mult, ["sS8", "identF"], ["sPSF"])
        flat_k = self.i_ck[l]
        flat_v = self.i_cv[l]
        for n in range(NS):
            self.B.dma("pool", lambda: nc.gpsimd.indirect_dma_start(
                out=KG.rearrange("p i c -> p (i c)"), out_offset=None, in_=flat_k,
                in_offset=bass.IndirectOffsetOnAxis(ap=IDL[:, n:n + 1], axis=0)), reads=["sIDL"], writes=["sKG"])
            self.B.dma("pool", lambda: nc.gpsimd.indirect_dma_start(
                out=VG.rearrange("p i c -> p (i c)"), out_offset=None, in_=flat_v,
                in_offset=bass.IndirectOffsetOnAxis(ap=IDL[:, n:n + 1], axis=0)), reads=["sIDL"], writes=["sVG"])
            self.ts("dve", QS[R, :], dq, self.identF[R, n:n + 1], ALU.mult, ["PSs", "identF", "sQS"], ["sQS"])
            self.mm(self.PS[0][:, 0:256], self.ones1[R, :], QS[R, :], True, True, ["ones1", "sQS"], ["ps0"])
            self.tt("dve", KG, KG, self.PS[0][:, 0:256].unsqueeze(1).to_broadcast([128, 16, 256]), ALU.mult,
                    ["sKG", "ps0"], ["sKG"])
            self.B.op("dve", lambda: nc.vector.tensor_reduce(out=S8.rearrange("p i j -> p (i j)"),
                                                             in_=KG.rearrange("p i (j d) -> p (i j) d", d=32),
                                                             axis=AX.X, op=ALU.add), ["sKG", "sS8"], ["sS8"])
            self.stt(Pm, S8, QSCALE, self.BS8, ALU.mult, ALU.add, ["sS8", "BS8"], ["sPm"])
            self.act(Pm, Pm, AF.Exp, ["sPm"], ["sPm"])
            self.B.op("dve", lambda: nc.vector.tensor_reduce(out=PRd, in_=Pm.rearrange("p i j -> p j i"),
                                                             axis=AX.X, op=ALU.add), ["sPm"], ["sPRd"])
            acc = self.PS[1]
            for i in range(16):
                self.mm(acc[0:8, 0:256], Pm[:, i, :], VG[:, i, :], i == 0, False, ["sPm", "sVG"], ["ps1"])
            self.mm(acc[0:8, 0:256], PSF[R, n, :], dv, False, True, ["sPSF", "PSs"], ["ps1"])
            self.mm(self.PS[2][0:8, 0:1], PRd, self.onesF[:, 0:1], True, False, ["sPRd", "onesF"], ["ps2"])
            self.mm(self.PS[2][0:8, 0:1], PSF[R, n, :], self.onesF[R, 0:1], False, True, ["sPSF", "onesF"], ["ps2"])
            self.B.op("dve", lambda: nc.vector.reciprocal(out=CL[0:8, 1:2], in_=self.PS[2][0:8, 0:1]), ["ps2", "sCL"], ["sCL"])
            self.ts("dve", CL[0:8, 2:3], CL[0:8, 1:2], W8, ALU.mult, ["sCL"], ["sCL"], s2=1.0 / D_MODEL, op1=ALU.mult)
            self.stt(A8[0:8, :], acc[0:8, 0:256], CL[0:8, 2:3], self.bm8[0:8, :], ALU.mult, ALU.mult, ["ps1", "sCL", "bm8"], ["sA8"])
            self.mm(self.PS[3][R, 0:256], self.oh8[0:8, n, :], A8[0:8, :], n == 0, n == NS - 1, ["oh8", "sA8"], ["ps3"])
        OC = A8
        self.cp("act", OC[R, :], self.PS[3][R, 0:256], ["ps3", "sA8"], ["sOC"])
        self.tt("dve", QS[R, :], OC[R, :], OC[R, :], ALU.mult, ["sOC", "sQS"], ["sQS"])
        ST2 = CL
        self.B.op("dve", lambda: nc.vector.tensor_reduce(out=ST2[R, 0:4], in_=QS[R, :].rearrange("p (h d) -> p h d", h=4),
                                                         axis=AX.X, op=ALU.add), ["sQS", "sCL"], ["sCL"])
        self.act(ST2[R, 0:4], ST2[R, 0:4], AF.Sqrt, ["sCL"], ["sCL"], scale=1.0 / 64, bias=EPS)
        self.B.op("dve", lambda: nc.vector.reciprocal(out=ST2[R, 0:4], in_=ST2[R, 0:4]), ["sCL"], ["sCL"])
        self.tt("dve", OC[R, :].rearrange("p (h d) -> p h d", h=4), OC[R, :].rearrange("p (h d) -> p h d", h=4),
                ST2[R, 0:4].unsqueeze(2).to_broadcast([NS, 4, 64]), ALU.mult, ["sOC", "sCL"], ["sOC"])
        self.tt("dve", Y[R, :].rearrange("p (h d) -> p h d", h=4), OC[R, :].rearrange("p (h d) -> p h d", h=4),
                P["subgrow"][R, :].unsqueeze(1).to_broadcast([NS, 4, 64]), ALU.mult, ["sOC", "subgrow", "sY"], ["sY"])
        for c in range(2):
            self.tok2fm(Y[R, c * 128:(c + 1) * 128], ["sY"], self.cats[:, 4 + c, :], ["cats"])
        self.B.barrier()
        ar.release(m)
        self.B.barrier()
        ar.release(m0)

    def sample_consts(self):
        ar = self.ar
        self.ones1 = ar.alloc([128])
        self.memset("dve", self.ones1, 1.0, ["ones1"])
        self.c8 = ar.alloc([2])
        self.bm8 = ar.alloc([256])
        self.oh8 = ar.alloc([16, 16])
        self.ld(self.c8[0:8, :], self.i_c8[:, 0:2], writes=["c8"])
        self.ld(self.bm8[0:8, :], self.i_c8[:, 2:258], writes=["bm8"])
        self.ld(self.oh8[0:8, :, :].rearrange("p n c -> p (n c)"), self.i_c8[:, 258:514], writes=["oh8"])
        self.BS8 = ar.alloc([16, 8])
        self.B0 = ar.alloc([8])
        for h in range(4):
            for mm_ in range(2):
                j = h * 2 + mm_
                self.cp("dve", self.BS8[:, :, j], self.BSh[:, h, :], ["BSh", "BS8"], ["BS8"])
                self.cp("dve", self.B0[:, j:j + 1], self.RB[:, h:h + 1], ["RB", "B0"], ["B0"])
        self.IDXF = ar.alloc([NS])
        m = ar.mark()
        PTI = ar.alloc([NS], I32)
        SUB = ar.alloc([1])
        for p in range(NPG):
            self.ld(PTI[p * 8:(p + 1) * 8, :], self.i_pt[:, p:p + 1].rearrange("n o -> o n").partition_broadcast(8),
                    writes=["PTI"], allow_slow_non_contiguous=True)
        self.ld(SUB, self.i_misc[:, 0:1], writes=["SUB"], allow_slow_non_contiguous=True)
        self.cp("dve", self.IDXF, PTI, ["PTI"], ["IDXF"])
        self.ts("dve", self.IDXF, self.IDXF, 8.0, ALU.mult, ["IDXF", "SUB"], ["IDXF"], s2=SUB[:, 0:1], op1=ALU.add)
        self.B.barrier()
        ar.release(m)

    def proj_fm(self, slab, wk, ti, ps, pk):
        t0, tn = self.ptiles[ti]
        for k in range(KC):
            self.mm(ps[:, :tn], slab[:, k, :], self.xbf[:, k, t0:t0 + tn], k == 0, k == KC - 1,
                    [wk, "bx.%d.%d" % (k, ti)], [pk])

    def proj_tm(self, slab, wk, t0, M, ps_ap, pk):
        ti = min(t0 // 512, self.NT)
        for k in range(KC):
            self.mm(ps_ap, self.xbf[:, k, t0:t0 + M], slab[:, k, :], k == 0, k == KC - 1,
                    [wk, "bx.%d.%d" % (k, ti)], [pk])

    def slabv(self, buf):
        return buf[:, :1024].rearrange("p (k c) -> p k c", k=KC)

    def kstop(self, n):
        import os
        if int(os.environ.get('KSTOP', 99)) == n:
            raise StopBuild()

    def mixer(self, l):
        ar = self.ar
        m0 = ar.mark()
        P = self.layer_params(l)
        self.cats = ar.alloc([KC, NS], BF16)
        self.memset("pool", self.cats, 0.0, ["cats"])
        if self.stages >= 4:
            self.sample_mixer(l, P)
        self.cat = {n: ar.alloc([2, self.TT], BF16) for n in ("a", "b", "d")}
        self.prompt_proj(l, P)
        self.kstop(9)
        self.B.barrier()
        self.prompt_attn(l, P)
        self.kstop(10)
        self.wout(l, P)
        self.B.barrier()
        ar.release(m0)
        self.layernorm(l, 1)

    def sgu_prompt(self, l, P):
        ar = self.ar
        m = ar.mark()
        b18, k18 = self.ws_next(("win", l, 18))
        b19, k19 = self.ws_next(("win", l, 19), hold=True)
        slabs = [(self.slabv(b18), k18), (self.slabv(b19), k19)]
        VN = [ar.alloc([256]) for _ in range(2)]
        VNB = [ar.alloc([256], BF16) for _ in range(2)]
        ST = ar.alloc([8])
        T1 = ar.alloc([128])
        nc = self.nc
        catd = self.cat["d"]
        for blk in range(self.NB):
            b = blk % 2
            t0 = blk * 128
            ps = self.PS[b]
            pk = "ps%d" % b
            for e in range(2):
                self.proj_tm(slabs[e][0], slabs[e][1], t0, 128, ps[:, e * 128:(e + 1) * 128], pk)
            self.B.op("dve", lambda: nc.vector.bn_stats(out=ST[:, 0:6], in_=ps[:, 0:256]), [pk], ["sgST"])
            self.B.op("dve", lambda: nc.vector.bn_aggr(out=ST[:, 6:8], in_=ST[:, 0:6]), ["sgST"], ["sgST"])
            self.act(ST[:, 7:8], ST[:, 7:8], AF.Sqrt, ["sgST"], ["sgST"], bias=EPS)
            self.B.op("dve", lambda: nc.vector.reciprocal(out=ST[:, 7:8], in_=ST[:, 7:8]), ["sgST"], ["sgST"])
            vn = VN[b]
            vk = "sgVN%d" % b
            self.ts("dve", vn, ps[:, 0:256], ST[:, 6:7], ALU.subtract, [pk, "sgST"], [vk], s2=ST[:, 7:8], op1=ALU.mult)
            self.tt("dve", vn, vn, P["sglg"], ALU.mult, [vk, "sglg"], [vk])
            self.tt("dve", vn, vn, P["sglb"], ALU.add, [vk, "sglb"], [vk])
            self.cp("act", VNB[b], vn, [vk], ["sgVB%d" % b])
            if blk == self.NB - 1:
                self.st(self.o_sgp[l], vn, reads=[vk])
            for c in range(2):
                for hh in range(2):
                    g = 2 * c + hh
                    pg = self.PS[2 + hh]
                    pgk = "ps%d" % (2 + hh)
                    self.mm(pg[:, 0:128], VNB[b][:, c * 128:(c + 1) * 128], P["wt"][:, g, :], True, True,
                            ["sgVB%d" % b, "wt"], [pgk])
                    hp = slice(hh * 64, (hh + 1) * 64)
                    self.tt("dve", catd[hp, c, t0:t0 + 128], pg[hp, 0:128], P["sgb"][hp, c, :], ALU.add,
                            [pgk, "sgb"], ["catd.%d" % c])
        for c in range(2):
            buf, wk = self.ws_next(("win", l, 16 + c))
            slab = self.slabv(buf)
            for ti, (t0, tn) in enumerate(self.ptiles):
                ps = self.PS[4 + ti % 2]
                pk = "ps%d" % (4 + ti % 2)
                self.proj_fm(slab, wk, ti, ps, pk)
                self.tt("dve", catd[:, c, t0:t0 + tn], catd[:, c, t0:t0 + tn], ps[:, :tn], ALU.mult,
                        ["catd.%d" % c, pk], ["catd.%d" % c])
        self.B.barrier()
        ar.release(m)

    def pool_prompt(self, l, P, GS):
        ar = self.ar
        m = ar.mark()
        TL = self.TL
        W = 16 + 512
        A = ar.alloc([W])
        PA = ar.alloc([W])
        PB = ar.alloc([W])
        Dbf = ar.alloc([512], BF16)
        PT = ar.alloc([128])
        cata = self.cat["a"]
        for c in range(2):
            buf, wk = self.ws_next(("win", l, c))
            slab = self.slabv(buf)
            self.memset("pool", A[:, 0:16], 0.0, ["plA"])
            for ti, (t0, tn) in enumerate(self.ptiles):
                ps = self.PS[ti % 2]
                pk = "ps%d" % (ti % 2)
                self.proj_fm(slab, wk, ti, ps, pk)
                self.act(A[:, 16:16 + tn], ps[:, :tn], AF.Copy, [pk], ["plA"])
                if ti == 0:
                    self.cp("pool", self.A16s[:, c, :], A[:, 16:32], ["plA"], ["A16s"])
                if ti == self.NT - 1:
                    self.cp("pool", GS[:, 129 + c * 16:129 + c * 16 + 16], A[:, tn:tn + 16], ["plA"], ["GS"])
                    self.cp("dve", PT, A[:, tn + 16 - 128:tn + 16], ["plA"], ["plPT"])
                    self.tr(self.PS[6][:, 0:128], PT, self.identF, ["plPT", "identF"], ["ps6"])
                    PTo = PA[:, 0:128]
                    self.cp("act", PTo, self.PS[6][:, 0:128], ["ps6"], ["plPA"])
                    self.st(self.o_poolp[l][:, c * 128:(c + 1) * 128], PTo[113:128, :], reads=["plPA"])
                self.tt("pool", PA[:, 2:W], A[:, 2:W], A[:, 1:W - 1], ALU.add, ["plA"], ["plPA"])
                self.tt("pool", PB[:, 4:W], PA[:, 4:W], PA[:, 2:W - 2], ALU.add, ["plPA"], ["plPB"])
                wins = {(0, 0): (PA, "plPA", 2), (0, 1): (PB, "plPB", 4)}
                if c == 1:
                    self.tt("pool", PA[:, 8:W], PB[:, 8:W], PB[:, 4:W - 4], ALU.add, ["plPB", "plPA"], ["plPA"])
                    self.tt("pool", PB[:, 16:W], PA[:, 16:W], PA[:, 8:W - 8], ALU.add, ["plPA", "plPB"], ["plPB"])
                    wins = {(1, 0): (PA, "plPA", 8), (1, 1): (PB, "plPB", 16)}
                for hh in range(2):
                    src, sk, win = wins[(c, hh)]
                    hp = slice(hh * 64, (hh + 1) * 64)
                    self.stt(Dbf[hp, :tn], src[hp, 16:16 + tn], 1.0 / win, A[hp, 16:16 + tn], ALU.mult, ALU.subtract,
                             [sk, "plA"], ["plD"])
                pl = self.PS[2 + ti % 2]
                plk = "ps%d" % (2 + ti % 2)
                self.mm(pl[:, :tn], P["pw"][:, c, :], Dbf[:, :tn], True, True, ["pw", "plD"], [plk])
                self.ts("dve", cata[:, c, t0:t0 + tn], pl[:, :tn], P["psc"][:, c:c + 1], ALU.mult, [plk, "psc"],
                        ["cata.%d" % c])
                self.cp("pool", A[:, 0:16], A[:, tn:tn + 16], ["plA"], ["plA"])
        self.B.barrier()
        ar.release(m)

    def pool_fix(self, l, P, GO):
        ar = self.ar
        m = ar.mark()
        A = ar.alloc([2, 32])
        PA = ar.alloc([2, 32])
        PB = ar.alloc([2, 32])
        Dbf = ar.alloc([2, 16], BF16)
        INV = ar.alloc([2, 16])
        self.ld(INV.rearrange("p c t -> p (c t)"), self.i_invc[:, :], writes=["pfINV"])
        self.memset("dve", A, 0.0, ["pfA"])
        for i in range(3):
            tail = GO[:, i, 129:161].rearrange("p (c t) -> p c t", c=2)
            self.stt(A[:, :, 0:16], tail, self.CC[:, 4 + i:5 + i], A[:, :, 0:16], ALU.mult, ALU.add,
                     ["GO", "CC", "pfA"], ["pfA"])
        self.cp("dve", A[:, :, 16:32], self.A16s, ["A16s", "pfA"], ["pfA"])
        self.tt("dve", PA[:, :, 2:32], A[:, :, 2:32], A[:, :, 1:31], ALU.add, ["pfA"], ["pfPA"])
        self.tt("dve", PB[:, :, 4:32], PA[:, :, 4:32], PA[:, :, 2:30], ALU.add, ["pfPA"], ["pfPB"])
        PC = ar.alloc([2, 32])
        PD = ar.alloc([2, 32])
        self.tt("dve", PC[:, :, 8:32], PB[:, :, 8:32], PB[:, :, 4:28], ALU.add, ["pfPB"], ["pfPC"])
        self.tt("dve", PD[:, :, 16:32], PC[:, :, 16:32], PC[:, :, 8:24], ALU.add, ["pfPC"], ["pfPD"])
        D32 = ar.alloc([2, 16])
        for c, hh, src, sk in ((0, 0, PA, "pfPA"), (0, 1, PB, "pfPB"), (1, 0, PC, "pfPC"), (1, 1, PD, "pfPD")):
            hp = slice(hh * 64, (hh + 1) * 64)
            self.tt("dve", D32[hp, c, :], src[hp, c, 16:32], INV[hp, c, :], ALU.mult, [sk, "pfINV", "pfD32"], ["pfD32"])
        self.tt("dve", D32, D32, A[:, :, 16:32], ALU.subtract, ["pfD32", "pfA"], ["pfD32"])
        self.cp("dve", Dbf, D32, ["pfD32"], ["pfD"])
        for c in range(2):
            self.mm(self.PS[6][:, 0:16], P["pw"][:, c, :], Dbf[:, c, :], True, True, ["pw", "pfD"], ["ps6"])
            self.ts("dve", self.cat["a"][:, c, 0:16], self.PS[6][:, 0:16], P["psc"][:, c:c + 1], ALU.mult,
                    ["ps6", "psc", "cata.%d" % c], ["cata.%d" % c])
        self.B.barrier()
        ar.release(m)

    def hgrn_prep(self, l, P, c, H):
        ar = self.ar
        m = ar.mark()
        TL = self.TL
        bf_, kf_ = self.ws_next(("win", l, 4 + c))
        bq_, kq_ = self.ws_next(("win", l, 2 + c), hold=True)
        bg_, kg_ = self.ws_next(("win", l, 8 + c), hold=True)
        bv_, kv_ = self.ws_next(("win", l, 6 + c), hold=True)
        sf, sq, sg_, sv = self.slabv(bf_), self.slabv(bq_), self.slabv(bg_), self.slabv(bv_)
        S = [ar.alloc([512]) for _ in range(4)]
        KHf = ar.alloc([512], BF16)
        lbc = self.LBC[:, l, c:c + 1]
        omc = self.OMC[:, l, c:c + 1]
        nc = self.nc
        catb = self.cat["b"]
        for ti, (t0, tn) in enumerate(self.ptiles):
            pf, pq, pg = self.PS[0], self.PS[1], self.PS[2]
            self.proj_fm(sf, kf_, ti, pf, "ps0")
            self.proj_fm(sq, kq_, ti, pq, "ps1")
            self.proj_fm(sg_, kg_, ti, pg, "ps2")
            self.act(catb[:, c, t0:t0 + tn], pg[:, :tn], AF.Silu, ["ps2"], ["catb.%d" % c])
            Fv, LG, BCv, E = S
            self.act(Fv, pf[:, :tn], AF.Sigmoid, ["ps0"], ["hF"])
            self.ts("dve", Fv, Fv, omc, ALU.mult, ["hF", "OMC"], ["hF"], s2=lbc, op1=ALU.add)
            self.act(LG, Fv, AF.Ln, ["hF"], ["hLG"])
            self.ts("dve", Fv, Fv, -1.0, ALU.mult, ["hF"], ["hF"], s2=1.0, op1=ALU.add)
            self.B.op("dve", lambda: nc.vector.tensor_tensor_scan(out=BCv, data0=self.segm, data1=LG, initial=0.0,
                                                                  op0=ALU.mult, op1=ALU.add),
                      ["segm", "hLG"], ["hBC"])
            bl = BCv.rearrange("p (n s) -> p n s", s=64)[:, :, 63]
            n8 = tn // 64
            self.cp("dve", H["BL"][:, ti * 8:ti * 8 + n8], bl, ["hBC"], ["hBL"])
            self.act(E, BCv, AF.Exp, ["hBC"], ["hE"])
            self.tt("dve", H["QT"][:, t0:t0 + tn], pq[:, :tn], E, ALU.mult, ["ps1", "hE"], ["hQT"])
            self.act(E, BCv, AF.Exp, ["hBC", "hE"], ["hE"], scale=-1.0)
            self.tt("dve", H["KT"][:, t0:t0 + tn], Fv, E, ALU.mult, ["hF", "hE"], ["hKT"])
            blb = H["BL"][:, ti * 8:ti * 8 + n8].unsqueeze(2).to_broadcast([128, n8, 64])
            self.tt("dve", LG.rearrange("p (n s) -> p n s", s=64), blb, BCv.rearrange("p (n s) -> p n s", s=64),
                    ALU.subtract, ["hBL", "hBC", "hLG"], ["hLG"])
            self.act(LG, LG, AF.Exp, ["hLG"], ["hLG"])
            self.tt("dve", KHf, Fv, LG, ALU.mult, ["hF", "hLG"], ["hKHf"])
            for bi in range(tn // 128):
                blk = ti * 4 + bi
                pt = self.PS[3 + bi % 2]
                ptk = "ps%d" % (3 + bi % 2)
                self.mm(pt[:, 0:128], KHf[:, bi * 128:(bi + 1) * 128], self.identB, True, True, ["hKHf", "identB"], [ptk])
                self.proj_tm(sv, kv_, t0 + bi * 128, 128, pt[:, 128:256], ptk)
                self.cp("act", H["KHt"][:, blk, :], pt[:, 0:128], [ptk], ["hKHt"])
                self.cp("dve", H["VHt"][:, blk, :], pt[:, 128:256], [ptk], ["hVHt"])
        self.act(H["EBL"], H["BL"], AF.Exp, ["hBL"], ["hEBL"])
        self.B.barrier()
        ar.release(m)

    def hgrn_state_step(self, H, ch, S32):
        psS = self.PS[5]
        rows = slice((ch % 2) * 64, (ch % 2) * 64 + 64)
        self.mm(psS[:, 0:128], H["KHt"][rows, ch // 2, :], H["VHt"][rows, ch // 2, :], True, True, ["hKHt", "hVHt"], ["ps5"])
        for hh in range(2):
            hp = slice(hh * 64, (hh + 1) * 64)
            self.stt(S32[hp, hp], S32[hp, hp], H["EBL"][hp, ch:ch + 1], psS[hp, hp], ALU.mult, ALU.add,
                     ["hS32", "hEBL", "ps5"], ["hS32"])

    def hgrn_pass1(self, l, c, H, GS, gi):
        nc = self.nc
        S32 = H["S32"]
        self.memset("dve", S32, 0.0, ["hS32"])
        for ch in range(self.NCH):
            self.hgrn_state_step(H, ch, S32)
        self.cp("dve", GS[:, 0:128], S32, ["hS32"], ["GS"])
        self.B.op("dve", lambda: nc.vector.reduce_sum(out=GS[:, 128:129], in_=H["BL"], axis=AX.X), ["hBL", "GS"], ["GS"])
        self.act(GS[:, 128:129], GS[:, 128:129], AF.Exp, ["GS"], ["GS"])
        gin, gout = self.g_sm_in[l][gi], self.g_sm_out[l][gi]
        self.st(gin[:, :], GS, reads=["GS"], writes=["gsmi%d" % gi])
        self.B.cc(lambda: nc.gpsimd.collective_compute("AllGather", ALU.bypass, replica_groups=[[0, 1, 2, 3], [4, 5, 6, 7]],
                                                       ins=[gin[:, :]], outs=[gout[:, :]]),
                  reads=["gsmi%d" % gi], writes=["gsmo%d" % gi])

    def hgrn_pass2(self, l, P, c, H, GO, gi):
        ar = self.ar
        m = ar.mark()
        nc = self.nc
        gout = self.g_sm_out[l][gi]
        self.ld(GO, gout.ap().rearrange("(i p) c -> p i c", p=128), writes=["GO"], reads=["gsmo%d" % gi])
        S32, Sbf = H["S32"], H["Sbf"]
        Pst = ar.alloc([128])
        self.memset("dve", Pst, 0.0, ["hP"])
        self.memset("dve", S32, 0.0, ["hS32"])
        for i in range(4):
            self.stt(S32, Pst, self.CC[:, i:i + 1], S32, ALU.mult, ALU.add, ["hP", "CC", "hS32"], ["hS32"])
            self.stt(Pst, Pst, GO[:, i, 128:129], GO[:, i, 0:128], ALU.mult, ALU.add, ["hP", "GO"], ["hP"])
        for hh in range(2):
            hp = slice(hh * 64, (hh + 1) * 64)
            self.st(self.o_hgp[l, 2 * c + hh], Pst[hp, hp], reads=["hP"])
        self.cp("act", Sbf, S32, ["hS32"], ["hSbf"])
        import os
        P2 = int(os.environ.get("P2STOP", 99))
        if P2 == 1:
            self.B.barrier()
            ar.release(m)
            return
        AT = [ar.alloc([256], BF16) for _ in range(2)]
        OB = ar.alloc([512])
        SQ = ar.alloc([512])
        catb = self.cat["b"]
        for ch in range(self.NCH):
            c0 = ch * 64
            blk, half = ch // 2, ch % 2
            rows = slice(half * 64, half * 64 + 64)
            at = AT[blk % 2]
            atk = "hAT%d" % (blk % 2)
            if half == 0:
                b0 = blk * 128
                banks = (0, 1) if blk % 2 == 0 else (6, 7)
                for hh in range(2):
                    hp = slice(hh * 64, (hh + 1) * 64)
                    psA = self.PS[banks[hh]]
                    pak = "ps%d" % banks[hh]
                    self.mm(psA[:, 0:128], H["KT"][hp, b0:b0 + 128], H["QT"][hp, b0:b0 + 128], True, True,
                            ["hKT", "hQT"], [pak])
                    self.tt("dve", at[:, hh * 128:(hh + 1) * 128], psA[:, 0:128], self.mask2, ALU.mult,
                            [pak, "mask2"], [atk])
            if P2 == 2:
                continue
            psO = self.PS[2 + ch % 2]
            pok = "ps%d" % (2 + ch % 2)
            rhs = at.rearrange("p (h t) -> p h t", h=2)[rows, :, half * 64:(half + 1) * 64]
            self.mm(psO[:, 0:128].rearrange("p (h t) -> p h t", h=2), H["VHt"][rows, blk, :], rhs, True, False, ["hVHt", atk], [pok])
            if P2 == 3:
                continue
            for hh in range(2):
                self.mm(psO[:, hh * 64:(hh + 1) * 64], Sbf, H["QT"][:, c0:c0 + 64], False, hh == 1, ["hSbf", "hQT"], [pok])
            if P2 == 4:
                continue
            o0 = (ch % 8) * 64
            for hh in range(2):
                hp = slice(hh * 64, (hh + 1) * 64)
                self.cp("act", OB[hp, o0:o0 + 64], psO[hp, hh * 64:(hh + 1) * 64], [pok], ["hOB"])
            self.hgrn_state_step(H, ch, S32)
            self.cp("act", Sbf, S32, ["hS32"], ["hSbf"])
            if ch % 8 == 7 or ch == self.NCH - 1:
                t0 = (ch // 8) * 512
                tn = o0 + 64
                self.act(SQ[:, :tn], OB[:, :tn], AF.Square, ["hOB"], ["hSQ"])
                self.mm(self.PS[4][:, :tn], self.blk1, SQ[:, :tn], True, True, ["blk1", "hSQ"], ["ps4"])
                self.act(SQ[:, :tn], self.PS[4][:, :tn], AF.Sqrt, ["ps4", "hSQ"], ["hSQ"], bias=EPS)
                self.B.op("dve", lambda: nc.vector.reciprocal(out=SQ[:, :tn], in_=SQ[:, :tn]), ["hSQ"], ["hSQ"])
                self.stt(OB[:, :tn], OB[:, :tn], P["hgg"][:, 0:1], SQ[:, :tn], ALU.mult, ALU.mult, ["hOB", "hgg", "hSQ"], ["hOB"])
                self.tt("dve", catb[:, c, t0:t0 + tn], catb[:, c, t0:t0 + tn], OB[:, :tn], ALU.mult,
                        ["catb.%d" % c, "hOB"], ["catb.%d" % c])
        self.B.barrier()
        ar.release(m)

    def qkv_prompt(self, l, P):
        ar = self.ar
        m = ar.mark()
        TL = self.TL
        hb = self.NB // 2
        gKp = [self.g_kv_in[l][c].ap().rearrange("r c -> (r c)").rearrange("(p t) -> p t", p=128) for c in range(2)]
        gVp = [self.g_kv_in[l][2 + q].ap().rearrange("r c -> (r c)").rearrange("(b p c) -> p b c", p=128, c=256) for q in range(2)]
        sl = []
        for e in (12, 13, 14, 15):
            b_, k_ = self.ws_next(("win", l, e), hold=(e != 12))
            sl.append((self.slabv(b_), k_))
        KB = [ar.alloc([512], BF16) for _ in range(2)]
        for c in range(2):
            for ti, (t0, tn) in enumerate(self.ptiles):
                ps = self.PS[ti % 2]
                pk = "ps%d" % (ti % 2)
                self.proj_fm(sl[c][0], sl[c][1], ti, ps, pk)
                kb = KB[ti % 2]
                self.cp("act", kb[:, :tn], ps[:, :tn], [pk], ["qkKB%d" % (ti % 2)])
                self.st(gKp[c][:, t0:t0 + tn], kb[:, :tn], reads=["qkKB%d" % (ti % 2)], writes=["gkvi%d" % c])
        KV32 = [ar.alloc([512]) for _ in range(2)]
        VB = [ar.alloc([256], BF16) for _ in range(2)]
        for blk in range(self.NB):
            b = blk % 2
            t0 = blk * 128
            ps = self.PS[2 + b]
            pk = "ps%d" % (2 + b)
            for e in range(4):
                self.proj_tm(sl[e][0], sl[e][1], t0, 128, ps[:, e * 128:(e + 1) * 128], pk)
            self.cp("act", KV32[b], ps[:, :], [pk], ["qkKV%d" % b])
            self.cp("dve", VB[b], ps[:, 256:512], [pk], ["qkVB%d" % b])
            self.st(self.o_kp[l, t0:t0 + 128, :], KV32[b][:, 0:256], reads=["qkKV%d" % b])
            self.st(self.o_vp[l, t0:t0 + 128, :], KV32[b][:, 256:512], reads=["qkKV%d" % b])
            self.st(gVp[blk // hb][:, blk % hb, :], VB[b], reads=["qkVB%d" % b], writes=["gkvi%d" % (2 + blk // hb)])
        self.B.barrier()
        ar.release(m)

    def q_prompt(self, l, P):
        for c in range(2):
            buf, wk = self.ws_next(("win", l, 10 + c))
            slab = self.slabv(buf)
            for ti, (t0, tn) in enumerate(self.ptiles):
                ps = self.PS[ti % 2]
                pk = "ps%d" % (ti % 2)
                self.proj_fm(slab, wk, ti, ps, pk)
                for mm_ in range(2):
                    eng = "dve" if mm_ == 0 else "dve"
                    self.ts(eng, self.QZ[:, c, mm_, t0:t0 + tn], ps[:, :tn], self.mapm[:, mm_:mm_ + 1], ALU.mult,
                            [pk, "mapm"], ["QZ.%d" % c])

    def prompt_proj(self, l, P):
        ar = self.ar
        nc = self.nc
        TL = self.TL
        self.A16s = ar.alloc([2, 16])
        GS = ar.alloc([161])
        GO = ar.alloc([4, 161])
        self.memset("dve", GS, 0.0, ["GS"])

        def new_H():
            H = {}
            H["QT"] = ar.alloc([TL], BF16)
            H["KT"] = ar.alloc([TL], BF16)
            H["KHt"] = ar.alloc([self.NB, 128], BF16)
            H["VHt"] = ar.alloc([self.NB, 128], BF16)
            H["BL"] = ar.alloc([self.NCH])
            H["EBL"] = ar.alloc([self.NCH])
            H["S32"] = ar.alloc([128])
            H["Sbf"] = ar.alloc([128], BF16)
            return H

        m = ar.mark()
        H = new_H()
        self.kstop(1)
        self.hgrn_prep(l, P, 0, H)
        self.kstop(2)
        self.hgrn_pass1(l, 0, H, GS, 0)
        self.kstop(3)
        self.sgu_prompt(l, P)
        self.kstop(4)
        self.hgrn_pass2(l, P, 0, H, GO, 0)
        self.kstop(5)
        self.B.barrier()
        self.B.barrier()
        ar.release(m)
        self.pool_prompt(l, P, GS)
        self.kstop(6)
        m = ar.mark()
        H = new_H()
        self.hgrn_prep(l, P, 1, H)
        self.hgrn_pass1(l, 1, H, GS, 1)
        self.qkv_prompt(l, P)
        self.kstop(7)
        self.hgrn_pass2(l, P, 1, H, GO, 1)
        self.pool_fix(l, P, GO)
        self.kstop(8)
        self.B.barrier()
        self.B.barrier()
        ar.release(m)
        self.QZ = ar.alloc([2, 2, TL], BF16)
        self.q_prompt(l, P)
        for q in range(4):
            gin, gout = self.g_kv_in[l][q], self.g_kv_out[l][q]
            self.B.cc(lambda: nc.gpsimd.collective_compute("AllGather", ALU.bypass, replica_groups=[[0, 1, 2, 3], [4, 5, 6, 7]],
                                                           ins=[gin[:, :]], outs=[gout[:, :]]),
                      reads=["gkvi%d" % q], writes=["gkvo%d" % q])

    def prompt_attn(self, l, P):
        ar, nc, TL, NB = self.ar, self.nc, self.TL, self.NB
        m = ar.mark()
        a2 = self.xbf_words
        ar2 = Arena.__new__(Arena)
        ar2.t, ar2.words, ar2.top, ar2.peak = self.ar.t, a2[1], a2[0], a2[0]
        def a2(shape, dt=F32):
            n = 1
            for s_ in shape:
                n *= s_
            w = n if dt == F32 else (n + 1) // 2
            return ar2.alloc(shape, dt) if ar2.top + w <= ar2.words else ar.alloc(shape, dt)
        self.catc = a2([2, self.TT], BF16)
        KR = [a2([128], BF16) for _ in range(8)]
        VR = [a2([2, 65], BF16) for _ in range(8)]
        PR = [a2([512], BF16) for _ in range(4)]
        YH = [a2([512], BF16) for _ in range(2)]
        TMPn = [a2([128]) for _ in range(2)]
        self.BND = a2([3, 4, 128])
        for i in range(3):
            for h in range(4):
                self.ts("dve", self.BND[:, i, h, :], self.T01[:, h, 128:256], self.CC[:, 4 + i:5 + i], ALU.mult,
                        ["T01", "CC", "COLV"], ["BND"], s2=self.COLV[:, i, h:h + 1], op1=ALU.add)
        for v in VR:
            self.memset("pool", v, 1.0, ["vr_init"])
        self.B.barrier()
        OS = [ar.alloc([512]) for _ in range(2)]
        RL = ar.alloc([512])
        SQ = ar.alloc([512])
        nq = self.nq
        hb = NB // 2

        def kview(t, base):
            return t.ap().rearrange("r c -> (r c)")[base:base + nq].rearrange("(p t) -> p t", p=128)

        def vview(t, base):
            return t.ap().rearrange("r c -> (r c)")[base:base + nq].rearrange("(b p c) -> p b c", p=128, c=256)

        remote = [([kview(self.g_kv_out[l][c_], i * nq) for c_ in range(2)],
                   [vview(self.g_kv_out[l][2 + q], i * nq) for q in range(2)]) for i in range(3)]
        local = ([kview(self.g_kv_in[l][c_], 0) for c_ in range(2)], [vview(self.g_kv_in[l][2 + q], 0) for q in range(2)])
        ring = 0
        pr = 0
        sc = 0
        for c in range(2):
            for qt in range(self.NT):
                q0 = qt * 512
                blocks = [("r", i, kb) for i in range(3) for kb in range(NB)] + [("l", 0, kb) for kb in range(4 * qt + 4)]
                nblk = len(blocks)
                for bi, (kind, i, kb) in enumerate(blocks):
                    gk, gv = remote[i] if kind == "r" else local
                    rk = ring % 8
                    ring += 1
                    pre = "gkvo%d" if kind == "r" else "gkvi%d"
                    self.ld(KR[rk], gk[c][:, kb * 128:(kb + 1) * 128], writes=["KR%d" % rk], reads=[pre % c])
                    self.ld(VR[rk][:, :, 0:64], gv[kb // hb][:, kb % hb, c * 128:(c + 1) * 128].rearrange("p (h d) -> p h d", h=2),
                            writes=["VR%d" % rk], reads=[pre % (2 + kb // hb)])
                    for hh in range(2):
                        h = 2 * c + hh
                        hp = slice(hh * 64, (hh + 1) * 64)
                        for mm_ in range(2):
                            ps = self.PS[sc % 4]
                            pk = "ps%d" % (sc % 4)
                            sc += 1
                            self.mm(ps[:, :512], KR[rk][hp, :], self.QZ[hp, c, mm_, q0:q0 + 512], True, True,
                                    ["KR%d" % rk, "QZ.%d" % c], [pk])
                            pt = PR[pr % 4]
                            ptk = "PR%d" % (pr % 4)
                            pr += 1
                            if kind == "r":
                                lo = 0
                                if qt == 0 and kb == NB - 1:
                                    lo = 128
                                    tmp = TMPn[pr % 2]
                                    self.stt(tmp, ps[:, 0:128], QSCALE, self.BND[:, i, h, :], ALU.mult, ALU.add,
                                             [pk, "BND"], ["TMPn%d" % (pr % 2)])
                                    self.act(pt[:, 0:128], tmp, AF.Exp, ["TMPn%d" % (pr % 2)], [ptk])
                                self.act(pt[:, lo:512], ps[:, lo:512], AF.Exp, [pk, "BC"], [ptk], scale=QSCALE,
                                         bias=self.BC[:, i, h:h + 1])
                            else:
                                r0 = kb - 4 * qt
                                if r0 > 0:
                                    self.memset("pool", pt[:, 0:r0 * 128], 0.0, [ptk])
                                for (r, off) in ((r0, 0), (r0 + 1, 128)):
                                    if 0 <= r <= 3:
                                        tmp = TMPn[(pr + r) % 2]
                                        tk = "TMPn%d" % ((pr + r) % 2)
                                        self.stt(tmp, ps[:, r * 128:(r + 1) * 128], QSCALE, self.T01[:, h, off:off + 128],
                                                 ALU.mult, ALU.add, [pk, "T01"], [tk])
                                        self.act(pt[:, r * 128:(r + 1) * 128], tmp, AF.Exp, [tk], [ptk])
                                lo = max(r0 + 2, 0) * 128
                                if lo < 512:
                                    self.act(pt[:, lo:512], ps[:, lo:512], AF.Exp, [pk, "RB"], [ptk], scale=QSCALE,
                                             bias=self.CH[:, h:h + 1])
                            ob = 4 + hh * 2 + mm_
                            self.mm(self.PS[ob][0:65, :512], VR[rk][:, hh, 0:65], pt, bi == 0, bi == nblk - 1,
                                    ["VR%d" % rk, ptk], ["ps%d" % ob])
                for hh in range(2):
                    for mm_ in range(2):
                        ob = 4 + hh * 2 + mm_
                        self.cp("act", OS[mm_][0:65, :], self.PS[ob][0:65, :512], ["ps%d" % ob], ["OS%d" % mm_])
                        self.mm(self.PS[0][0:64, :512], self.e65[0:65, :], OS[mm_][0:65, :], True, True,
                                ["e65", "OS%d" % mm_], ["ps0"])
                        self.B.op("dve", lambda: nc.vector.reciprocal(out=RL[0:64, :], in_=self.PS[0][0:64, :512]),
                                  ["ps0"], ["RL"])
                        self.tt("dve", OS[mm_][0:64, :], OS[mm_][0:64, :], RL[0:64, :], ALU.mult, ["OS%d" % mm_, "RL"],
                                ["OS%d" % mm_])
                    A_ = OS[0]
                    self.stt(A_[0:64, :], OS[1][0:64, :], P["nlam"][0:64, 0:1], OS[0][0:64, :], ALU.mult, ALU.add,
                             ["OS0", "OS1", "nlam"], ["OS0"])
                    self.act(SQ[0:64, :], A_[0:64, :], AF.Square, ["OS0"], ["aSQ"])
                    self.mm(self.PS[1][0:64, :512], self.ones64[0:64, :], SQ[0:64, :], True, True, ["ones64", "aSQ"], ["ps1"])
                    self.act(SQ[0:64, :], self.PS[1][0:64, :512], AF.Sqrt, ["ps1", "aSQ"], ["aSQ"], bias=EPS)
                    self.B.op("dve", lambda: nc.vector.reciprocal(out=SQ[0:64, :], in_=SQ[0:64, :]), ["aSQ"], ["aSQ"])
                    self.stt(YH[hh][0:64, :], A_[0:64, :], P["subg"][0:64, 0:1], SQ[0:64, :], ALU.mult, ALU.mult,
                             ["OS0", "subg", "aSQ"], ["YH%d" % hh])
                for hh in range(2):
                    self.mm(self.PS[2][:, :512], self.sh64[0:64, hh, :], YH[hh][0:64, :], hh == 0, hh == 1,
                            ["sh64", "YH%d" % hh], ["ps2"])
                self.cp("act", self.catc[:, c, q0:q0 + 512], self.PS[2][:, :512], ["ps2"], ["catc.%d" % c])
        self.B.barrier()
        self.B.barrier()
        ar.release(m)

    def wout(self, l, P):
        srcs = [(self.cat["a"], "cata.%d"), (self.cat["b"], "catb.%d"), (self.catc, "catc.%d"), (self.cat["d"], "catd.%d")]
        cy = 0
        for dc in range(KC):
            buf, wk = self.ws_next(("wout", l, dc))
            slab = self.slabv(buf)
            for ti, (t0, tn) in enumerate(self.tiles):
                ps = self.PS[4 + cy % 2]
                pk = "ps%d" % (4 + cy % 2)
                cy += 1
                for e in range(KC):
                    if ti < self.NT:
                        src, kf = srcs[e // 2]
                        rhs = src[:, e % 2, t0:t0 + tn]
                        rk = kf % (e % 2)
                    else:
                        rhs = self.cats[:, e, :]
                        rk = "cats"
                    self.mm(ps[:, :tn], slab[:, e, :], rhs, e == 0, e == KC - 1, [wk, rk], [pk])
                xr = self.xres[:, dc, t0:t0 + tn]
                self.tt("dve", xr, ps[:, :tn], xr, ALU.add, [pk, "rx.%d.%d" % (dc, ti)], ["rx.%d.%d" % (dc, ti)])


_PROG_CACHE = {}


def get_prog(TL, DEPTH, n_pool_pages, stages=99, dbg=False):
    key = (TL, DEPTH, n_pool_pages, stages, dbg)
    if key not in _PROG_CACHE:
        _PROG_CACHE[key] = Prog(TL, DEPTH, n_pool_pages, stages, dbg)
    return _PROG_CACHE[key]


def host_consts(TL, core):
    j = core % 4
    c = {}
    c["c_ident"] = np.eye(128, dtype=np.float32)
    k = np.arange(128)[:, None]
    q = np.arange(128)[None, :]
    c["c_maskc"] = (k <= q).astype(np.float32)
    idx0 = np.where(q >= k, rel_bucket_np(q - k), -1).astype(np.float32)
    idx1 = rel_bucket_np(128 + q - k).astype(np.float32)
    c["c_idx01"] = np.concatenate([idx0, idx1], axis=1)
    p = np.arange(128)
    pos = (p // 8)[:, None] * 128 + (p % 8)[:, None] * 16 + np.arange(16)[None, :]
    c["c_idxs"] = rel_bucket_np(PAST - pos).astype(np.float32)
    segm = np.ones((128, 512), np.float32)
    segm[:, 0::64] = 0.0
    c["c_mask2"] = ((k <= q) & ((k // 64) == (q // 64))).astype(np.float32)
    c["c_segm"] = segm
    cc = np.zeros((1, 32), np.float32)
    cc[0, 0:4] = [1.0 if i == j else 0.0 for i in range(4)]
    cc[0, 4:8] = [1.0 if i == j - 1 else 0.0 for i in range(4)]
    cc[0, 8:12] = [1.0 if i < j else 0.0 for i in range(4)]
    cc[0, 12:16] = [1.0 if i < j - 1 else 0.0 for i in range(4)]
    c["c_core"] = cc
    invc = np.zeros((128, 2, 16), np.float32)
    for ch in range(2):
        for half in range(2):
            win = 2 ** (ch * 2 + half + 1)
            for t in range(16):
                cntv = min(t + 1, win) if j == 0 else win
                invc[half * 64:(half + 1) * 64, ch, t] = 1.0 / cntv
    c["c_invc"] = invc.reshape(128, 32)
    sel = np.zeros((16, 16, 128), np.float32)
    for n in range(16):
        sel[n, n, :] = 1.0
    c["c_sel16"] = sel.reshape(16, 16 * 128)
    misc = np.zeros((128, 64), np.float32)
    misc[:, 0] = np.arange(128) % 8
    c["c_misc"] = misc
    c8 = np.zeros((8, 514), np.float32)
    for jj in range(8):
        c8[jj, 0] = 1.0 if jj % 2 == 0 else 0.0
        c8[jj, 1] = 1.0 if jj % 2 == 1 else 0.0
        hh_ = jj // 2
        c8[jj, 2 + hh_ * 64:2 + (hh_ + 1) * 64] = 1.0
    oh = np.zeros((8, 16, 16), np.float32)
    for n_ in range(16):
        oh[:, n_, n_] = 1.0
    c8[:, 258:514] = oh.reshape(8, 256)
    c["c_c8"] = c8
    return c


def kernel(x_prompt, x_sample, state_pool, state_hgrn, cache_k, cache_v, page_table,
           rel_bias, ln_g, ln_b, ffn1_w_gu, ffn1_w_dn, ffn2_w_gu, ffn2_w_dn, w_in, w_out,
           pool_w, pool_scale, hgrn_lb, hgrn_norm_g, diff_lam_q1, diff_lam_k1,
           diff_lam_q2, diff_lam_k2, diff_subln_g, sgu_ln_g, sgu_ln_b, sgu_w, sgu_b,
           _stages=99, _dbg=False):
    f = lambda a: np.ascontiguousarray(np.asarray(a))
    x_prompt = f(x_prompt)
    Bp, T, _ = x_prompt.shape
    L = ln_g.shape[0]
    TL = T // 4
    n_pool = cache_k.shape[1]
    prog = get_prog(TL, L, n_pool, _stages, _dbg)
    shared = {

        "rel_bias": f(rel_bias).reshape(1, 128),
        "ln_g": f(ln_g).reshape(L * 3 * KC, 128), "ln_b": f(ln_b).reshape(L * 3 * KC, 128),
        "ffn1_w_gu": f(ffn1_w_gu), "ffn1_w_dn": f(ffn1_w_dn), "ffn2_w_gu": f(ffn2_w_gu), "ffn2_w_dn": f(ffn2_w_dn),
        "w_in": f(w_in), "w_out": f(w_out), "pool_w": f(pool_w), "pool_scale": f(pool_scale),
        "hgrn_lb": f(hgrn_lb).reshape(1, L * 256), "hgrn_norm_g": f(hgrn_norm_g),
        "diff_lam_q1": f(diff_lam_q1), "diff_lam_k1": f(diff_lam_k1), "diff_lam_q2": f(diff_lam_q2),
        "diff_lam_k2": f(diff_lam_k2), "diff_subln_g": f(diff_subln_g), "sgu_ln_g": f(sgu_ln_g),
        "sgu_ln_b": f(sgu_ln_b), "sgu_w": f(sgu_w), "sgu_b": f(sgu_b),
    }
    ck = f(cache_k).reshape(L, n_pool * 8, 4096)
    cv = f(cache_v).reshape(L, n_pool * 8, 4096)
    for l in range(L):
        shared["cache_k%d" % l] = ck[l]
        shared["cache_v%d" % l] = cv[l]
    xs = f(x_sample).reshape(8, NS, D_MODEL)
    sp = f(state_pool).reshape(L, 8, NS, 15 * 256)
    sh = f(state_hgrn).reshape(L, 8, NS, 4, 4096)
    pt = f(page_table).astype(np.int32).reshape(8, NS, NPG)
    in_maps = []
    for c in range(8):
        b, j = c // 4, c % 4
        m = dict(shared)
        m["xp"] = x_prompt[b, j * TL:(j + 1) * TL]
        m["xs"] = xs[c]
        m["spool"] = np.ascontiguousarray(sp[:, c])
        m["shgrn"] = np.ascontiguousarray(sh[:, c])
        m["pt"] = pt[c]
        m.update(host_consts(TL, c))
        in_maps.append(m)
    res = run_bass_kernel_spmd(prog.nc, in_maps, core_ids=list(range(8)))
    R = res.results
    y = np.stack([np.concatenate([R[b * 4 + j]["o_y"] for j in range(4)], 0) for b in range(2)], 0)
    ys = np.concatenate([R[c]["o_ys"] for c in range(8)], 0).reshape(8 * NS, 1, D_MODEL)
    pool_p = np.stack([R[b * 4 + 3]["o_poolp"] for b in range(2)], 1)
    pool_s = np.concatenate([R[c]["o_pools"].reshape(L, NS, 15, 256) for c in range(8)], 1)
    hg_p = np.stack([R[b * 4 + 3]["o_hgp"] for b in range(2)], 1)
    hg_s = np.concatenate([R[c]["o_hgs"].reshape(L, NS, 4, 64, 64) for c in range(8)], 1)
    k_p = np.stack([np.concatenate([R[b * 4 + j]["o_kp"] for j in range(4)], 1) for b in range(2)], 1).reshape(L, 2, T, 4, 64)
    k_s = np.concatenate([R[c]["o_ks"] for c in range(8)], 1).reshape(L, 8 * NS, 1, 4, 64)
    v_p = np.stack([np.concatenate([R[b * 4 + j]["o_vp"] for j in range(4)], 1) for b in range(2)], 1).reshape(L, 2, T, 4, 64)
    v_s = np.concatenate([R[c]["o_vs"] for c in range(8)], 1).reshape(L, 8 * NS, 1, 4, 64)
    sg_p = np.stack([R[b * 4 + 3]["o_sgp"] for b in range(2)], 1)
    sg_s = np.concatenate([R[c]["o_sgs"] for c in range(8)], 1).reshape(L, 8 * NS, 1, 256)
    return (y, ys, pool_p, pool_s, hg_p, hg_s, k_p, k_s, v_p, v_s, sg_p, sg_s)
```

```python
import bisect
import math
import numpy as np
import concourse.bass as bass
import concourse.mybir as mybir
from concourse.bass_utils import run_bass_kernel_spmd

F32 = mybir.dt.float32
BF16 = mybir.dt.bfloat16
I32 = mybir.dt.int32
AF = mybir.ActivationFunctionType
ALU = mybir.AluOpType
AX = mybir.AxisListType

D_MODEL = 1024
KC = 8
D_FF = 2816
FC = 22
FH = 11
D_IN = 2560
NS = 16
PAGE = 128
NPG = 16
PAST = 2048
EPS = 1e-5
NEG = -30000.0
QSCALE = 32 ** -0.5


class Ev:
    __slots__ = ("eng", "seq", "instr", "sem", "val", "dma")

    def __init__(self, eng, seq, instr, dma=False):
        self.eng = eng
        self.seq = seq
        self.instr = instr
        self.sem = None
        self.val = None
        self.dma = dma


class Builder:
    COMPUTE = ("pe", "dve", "act", "pool")

    def __init__(self, nc, n_dma_sems=48):
        self.nc = nc
        self.E = {"pe": nc.tensor, "dve": nc.vector, "act": nc.scalar, "pool": nc.gpsimd, "sp": nc.sync}
        self.csem = {e: nc.semaphore("c_" + e).__enter__() for e in self.COMPUTE}
        self.ccount = {e: 0 for e in self.COMPUTE}
        self.marked = {e: [] for e in self.COMPUTE}
        self.seq = {e: 0 for e in self.E}
        self.last = {}
        self.dsems = [nc.semaphore("d%d" % i).__enter__() for i in range(n_dma_sems)]
        self.dval = [0] * n_dma_sems
        self.dnext = 0
        self.dnext_sw = 0
        self.ccsem = nc.semaphore("ccsem").__enter__()
        self.ccval = 0
        self.waited = {}
        self.state = {}
        self.n_wait = 0
        self.n_ins = 0

    def _resolve(self, ev):
        if ev.val is not None:
            return ev.sem, ev.val
        e = ev.eng
        lst = self.marked[e]
        i = bisect.bisect_left(lst, (ev.seq, -1))
        if i < len(lst):
            return self.csem[e], lst[i][1]
        self.ccount[e] += 1
        ev.instr.then_inc(self.csem[e], 1)
        ev.sem = self.csem[e]
        ev.val = self.ccount[e]
        lst.append((ev.seq, ev.val))
        return ev.sem, ev.val

    def _wait(self, eng, sem, val):
        key = (eng, id(sem))
        if self.waited.get(key, 0) >= val:
            return
        self.E[eng].wait_ge(sem, val)
        self.waited[key] = val
        self.n_wait += 1

    def _deps(self, eng, reads, writes):
        deps = []
        for k in reads:
            st = self.state.get(k)
            if st:
                deps.extend(st[0])
                if k.startswith("ps"):
                    deps.extend(o for o in st[1] if o.eng != eng)
        for k in writes:
            st = self.state.get(k)
            if st:
                deps.extend(st[0])
                deps.extend(st[1])
        seen = set()
        for ev in deps:
            if id(ev) in seen:
                continue
            seen.add(id(ev))
            if ev.eng == eng and eng == "pe" and not ev.dma:
                continue
            sem, val = self._resolve(ev)
            self._wait(eng, sem, val)

    def _update(self, ev, reads, writes):
        for k in writes:
            self.state[k] = ([ev], [])
        for k in reads:
            st = self.state.get(k)
            if st is None:
                st = ([], [])
                self.state[k] = st
            rl = st[1]
            if not ev.dma:
                for i, o in enumerate(rl):
                    if (not o.dma) and o.eng == ev.eng:
                        rl[i] = ev
                        break
                else:
                    rl.append(ev)
            else:
                rl.append(ev)

    def op(self, eng, fn, reads=(), writes=()):
        self._deps(eng, reads, writes)
        instr = fn()
        ev = Ev(eng, self.seq[eng], instr)
        self.seq[eng] += 1
        self.last[eng] = ev
        self._update(ev, reads, writes)
        self.n_ins += 1
        return ev

    def dma(self, eng, fn, reads=(), writes=()):
        self._deps(eng, reads, writes)
        half = len(self.dsems) // 2
        if eng == "pool":
            i = half + self.dnext_sw
            self.dnext_sw = (self.dnext_sw + 1) % (len(self.dsems) - half)
        else:
            i = self.dnext
            self.dnext = (self.dnext + 1) % half
        self._wait(eng, self.dsems[i], self.dval[i])
        instr = fn()
        self.dval[i] += 16
        instr.then_inc(self.dsems[i], 16)
        ev = Ev(eng, self.seq[eng], instr, dma=True)
        self.seq[eng] += 1
        ev.sem = self.dsems[i]
        ev.val = self.dval[i]
        self._update(ev, reads, writes)
        self.n_ins += 1
        return ev

    def cc(self, fn, reads=(), writes=()):
        eng = "pool"
        self._deps(eng, reads, writes)
        instr = fn()
        self.ccval += 1
        instr.then_inc(self.ccsem)
        ev = Ev(eng, self.seq[eng], instr, dma=True)
        self.seq[eng] += 1
        ev.sem = self.ccsem
        ev.val = self.ccval
        self._update(ev, reads, writes)
        return ev

    def barrier(self):
        evs = [self.last[e] for e in self.COMPUTE if e in self.last]
        res = [self._resolve(ev) for ev in evs]
        for y in self.E:
            for (sem, val) in res:
                self._wait(y, sem, val)
            for i, s in enumerate(self.dsems):
                if self.dval[i]:
                    self._wait(y, s, self.dval[i])
            if self.ccval:
                self._wait(y, self.ccsem, self.ccval)
        self.state = {}

    def finish(self):
        self.barrier()


class Arena:
    def __init__(self, nc, words):
        self.t = nc.sbuf_tensor("arena", [128, words], F32).__enter__()
        self.words = words
        self.top = 0
        self.peak = 0

    def alloc(self, shape, dtype=F32):
        n = 1
        for s in shape:
            n *= s
        w = n if dtype in (F32, I32) else (n + 1) // 2
        a = self.top
        self.top += w
        self.peak = max(self.peak, self.top)
        assert self.top <= self.words, ("arena overflow", self.top, self.words)
        v = self.t[:, a:a + w]
        if dtype == BF16:
            v = v.bitcast(BF16)
            if 2 * w != n:
                v = v[:, :n]
        elif dtype == I32:
            v = v.bitcast(I32)
        if len(shape) == 2:
            v = v.rearrange("p (a b) -> p a b", a=shape[0], b=shape[1])
        elif len(shape) == 3:
            v = v.rearrange("p (a b c) -> p a b c", a=shape[0], b=shape[1], c=shape[2])
        elif len(shape) == 4:
            v = v.rearrange("p (a b c d) -> p a b c d", a=shape[0], b=shape[1], c=shape[2], d=shape[3])
        return v

    def mark(self):
        return self.top

    def release(self, m):
        self.top = m


def rel_bucket_np(dist):
    n = np.maximum(dist, 0)
    max_exact = 16
    large = max_exact + (np.log(np.maximum(n, 1).astype(np.float32) / np.float32(max_exact))
                         / np.float32(math.log(128 / max_exact)) * np.float32(32 - max_exact)).astype(np.int32)
    large = np.minimum(large, 31)
    return np.where(n < max_exact, n, large)


class StopBuild(Exception):
    pass


class Prog:
    def __init__(self, TL, DEPTH, n_pool_pages, stages=99, dbg=False):
        self.TL = TL
        self.TT = TL + NS
        self.DEPTH = DEPTH
        self.NB = TL // 128
        self.NT = TL // 512
        self.NCH = TL // 64
        self.n_pool_pages = n_pool_pages
        self.stages = stages
        self.dbg = dbg
        self.tiles = [(i * 512, 512) for i in range(self.NT)] + [(TL, NS)]
        self.ptiles = self.tiles[:-1]
        self.ALPHA = (2.0 * DEPTH) ** 0.25
        nc = bass.Bass("TRN2", target_bir_lowering=False)
        self.nc = nc
        self.B = Builder(nc)
        self.ar = Arena(nc, 52000)
        self.PS = [nc.psum_tensor("ps%d" % i, [128, 512], F32).__enter__() for i in range(8)]
        self.outs = []
        self.declare_io()
        self.build()

    def din(self, name, shape, dt=F32):
        return self.nc.dram_tensor(name, list(shape), dt, kind="ExternalInput").ap()

    def dout(self, name, shape, dt=F32):
        self.outs.append(name)
        return self.nc.dram_tensor(name, list(shape), dt, kind="ExternalOutput").ap()

    def declare_io(self):
        TL, L = self.TL, self.DEPTH
        self.i_xp = self.din("xp", [TL, D_MODEL])
        self.i_xs = self.din("xs", [NS, D_MODEL])
        self.i_spool = self.din("spool", [L, NS, 15 * 256])
        self.i_shgrn = self.din("shgrn", [L, NS, 4, 4096])
        self.i_ck = [self.din("cache_k%d" % l, [self.n_pool_pages * 8, 4096]) for l in range(L)]
        self.i_cv = [self.din("cache_v%d" % l, [self.n_pool_pages * 8, 4096]) for l in range(L)]
        self.i_pt = self.din("pt", [NS, NPG], I32)
        self.i_relb = self.din("rel_bias", [1, 128])
        self.i_lng = self.din("ln_g", [L * 3 * KC, 128])
        self.i_lnb = self.din("ln_b", [L * 3 * KC, 128])
        self.i_wgu = [self.din("ffn1_w_gu", [L, D_MODEL, 2 * D_FF]), self.din("ffn2_w_gu", [L, D_MODEL, 2 * D_FF])]
        self.i_wdn = [self.din("ffn1_w_dn", [L, D_FF, D_MODEL]), self.din("ffn2_w_dn", [L, D_FF, D_MODEL])]
        self.i_win = self.din("w_in", [L, D_MODEL, D_IN])
        self.i_wout = self.din("w_out", [L, D_MODEL, D_MODEL])
        self.i_poolw = self.din("pool_w", [L, 4, 64, 64])
        self.i_pscale = self.din("pool_scale", [L, 256])
        self.i_lb = self.din("hgrn_lb", [1, L * 256])
        self.i_hgg = self.din("hgrn_norm_g", [L, 64])
        self.i_lam = [self.din(n, [L, 32]) for n in ("diff_lam_q1", "diff_lam_k1", "diff_lam_q2", "diff_lam_k2")]
        self.i_subg = self.din("diff_subln_g", [L, 64])
        self.i_sglg = self.din("sgu_ln_g", [L, 256])
        self.i_sglb = self.din("sgu_ln_b", [L, 256])
        self.i_sgw = self.din("sgu_w", [L, 4, 128, 128])
        self.i_sgb = self.din("sgu_b", [L, 4, 128])
        self.i_ident = self.din("c_ident", [128, 128])
        self.i_maskc = self.din("c_maskc", [128, 128])
        self.i_idx01 = self.din("c_idx01", [128, 256])
        self.i_idxs = self.din("c_idxs", [128, 16])
        self.i_segm = self.din("c_segm", [128, 512])
        self.i_mask2 = self.din("c_mask2", [128, 128])
        self.i_ccore = self.din("c_core", [1, 32])
        self.i_invc = self.din("c_invc", [128, 32])
        self.i_sel16 = self.din("c_sel16", [16, 16 * 128])
        self.i_misc = self.din("c_misc", [128, 64])
        self.i_c8 = self.din("c_c8", [8, 514])
        self.o_y = self.dout("o_y", [TL, D_MODEL])
        self.o_ys = self.dout("o_ys", [NS, D_MODEL])
        self.o_poolp = self.dout("o_poolp", [L, 15, 256])
        self.o_pools = self.dout("o_pools", [L, NS, 15 * 256])
        self.o_hgp = self.dout("o_hgp", [L, 4, 64, 64])
        self.o_hgs = self.dout("o_hgs", [L, NS, 4, 4096])
        self.o_kp = self.dout("o_kp", [L, TL, 256])
        self.o_ks = self.dout("o_ks", [L, NS, 256])
        self.o_vp = self.dout("o_vp", [L, TL, 256])
        self.o_vs = self.dout("o_vs", [L, NS, 256])
        self.o_sgp = self.dout("o_sgp", [L, 128, 256])
        self.o_sgs = self.dout("o_sgs", [L, NS, 256])
        nq = 128 * TL
        self.nq = nq
        self.g_kv_in = [[self.nc.dram_tensor("gkvi%d_%d" % (l, q), [nq // 512, 512], BF16) for q in range(4)] for l in range(L)]
        self.g_kv_out = [[self.nc.dram_tensor("gkvo%d_%d" % (l, q), [4 * nq // 512, 512], BF16) for q in range(4)] for l in range(L)]
        self.g_sm_in = [[self.nc.dram_tensor("gsmi%d_%d" % (l, g), [128, 161], F32) for g in range(2)] for l in range(L)]
        self.g_sm_out = [[self.nc.dram_tensor("gsmo%d_%d" % (l, g), [512, 161], F32) for g in range(2)] for l in range(L)]

    def mm(self, out, lhsT, rhs, start, stop, reads, writes):
        nc = self.nc
        return self.B.op("pe", lambda: nc.tensor.matmul(out, lhsT=lhsT, rhs=rhs, start=start, stop=stop),
                         reads=reads, writes=writes)

    def tr(self, out, in_, ident, reads, writes):
        nc = self.nc
        return self.B.op("pe", lambda: nc.tensor.transpose(out, in_, ident), reads=reads, writes=writes)

    def act(self, out, in_, func, reads, writes, scale=1.0, bias=0.0):
        nc = self.nc
        return self.B.op("act", lambda: nc.scalar.activation(out=out, in_=in_, func=func, scale=scale, bias=bias),
                         reads=reads, writes=writes)

    def tt(self, eng, out, in0, in1, op, reads, writes):
        e = self.B.E[eng]
        return self.B.op(eng, lambda: e.tensor_tensor(out=out, in0=in0, in1=in1, op=op), reads=reads, writes=writes)

    def ts(self, eng, out, in0, s1, op0, reads, writes, s2=None, op1=None):
        e = self.B.E[eng]
        if op1 is None:
            return self.B.op(eng, lambda: e.tensor_scalar(out=out, in0=in0, scalar1=s1, scalar2=None, op0=op0),
                             reads=reads, writes=writes)
        return self.B.op(eng, lambda: e.tensor_scalar(out=out, in0=in0, scalar1=s1, scalar2=s2, op0=op0, op1=op1),
                         reads=reads, writes=writes)

    def stt(self, out, in0, scalar, in1, op0, op1, reads, writes):
        nc = self.nc
        return self.B.op("dve", lambda: nc.vector.scalar_tensor_tensor(out=out, in0=in0, scalar=scalar, in1=in1,
                                                                       op0=op0, op1=op1), reads=reads, writes=writes)

    def cp(self, eng, out, in_, reads, writes):
        e = self.B.E[eng]
        if eng == "act":
            return self.act(out, in_, AF.Copy, reads, writes)
        return self.B.op(eng, lambda: e.tensor_copy(out=out, in_=in_), reads=reads, writes=writes)

    def memset(self, eng, out, val, writes):
        e = self.B.E[eng]
        return self.B.op(eng, lambda: e.memset(out, val), writes=writes)

    def ld(self, out, in_, writes, reads=(), eng="sp", **kw):
        e = self.B.E[eng]
        return self.B.dma(eng, lambda: e.dma_start(out=out, in_=in_, **kw), reads=reads, writes=writes)

    def st(self, out, in_, reads, writes=(), eng="sp", **kw):
        e = self.B.E[eng]
        return self.B.dma(eng, lambda: e.dma_start(out=out, in_=in_, **kw), reads=reads, writes=writes)

    def ws_init(self):
        self.ws_nbuf = 4
        self.ws_depth = 3
        self.ws_bufs = [self.ar.alloc([2048], BF16) for _ in range(self.ws_nbuf)]
        self.ws_specs = []
        self.ws_issued = 0
        self.ws_cur = 0
        self.ws_rel = 0

    def ws_issue_upto(self, n):
        n = min(n, len(self.ws_specs))
        while self.ws_issued < n:
            i = self.ws_issued
            tag, parts = self.ws_specs[i]
            b = i % self.ws_nbuf
            for (off, shape, src) in parts:
                nel = 1
                for s in shape[1:]:
                    nel *= s
                dst = self.ws_bufs[b][:shape[0], off:off + nel]
                if len(shape) == 3:
                    dst = dst.rearrange("p (a b) -> p a b", a=shape[1], b=shape[2])
                elif len(shape) == 4:
                    dst = dst.rearrange("p (a b c) -> p a b c", a=shape[1], b=shape[2], c=shape[3])
                self.ld(dst, src, writes=["ws%d" % b], eng="pool")
            self.ws_issued += 1

    def ws_next(self, tag, hold=False):
        i = self.ws_cur
        t, parts = self.ws_specs[i]
        assert t == tag, (t, tag, i)
        if not hold:
            self.ws_rel = i
        self.ws_issue_upto(self.ws_rel + self.ws_nbuf)
        assert self.ws_issued > i, "holding more slabs than ring slots"
        self.ws_cur += 1
        b = i % self.ws_nbuf
        return self.ws_bufs[b], "ws%d" % b

    def ws_plan(self):
        specs = []
        for l in range(self.DEPTH):
            for which in (0, 1):
                if which == 1 and self.stages >= 2:
                    wv = self.i_win[l].rearrange("(k p) e -> p k e", p=128)
                    order = []
                    if self.stages >= 4:
                        order += list(range(20))
                    order += [4, 2, 8, 6, 18, 19, 16, 17, 0, 1, 5, 3, 9, 7, 12, 13, 14, 15, 10, 11]
                    for e in order:
                        specs.append((("win", l, e), [(0, [128, KC, 128], wv[:, :, e * 128:(e + 1) * 128])]))
                    wo = self.i_wout[l].rearrange("(k p) d -> p k d", p=128)
                    for dc in range(KC):
                        specs.append((("wout", l, dc), [(0, [128, KC, 128], wo[:, :, dc * 128:(dc + 1) * 128])]))
                wgu = self.i_wgu[which][l].rearrange("(k p) f -> p k f", p=128)
                wdn = self.i_wdn[which][l].rearrange("(f p) d -> p f d", p=128)
                for fh in range(2):
                    for fc in range(FH):
                        f = fh * FH + fc
                        specs.append((("gu", l, which, f),
                                      [(0, [128, KC, 128], wgu[:, :, f * 128:(f + 1) * 128]),
                                       (1024, [128, KC, 128], wgu[:, :, D_FF + f * 128:D_FF + (f + 1) * 128])]))
                    for dc in range(KC):
                        specs.append((("dn", l, which, fh, dc),
                                      [(0, [128, FH, 128], wdn[:, fh * FH:(fh + 1) * FH, dc * 128:(dc + 1) * 128])]))
        self.ws_specs = specs

    def build(self):
        import os
        self.const_setup()
        self.ws_init()
        self.ws_plan()
        if os.environ.get("KSKIP", "") == "load":
            self.st(self.o_y[0:128, 0:128], self.identF, reads=["identF"])
            self.B.finish()
            return
        self.load_x()
        if os.environ.get("KSKIP", "") == "store":
            self.st(self.o_y[0:128, :], self.xres[:, 0, 0:1024] if self.TL >= 1024 else self.xres[:, 0:2, 0:512], reads=[])
            self.B.finish()
            return
        for l in range(self.DEPTH):
            if self.stages < 1:
                break
            self.ffn(l, 0, last=(self.stages < 2))
            if self.stages >= 2:
                try:
                    self.mixer(l)
                except StopBuild:
                    self.B.barrier()
                    break
                self.ffn(l, 1, last=(l == self.DEPTH - 1))
        self.store_y()
        self.B.finish()

    def const_setup(self):
        ar, L = self.ar, self.DEPTH
        self.identF = ar.alloc([128])
        self.ld(self.identF, self.i_ident[:, :], writes=["identF"])
        self.identB = ar.alloc([128], BF16)
        self.cp("dve", self.identB, self.identF, ["identF"], ["identB"])
        self.onesF = ar.alloc([128])
        self.memset("dve", self.onesF, 1.0 / D_MODEL, ["onesF"])
        self.xres = ar.alloc([KC, self.TT + 112])
        a0 = ar.top
        self.xbf = ar.alloc([KC, self.TT + 112], BF16)
        self.xbf_words = (a0, ar.top)
        self.memset("dve", self.xres, 0.0, ["xres_init"])
        self.memset("pool", self.xbf, 0.0, ["xbf_init"])
        n = L * 3 * KC
        self.lng = ar.alloc([n])
        self.lnb = ar.alloc([n])
        self.lnag = ar.alloc([n])
        self.lnab = ar.alloc([n])
        m = ar.mark()
        tmp = ar.alloc([128])
        import os
        for (src, dst, key) in (() if os.environ.get("SKIPLN") else ((self.i_lng, self.lng, "lng"), (self.i_lnb, self.lnb, "lnb"))):
            self.memset("dve", tmp, 0.0, ["ptmp"])
            self.ld(tmp[:n, :], src[:, :], writes=["ptmp"])
            self.tr(self.PS[0][:, :128], tmp, self.identF, ["ptmp", "identF"], ["ps0"])
            self.cp("dve", dst, self.PS[0][:, :n], ["ps0"], [key])
        self.ts("dve", self.lnag, self.lng, self.ALPHA, ALU.mult, ["lng"], ["lnag"])
        self.ts("dve", self.lnab, self.lnb, self.ALPHA, ALU.mult, ["lnb"], ["lnab"])
        self.B.barrier()
        ar.release(m)
        self.B.barrier()
        self.mask2 = ar.alloc([128])
        self.ld(self.mask2, self.i_mask2[:, :], writes=["mask2"])
        self.attn_consts()
        self.sample_consts()

    def lncol(self, l, i, k):
        return (l * 3 + i) * KC + k

    def load_x(self):
        ar = self.ar
        m = ar.mark()
        xin = [ar.alloc([D_MODEL]) for _ in range(2)]
        import os
        for blk in range(self.NB + (0 if os.environ.get("NOSAMPLE") else 1)):
            b = blk % 2
            if blk < self.NB:
                rows, t0 = 128, blk * 128
                self.ld(xin[b], self.i_xp[t0:t0 + 128, :], writes=["xin%d" % b])
            else:
                rows, t0 = NS, self.TL
                self.memset("dve", xin[b], 0.0, ["xin%d" % b])
                self.ld(xin[b][:NS, :], self.i_xs[:, :], writes=["xin%d" % b])
            tt = min(t0 // 512, self.NT)
            for half in range(2):
                ps = self.PS[half]
                psf = ps[:, :].rearrange("p (k t) -> p k t", k=4)
                psv = psf[:, :, :rows]
                for kk in range(4):
                    k = half * 4 + kk
                    self.tr(psf[:, kk, :], xin[b][:, k * 128:(k + 1) * 128], self.identF,
                            ["xin%d" % b, "identF"], ["ps%d" % half])
                ks = ["x.%d.%d" % (half * 4 + kk, tt) for kk in range(4)]
                if os.environ.get("EV2D"):
                    for kk in range(4):
                        k = half * 4 + kk
                        self.act(self.xres[:, k, t0:t0 + rows], psv[:, kk, :], AF.Identity, ["ps%d" % half],
                                 ["r" + ks[kk]], scale=self.ALPHA)
                        self.cp("dve", self.xbf[:, k, t0:t0 + rows], psv[:, kk, :], ["ps%d" % half], ["b" + ks[kk]])
                    continue
                self.act(self.xres[:, half * 4:half * 4 + 4, t0:t0 + rows], psv, AF.Identity, ["ps%d" % half],
                         ["r" + s for s in ks], scale=self.ALPHA)
                self.cp("dve", self.xbf[:, half * 4:half * 4 + 4, t0:t0 + rows], psv, ["ps%d" % half],
                        ["b" + s for s in ks])
        self.B.barrier()
        self.B.barrier()
        ar.release(m)

    def store_y(self):
        ar = self.ar
        m = ar.mark()
        yt = [ar.alloc([D_MODEL]) for _ in range(2)]
        for blk in range(self.NB + 1):
            b = blk % 2
            rows, t0 = (128, blk * 128) if blk < self.NB else (NS, self.TL)
            tt = min(t0 // 512, self.NT)
            for half in range(2):
                ps = self.PS[half]
                psv = ps[:, :].rearrange("p (k c) -> p k c", k=4)
                for kk in range(4):
                    k = half * 4 + kk
                    self.tr(psv[:, kk, :], self.xres[:, k, t0:t0 + 128], self.identF[:, :],
                            ["rx.%d.%d" % (k, tt), "identF"], ["ps%d" % half])
                eng = "act" if half == 0 else "dve"
                r0 = 0
                self.cp(eng, yt[b][r0:r0 + rows, half * 512:(half + 1) * 512], ps[r0:r0 + rows, :], ["ps%d" % half],
                        ["yt%d.%d" % (b, half)])
            dst = self.o_y[t0:t0 + 128, :] if blk < self.NB else self.o_ys[:, :]
            r0 = 0
            self.st(dst, yt[b][r0:r0 + rows, :], reads=["yt%d.0" % b, "yt%d.1" % b])
        self.B.barrier()
        self.B.barrier()
        ar.release(m)

    def ffn(self, l, which, last=False):
        ar, B = self.ar, self.B
        m = ar.mark()
        G = ar.alloc([FH, self.TT], BF16)
        sg = [ar.alloc([512], BF16) for _ in range(2)]
        cnt = 0
        cy = 0
        for fh in range(2):
            for fc in range(FH):
                f = fh * FH + fc
                buf, wk = self.ws_next(("gu", l, which, f))
                slab = buf[:, :2048].rearrange("p (u k c) -> p k u c", k=KC, u=2)
                for ti, (t0, tn) in enumerate(self.tiles):
                    pb = 2 * (cnt % 2)
                    psg, psu = self.PS[pb], self.PS[pb + 1]
                    for u, ps in ((0, psg), (1, psu)):
                        for k in range(KC):
                            self.mm(ps[:, :tn], slab[:, k, u, :], self.xbf[:, k, t0:t0 + tn], k == 0, k == KC - 1,
                                    [wk, "bx.%d.%d" % (k, ti)], ["ps%d" % (pb + u)])
                    s = sg[cnt % 2]
                    self.act(s[:, :tn], psg[:, :tn], AF.Silu, ["ps%d" % pb], ["sg%d" % (cnt % 2)])
                    self.tt("dve", G[:, fc, t0:t0 + tn], s[:, :tn], psu[:, :tn], ALU.mult,
                            ["sg%d" % (cnt % 2), "ps%d" % (pb + 1)], ["G.%d.%d" % (fc, ti)])
                    cnt += 1
            for dc in range(KC):
                buf, wk = self.ws_next(("dn", l, which, fh, dc))
                slab = buf[:, :FH * 128].rearrange("p (f c) -> p f c", f=FH)
                for ti, (t0, tn) in enumerate(self.tiles):
                    ps = self.PS[4 + cy % 2]
                    pk = "ps%d" % (4 + cy % 2)
                    for fc in range(FH):
                        self.mm(ps[:, :tn], slab[:, fc, :], G[:, fc, t0:t0 + tn], fc == 0, fc == FH - 1,
                                [wk, "G.%d.%d" % (fc, ti)], [pk])
                    xr = self.xres[:, dc, t0:t0 + tn]
                    self.stt(xr, ps[:, :tn], 0.5, xr, ALU.mult, ALU.add, [pk, "rx.%d.%d" % (dc, ti)],
                             ["rx.%d.%d" % (dc, ti)])
                    cy += 1
        self.B.barrier()
        ar.release(m)
        self.layernorm(l, 0 if which == 0 else 2, last)

    def layernorm(self, l, i, last=False):
        ar = self.ar
        m = ar.mark()
        SQ = ar.alloc([KC, 512])
        MEAN = ar.alloc([512])
        T1 = ar.alloc([512])
        RSTD = ar.alloc([512])
        NBv = ar.alloc([512])
        for ti, (t0, tn) in enumerate(self.tiles):
            xk = ["rx.%d.%d" % (k, ti) for k in range(KC)]
            xv = self.xres[:, :, t0:t0 + tn]
            self.act(SQ[:, :, :tn], xv, AF.Square, xk, ["SQ"])
            for k in range(KC):
                self.mm(self.PS[6][:, :tn], self.onesF, self.xres[:, k, t0:t0 + tn], k == 0, k == KC - 1,
                        ["onesF", xk[k]], ["ps6"])
            for k in range(KC):
                self.mm(self.PS[7][:, :tn], self.onesF, SQ[:, k, :tn], k == 0, k == KC - 1, ["onesF", "SQ"], ["ps7"])
            self.act(MEAN[:, :tn], self.PS[6][:, :tn], AF.Copy, ["ps6"], ["MEAN"])
            self.tt("dve", T1[:, :tn], MEAN[:, :tn], MEAN[:, :tn], ALU.mult, ["MEAN"], ["T1"])
            self.tt("dve", T1[:, :tn], self.PS[7][:, :tn], T1[:, :tn], ALU.subtract, ["ps7", "T1"], ["T1"])
            self.act(T1[:, :tn], T1[:, :tn], AF.Sqrt, ["T1"], ["T1"], bias=EPS)
            nc = self.nc
            self.B.op("dve", lambda: nc.vector.reciprocal(out=RSTD[:, :tn], in_=T1[:, :tn]), ["T1"], ["RSTD"])
            self.stt(NBv[:, :tn], MEAN[:, :tn], -1.0, RSTD[:, :tn], ALU.mult, ALU.mult, ["MEAN", "RSTD"], ["NB"])
            rb = RSTD[:, :tn].unsqueeze(1).to_broadcast([128, KC, tn])
            nb = NBv[:, :tn].unsqueeze(1).to_broadcast([128, KC, tn])
            self.tt("dve", SQ[:, :, :tn], xv, rb, ALU.mult, xk + ["RSTD"], ["SQ"])
            self.tt("dve", SQ[:, :, :tn], SQ[:, :, :tn], nb, ALU.add, ["SQ", "NB"], ["SQ"])
            for k in range(KC):
                c = self.lncol(l, i, k)
                gs, bs = (self.lng, self.lnb) if last else (self.lnag, self.lnab)
                self.act(self.xres[:, k, t0:t0 + tn], SQ[:, k, :tn], AF.Identity, ["SQ", "lnag", "lnab", "lng", "lnb"],
                         ["rx.%d.%d" % (k, ti)], scale=gs[:, c:c + 1], bias=bs[:, c:c + 1])
                self.ts("pool", self.xbf[:, k, t0:t0 + tn], SQ[:, k, :tn], self.lng[:, c:c + 1], ALU.mult,
                        ["SQ", "lng", "lnb"], ["bx.%d.%d" % (k, ti)], s2=self.lnb[:, c:c + 1], op1=ALU.add)
        self.B.barrier()
        ar.release(m)

    def attn_consts(self):
        ar, L = self.ar, self.DEPTH
        RB = ar.alloc([128])
        self.ld(RB, self.i_relb.partition_broadcast(128), writes=["RB"])
        CC = ar.alloc([32])
        self.ld(CC, self.i_ccore.partition_broadcast(128), writes=["CC"])
        self.CC, self.RB = CC, RB
        self.CH = RB[:, 124:128]
        self.maskc = ar.alloc([128])
        self.ld(self.maskc, self.i_maskc[:, :], writes=["maskc"])
        self.segm = ar.alloc([512])
        self.ld(self.segm, self.i_segm[:, :], writes=["segm"])
        self.T01 = ar.alloc([4, 256])
        self.BSh = ar.alloc([4, 16])
        self.BC = ar.alloc([3, 4])
        self.COLV = ar.alloc([3, 4])
        self.blk1 = ar.alloc([128])
        self.memset("dve", self.blk1, 0.0, ["blk1"])
        self.memset("dve", self.blk1[0:64, 0:64], 1.0 / 64, ["blk1"])
        self.memset("dve", self.blk1[64:128, 64:128], 1.0 / 64, ["blk1"])
        self.sh64 = ar.alloc([2, 128], BF16)
        self.memset("dve", self.sh64, 0.0, ["sh64"])
        self.cp("dve", self.sh64[0:64, 0, 0:64], self.identF[0:64, 0:64], ["identF", "sh64"], ["sh64"])
        self.cp("dve", self.sh64[0:64, 1, 64:128], self.identF[0:64, 0:64], ["identF", "sh64"], ["sh64"])
        self.e65 = ar.alloc([64])
        self.memset("dve", self.e65, 0.0, ["e65"])
        self.memset("dve", self.e65[64:65, :], 1.0, ["e65"])
        self.ones64 = ar.alloc([64])
        self.memset("dve", self.ones64, 1.0 / 64, ["ones64"])
        self.mapm = ar.alloc([2])
        self.memset("dve", self.mapm, 0.0, ["mapm"])
        for hh in range(2):
            for mm_ in range(2):
                p0 = hh * 64 + mm_ * 32
                self.memset("dve", self.mapm[p0:p0 + 32, mm_:mm_ + 1], 1.0, ["mapm"])
        m = ar.mark()
        IDX = ar.alloc([256])
        IDS = ar.alloc([16])
        EQ = ar.alloc([272])
        MK = ar.alloc([128])
        self.ld(IDX, self.i_idx01[:, :], writes=["IDX"])
        self.ld(IDS, self.i_idxs[:, :], writes=["IDS"])
        self.memset("dve", self.T01, 0.0, ["T01"])
        self.memset("dve", self.BSh, 0.0, ["BSh"])
        for b in range(32):
            self.ts("dve", EQ[:, 0:256], IDX, float(b), ALU.is_equal, ["IDX"], ["EQ"])
            self.ts("dve", EQ[:, 256:272], IDS, float(b), ALU.is_equal, ["IDS"], ["EQ"])
            for h in range(4):
                col = RB[:, b * 4 + h:b * 4 + h + 1]
                self.stt(self.T01[:, h, :], EQ[:, 0:256], col, self.T01[:, h, :], ALU.mult, ALU.add,
                         ["EQ", "RB", "T01"], ["T01"])
                self.stt(self.BSh[:, h, :], EQ[:, 256:272], col, self.BSh[:, h, :], ALU.mult, ALU.add,
                         ["EQ", "RB", "BSh"], ["BSh"])
        self.ts("dve", MK, IDX[:, 0:128], 0.0, ALU.is_lt, ["IDX"], ["MK"], s2=NEG, op1=ALU.mult)
        for h in range(4):
            self.tt("dve", self.T01[:, h, 0:128], self.T01[:, h, 0:128], MK, ALU.add, ["T01", "MK"], ["T01"])
        TMPc = ar.alloc([8])
        for i in range(3):
            vis = CC[:, 8 + i:9 + i]
            near = CC[:, 4 + i:5 + i]
            far = CC[:, 12 + i:13 + i]
            self.ts("dve", TMPc[:, 0:4], self.CH, -NEG, ALU.add, ["RB", "CC"], ["TMPc"], s2=vis, op1=ALU.mult)
            self.ts("dve", self.BC[:, i, :], TMPc[:, 0:4], NEG, ALU.add, ["TMPc"], ["BC"])
            self.tt("dve", TMPc[:, 4:5], near, far, ALU.add, ["CC"], ["TMPc"])
            self.ts("dve", TMPc[:, 4:5], TMPc[:, 4:5], -NEG, ALU.mult, ["TMPc"], ["TMPc"], s2=NEG, op1=ALU.add)
            self.ts("dve", TMPc[:, 0:4], self.CH, far, ALU.mult, ["RB", "CC"], ["TMPc"], s2=TMPc[:, 4:5], op1=ALU.add)
            self.cp("dve", self.COLV[:, i, :], TMPc[:, 0:4], ["TMPc"], ["COLV"])
        self.B.barrier()
        self.B.barrier()
        ar.release(m)
        self.LBR = ar.alloc([L, 256])
        self.LBC = ar.alloc([L, 2])
        self.OMC = ar.alloc([L, 2])
        m = ar.mark()
        E = ar.alloc([L, 256])
        SUM = ar.alloc([256])
        self.ld(E.rearrange("p l c -> p (l c)"), self.i_lb.partition_broadcast(128), writes=["LBE"])
        self.act(E, E, AF.Exp, ["LBE"], ["LBE"])
        self.cp("dve", SUM, E[:, 0, :], ["LBE"], ["LBS"])
        for l in range(1, L):
            self.tt("dve", SUM, SUM, E[:, l, :], ALU.add, ["LBS", "LBE"], ["LBS"])
        nc = self.nc
        self.B.op("dve", lambda: nc.vector.reciprocal(out=SUM, in_=SUM), ["LBS"], ["LBS"])
        self.memset("dve", self.LBR[:, 0, :], 0.0, ["LBR"])
        for l in range(1, L):
            self.tt("dve", E[:, l, :], E[:, l, :], SUM, ALU.mult, ["LBE", "LBS"], ["LBE"])
            self.tt("dve", self.LBR[:, l, :], self.LBR[:, l - 1, :], E[:, l, :], ALU.add, ["LBR", "LBE"], ["LBR"])
        self.ts("dve", self.LBR, self.LBR, 0.0, ALU.max, ["LBR"], ["LBR"])
        scr = self.nc.dram_tensor("lbscr", [L, 256], F32)
        self.st(scr[:, :], self.LBR[0:1, :, :], reads=["LBR"], writes=["lbscr"])
        self.ld(self.LBC, scr.ap().rearrange("l (c p) -> p l c", p=128), writes=["LBC"], reads=["lbscr"],
                allow_slow_non_contiguous=True)
        self.ts("dve", self.OMC, self.LBC, -1.0, ALU.mult, ["LBC"], ["OMC"], s2=1.0, op1=ALU.add)
        self.B.barrier()
        self.B.barrier()
        ar.release(m)

    def layer_params(self, l):
        ar = self.ar
        P = {}
        P["psc"] = ar.alloc([2])
        self.ld(P["psc"], self.i_pscale[l].rearrange("(c p) -> p c", p=128), writes=["psc"], allow_slow_non_contiguous=True)
        P["hgg"] = ar.alloc([1])
        for hh in range(2):
            self.ld(P["hgg"][hh * 64:(hh + 1) * 64, :], self.i_hgg[l].rearrange("(p o) -> p o", o=1), writes=["hgg"],
                    allow_slow_non_contiguous=True)
        lam_init = 0.8 - 0.6 * math.exp(-0.3 * l)
        P["subg"] = ar.alloc([1])
        self.ld(P["subg"][0:64, :], self.i_subg[l].rearrange("(p o) -> p o", o=1), writes=["subg"], allow_slow_non_contiguous=True)
        self.ts("dve", P["subg"][0:64, :], P["subg"][0:64, :], 1.0 - lam_init, ALU.mult, ["subg"], ["subg"])
        LQ = ar.alloc([4, 32])
        for i in range(4):
            self.ld(LQ[:, i, :], self.i_lam[i][l:l + 1, :].partition_broadcast(128), writes=["LQ%d" % i])
        S2 = ar.alloc([4])
        nc = self.nc
        self.tt("dve", LQ[:, 0, :], LQ[:, 0, :], LQ[:, 1, :], ALU.mult, ["LQ0", "LQ1"], ["LQ0"])
        self.tt("dve", LQ[:, 2, :], LQ[:, 2, :], LQ[:, 3, :], ALU.mult, ["LQ2", "LQ3"], ["LQ2"])
        self.B.op("dve", lambda: nc.vector.reduce_sum(out=S2[:, 0:1], in_=LQ[:, 0, :], axis=AX.X), ["LQ0"], ["S2"])
        self.B.op("dve", lambda: nc.vector.reduce_sum(out=S2[:, 1:2], in_=LQ[:, 2, :], axis=AX.X), ["LQ2", "S2"], ["S2"])
        self.act(S2[:, 0:2], S2[:, 0:2], AF.Exp, ["S2"], ["S2"])
        self.tt("dve", S2[:, 2:3], S2[:, 0:1], S2[:, 1:2], ALU.subtract, ["S2"], ["S2"])
        P["nlam"] = ar.alloc([1])
        self.ts("dve", P["nlam"], S2[:, 2:3], lam_init, ALU.add, ["S2"], ["nlam"], s2=-1.0, op1=ALU.mult)
        P["pw"] = ar.alloc([2, 128], BF16)
        self.memset("pool", P["pw"], 0.0, ["pw"])
        for g in range(4):
            c, hh = g // 2, g % 2
            self.ld(P["pw"][hh * 64:(hh + 1) * 64, c, hh * 64:(hh + 1) * 64], self.i_poolw[l, g], writes=["pw"], eng="pool")
        P["sglg"] = ar.alloc([256])
        P["sglb"] = ar.alloc([256])
        self.ld(P["sglg"], self.i_sglg[l:l + 1, :].partition_broadcast(128), writes=["sglg"])
        self.ld(P["sglb"], self.i_sglb[l:l + 1, :].partition_broadcast(128), writes=["sglb"])
        P["sgb"] = ar.alloc([2, 128])
        for g in range(4):
            c, hh = g // 2, g % 2
            self.ld(P["sgb"][hh * 64:(hh + 1) * 64, c, :], self.i_sgb[l, g:g + 1, :].partition_broadcast(64), writes=["sgb"])
        P["wt"] = ar.alloc([4, 128], BF16)
        P["w00"] = ar.alloc([256])
        P["b00"] = ar.alloc([256])
        WC = ar.alloc([8])
        for g in range(4):
            self.ld(WC[:, g:g + 1], self.i_sgw[l, g, 0, 0:1].partition_broadcast(128), writes=["WC"])
            self.ld(WC[:, 4 + g:5 + g], self.i_sgb[l, g, 0:1].partition_broadcast(128), writes=["WC"])
        for g in range(4):
            self.cp("dve", P["w00"][:, g * 64:(g + 1) * 64], WC[:, g:g + 1].to_broadcast([128, 64]), ["WC"], ["w00"])
            self.cp("dve", P["b00"][:, g * 64:(g + 1) * 64], WC[:, 4 + g:5 + g].to_broadcast([128, 64]), ["WC"], ["b00"])
        P["hggrow"] = ar.alloc([64])
        self.ld(P["hggrow"], self.i_hgg[l:l + 1, :].partition_broadcast(128), writes=["hggrow"])
        P["subgrow"] = ar.alloc([64])
        self.ld(P["subgrow"], self.i_subg[l:l + 1, :].partition_broadcast(128), writes=["subgrow"])
        self.ts("dve", P["subgrow"], P["subgrow"], 1.0 - lam_init, ALU.mult, ["subgrow"], ["subgrow"])
        m = ar.mark()
        WL = ar.alloc([128])
        for g in range(4):
            self.ld(WL, self.i_sgw[l, g], writes=["WL"])
            self.tr(self.PS[0][:, 0:128], WL, self.identF, ["WL", "identF"], ["ps0"])
            self.tt("dve", P["wt"][:, g, :], self.PS[0][:, 0:128], self.maskc, ALU.mult, ["ps0", "maskc"], ["wt"])
        self.B.barrier()
        self.B.barrier()
        ar.release(m)
        return P

    def tok2fm(self, src, skeys, dst, dkeys, post=None):
        self.mm(self.PS[6][:, 0:16], src, self.identF[0:16, 0:16], True, True, list(skeys) + ["identF"], ["ps6"])
        if post is None:
            self.cp("act", dst, self.PS[6][:, 0:16], ["ps6"], dkeys)
        else:
            post(self.PS[6][:, 0:16])

    def sample_mixer(self, l, P):
        ar, nc, TL = self.ar, self.nc, self.TL
        m0 = ar.mark()
        R = slice(0, NS)
        PSs = ar.alloc([D_IN])
        for e in range(20):
            buf, wk = self.ws_next(("win", l, e))
            slab = self.slabv(buf)
            ps = self.PS[e % 2]
            pk = "ps%d" % (e % 2)
            for k in range(KC):
                self.mm(ps[R, 0:128], self.xbf[:, k, TL:TL + NS], slab[:, k, :], k == 0, k == KC - 1,
                        [wk, "bx.%d.%d" % (k, self.NT)], [pk])
            self.cp("act" if e % 2 else "dve", PSs[R, e * 128:(e + 1) * 128], ps[R, 0:128], [pk], ["PSs"])
        col = lambda i: PSs[R, i * 256:(i + 1) * 256]
        a_s, hq, hf, hi, hg, dq, dk, dv, su, sv = [col(i) for i in range(10)]
        self.st(self.o_ks[l], dk, reads=["PSs"])
        self.st(self.o_vs[l], dv, reads=["PSs"])
        Y = ar.alloc([256])
        m = ar.mark()
        AE = ar.alloc([16, 256])
        self.ld(AE[R, 0:15, :].rearrange("p r c -> p (r c)"), self.i_spool[l], writes=["AE"])
        self.cp("dve", AE[R, 15, :], a_s, ["PSs", "AE"], ["AE"])
        self.st(self.o_pools[l], AE[R, 1:16, :].rearrange("p r c -> p (r c)"), reads=["AE"])
        for g in range(4):
            win = 2 ** (g + 1)
            gs = slice(g * 64, (g + 1) * 64)
            self.B.op("dve", lambda: nc.vector.tensor_reduce(out=Y[R, gs], in_=AE[R, 16 - win:16, gs].rearrange("p r c -> p c r"),
                                                             axis=AX.X, op=ALU.add), ["AE", "sY"], ["sY"])
            self.stt(Y[R, gs], Y[R, gs], 1.0 / win, a_s[:, gs], ALU.mult, ALU.subtract, ["sY", "PSs"], ["sY"])
        Dbf = ar.alloc([2, NS], BF16)
        for c in range(2):
            self.tok2fm(Y[R, c * 128:(c + 1) * 128], ["sY"], Dbf[:, c, :], ["sDbf"])
            self.mm(self.PS[7][:, 0:NS], P["pw"][:, c, :], Dbf[:, c, :], True, True, ["pw", "sDbf"], ["ps7"])
            self.ts("dve", self.cats[:, c, :], self.PS[7][:, 0:NS], P["psc"][:, c:c + 1], ALU.mult, ["ps7", "psc", "cats"], ["cats"])
        self.B.barrier()
        ar.release(m)
        m = ar.mark()
        ST = ar.alloc([8])
        VN = ar.alloc([256])
        self.B.op("dve", lambda: nc.vector.bn_stats(out=ST[R, 0:6], in_=sv), ["PSs"], ["sST"])
        self.B.op("dve", lambda: nc.vector.bn_aggr(out=ST[R, 6:8], in_=ST[R, 0:6]), ["sST"], ["sST"])
        self.act(ST[R, 7:8], ST[R, 7:8], AF.Sqrt, ["sST"], ["sST"], bias=EPS)
        self.B.op("dve", lambda: nc.vector.reciprocal(out=ST[R, 7:8], in_=ST[R, 7:8]), ["sST"], ["sST"])
        self.ts("dve", VN[R, :], sv, ST[R, 6:7], ALU.subtract, ["PSs", "sST"], ["sVN"], s2=ST[R, 7:8], op1=ALU.mult)
        self.tt("dve", VN[R, :], VN[R, :], P["sglg"][R, :], ALU.mult, ["sVN", "sglg"], ["sVN"])
        self.tt("dve", VN[R, :], VN[R, :], P["sglb"][R, :], ALU.add, ["sVN", "sglb"], ["sVN"])
        self.st(self.o_sgs[l], VN[R, :], reads=["sVN"])
        self.tt("dve", Y[R, :], VN[R, :], P["w00"][R, :], ALU.mult, ["sVN", "w00", "sY"], ["sY"])
        self.tt("dve", Y[R, :], Y[R, :], P["b00"][R, :], ALU.add, ["sY", "b00"], ["sY"])
        self.tt("dve", Y[R, :], Y[R, :], su, ALU.mult, ["sY", "PSs"], ["sY"])
        for c in range(2):
            self.tok2fm(Y[R, c * 128:(c + 1) * 128], ["sY"], self.cats[:, 6 + c, :], ["cats"])
        self.B.barrier()
        ar.release(m)
        m = ar.mark()
        Fs = ar.alloc([256])
        Ks = ar.alloc([256])
        O = ar.alloc([256])
        SS = ar.alloc([64, 64])
        TM = ar.alloc([32, 64])
        self.act(Fs[R, :], hf, AF.Sigmoid, ["PSs"], ["sF"])
        OMR = Ks
        self.ts("dve", OMR[R, :], self.LBR[R, l, :], -1.0, ALU.mult, ["LBR"], ["sK"], s2=1.0, op1=ALU.add)
        self.tt("dve", Fs[R, :], Fs[R, :], OMR[R, :], ALU.mult, ["sF", "sK"], ["sF"])
        self.tt("dve", Fs[R, :], Fs[R, :], self.LBR[R, l, :], ALU.add, ["sF", "LBR"], ["sF"])
        self.ts("dve", Ks[R, :], Fs[R, :], -1.0, ALU.mult, ["sF", "sK"], ["sK"], s2=1.0, op1=ALU.add)
        O2 = ar.alloc([64])
        for h in range(4):
            hs = slice(h * 64, (h + 1) * 64)
            self.ld(SS[R, :, :].rearrange("p k v -> p (k v)"), self.i_shgrn[l, :, h, :], writes=["sSS"])
            for kh in range(2):
                ks_ = slice(h * 64 + kh * 32, h * 64 + kh * 32 + 32)
                kr = slice(kh * 32, kh * 32 + 32)
                kb = Ks[R, ks_].unsqueeze(2).to_broadcast([NS, 32, 64])
                fb = Fs[R, ks_].unsqueeze(2).to_broadcast([NS, 32, 64])
                qb = hq[:, ks_].unsqueeze(2).to_broadcast([NS, 32, 64])
                vb = hi[:, hs].unsqueeze(1).to_broadcast([NS, 32, 64])
                self.tt("dve", TM[R, :, :], kb, vb, ALU.mult, ["sK", "PSs", "sTM"], ["sTM"])
                self.tt("dve", SS[R, kr, :], SS[R, kr, :], fb, ALU.mult, ["sSS", "sF"], ["sSS"])
                self.tt("dve", SS[R, kr, :], SS[R, kr, :], TM[R, :, :], ALU.add, ["sSS", "sTM"], ["sSS"])
                self.tt("dve", TM[R, :, :], SS[R, kr, :], qb, ALU.mult, ["sSS", "PSs", "sTM"], ["sTM"])
                dst = O[R, hs] if kh == 0 else O2[R, :]
                self.B.op("dve", lambda: nc.vector.tensor_reduce(out=dst, in_=TM[R, :, :].rearrange("p k v -> p v k"),
                                                                 axis=AX.X, op=ALU.add), ["sTM", "sO"], ["sO"])
            self.tt("dve", O[R, hs], O[R, hs], O2[R, :], ALU.add, ["sO"], ["sO"])
            self.st(self.o_hgs[l, :, h, :], SS[R, :, :].rearrange("p k v -> p (k v)"), reads=["sSS"])
        SQ = Fs
        self.tt("dve", SQ[R, :], O[R, :], O[R, :], ALU.mult, ["sO", "sF"], ["sF"])
        self.B.op("dve", lambda: nc.vector.tensor_reduce(out=ST[R, 0:4], in_=SQ[R, :].rearrange("p (h d) -> p h d", h=4),
                                                         axis=AX.X, op=ALU.add), ["sF", "sST"], ["sST"])
        self.act(ST[R, 0:4], ST[R, 0:4], AF.Sqrt, ["sST"], ["sST"], scale=1.0 / 64, bias=EPS)
        self.B.op("dve", lambda: nc.vector.reciprocal(out=ST[R, 0:4], in_=ST[R, 0:4]), ["sST"], ["sST"])
        self.tt("dve", O[R, :].rearrange("p (h d) -> p h d", h=4), O[R, :].rearrange("p (h d) -> p h d", h=4),
                ST[R, 0:4].unsqueeze(2).to_broadcast([NS, 4, 64]), ALU.mult, ["sO", "sST"], ["sO"])
        self.tt("dve", O[R, :].rearrange("p (h d) -> p h d", h=4), O[R, :].rearrange("p (h d) -> p h d", h=4),
                P["hggrow"][R, :].unsqueeze(1).to_broadcast([NS, 4, 64]), ALU.mult, ["sO", "hggrow"], ["sO"])
        self.act(Y[R, :], hg, AF.Silu, ["PSs", "sY"], ["sY"])
        self.tt("dve", Y[R, :], Y[R, :], O[R, :], ALU.mult, ["sY", "sO"], ["sY"])
        for c in range(2):
            self.tok2fm(Y[R, c * 128:(c + 1) * 128], ["sY"], self.cats[:, 2 + c, :], ["cats"])
        self.B.barrier()
        ar.release(m)
        m = ar.mark()
        KG = ar.alloc([16, 256])
        VG = ar.alloc([16, 256])
        S8 = ar.alloc([16, 8])
        Pm = ar.alloc([16, 8])
        PRd = ar.alloc([8])
        A8 = ar.alloc([256])
        CL = ar.alloc([4])
        QS = ar.alloc([256])
        PSF = ar.alloc([16, 8])
        IDL = ar.alloc([NS], I32)
        IDF = ar.alloc([NS])
        self.cp("dve", IDL, self.IDXF, ["IDXF"], ["sIDL"])
        W8 = CL[0:8, 0:1]
        self.stt(W8, self.c8[0:8, 1:2], P["nlam"][0:8, 0:1], self.c8[0:8, 0:1], ALU.mult, ALU.add, ["c8", "nlam"], ["sCL"])
        self.tt("dve", QS[R, :], dq, dk, ALU.mult, ["PSs", "sQS"], ["sQS"])
        self.B.op("dve", lambda: nc.vector.tensor_reduce(out=S8[R, 0, :], in_=QS[R, :].rearrange("p (j d) -> p j d", d=32),
                                                         axis=AX.X, op=ALU.add), ["sQS", "sS8"], ["sS8"])
        self.stt(S8[R, 0, :], S8[R, 0, :], QSCALE, self.B0[R, :], ALU.mult, ALU.add, ["sS8", "B0"], ["sS8"])
        self.act(S8[R, 0, :], S8[R, 0, :], AF.Exp, ["sS8"], ["sS8"])
        self.tt("dve", PSF[R, :, :], S8[R, 0, :].unsqueeze(1).to_broadcast([NS, NS, 8]),
                self.identF[R, 0:NS].unsqueeze(2).to_broadcast([NS, NS, 8]), ALU.mult, ["sS8", "identF"], ["sPSF"])
        flat_k = self.i_ck[l]
        flat_v = self.i_cv[l]
        for n in range(NS):
            self.B.dma("pool", lambda: nc.gpsimd.indirect_dma_start(
                out=KG.rearrange("p i c -> p (i c)"), out_offset=None, in_=flat_k,
                in_offset=bass.IndirectOffsetOnAxis(ap=IDL[:, n:n + 1], axis=0)), reads=["sIDL"], writes=["sKG"])
            self.B.dma("pool", lambda: nc.gpsimd.indirect_dma_start(
                out=VG.rearrange("p i c -> p (i c)"), out_offset=None, in_=flat_v,
                in_offset=bass.IndirectOffsetOnAxis(ap=IDL[:, n:n + 1], axis=0)), reads=["sIDL"], writes=["sVG"])
            self.ts("dve", QS[R, :], dq, self.identF[R, n:n + 1], ALU.mult, ["PSs", "identF", "sQS"], ["sQS"])
            self.mm(self.PS[0][:, 0:256], self.ones1[R, :], QS[R, :], True, True, ["ones1", "sQS"], ["ps0"])
            self.tt("dve", KG, KG, self.PS[0][:, 0:256].unsqueeze(1).to_broadcast([128, 16, 256]), ALU.mult,
                    ["sKG", "ps0"], ["sKG"])
            self.B.op("dve", lambda: nc.vector.tensor_reduce(out=S8.rearrange("p i j -> p (i j)"),
                                                             in_=KG.rearrange("p i (j d) -> p (i j) d", d=32),
                                                             axis=AX.X, op=ALU.add), ["sKG", "sS8"], ["sS8"])
            self.stt(Pm, S8, QSCALE, self.BS8, ALU.mult, ALU.add, ["sS8", "BS8"], ["sPm"])
            self.act(Pm, Pm, AF.Exp, ["sPm"], ["sPm"])
            self.B.op("dve", lambda: nc.vector.tensor_reduce(out=PRd, in_=Pm.rearrange("p i j -> p j i"),
                                                             axis=AX.X, op=ALU.add), ["sPm"], ["sPRd"])
            acc = self.PS[1]
            for i in range(16):
                self.mm(acc[0:8, 0:256], Pm[:, i, :], VG[:, i, :], i == 0, False, ["sPm", "sVG"], ["ps1"])
            self.mm(acc[0:8, 0:256], PSF[R, n, :], dv, False, True, ["sPSF", "PSs"], ["ps1"])
            self.mm(self.PS[2][0:8, 0:1], PRd, self.onesF[:, 0:1], True, False, ["sPRd", "onesF"], ["ps2"])
            self.mm(self.PS[2][0:8, 0:1], PSF[R, n, :], self.onesF[R, 0:1], False, True, ["sPSF", "onesF"], ["ps2"])
            self.B.op("dve", lambda: nc.vector.reciprocal(out=CL[0:8, 1:2], in_=self.PS[2][0:8, 0:1]), ["ps2", "sCL"], ["sCL"])
            self.ts("dve", CL[0:8, 2:3], CL[0:8, 1:2], W8, ALU.mult, ["sCL"], ["sCL"], s2=1.0 / D_MODEL, op1=ALU.mult)
            self.stt(A8[0:8, :], acc[0:8, 0:256], CL[0:8, 2:3], self.bm8[0:8, :], ALU.mult, ALU.mult, ["ps1", "sCL", "bm8"], ["sA8"])
            self.mm(self.PS[3][R, 0:256], self.oh8[0:8, n, :], A8[0:8, :], n == 0, n == NS - 1, ["oh8", "sA8"], ["ps3"])
        OC = A8
        self.cp("act", OC[R, :], self.PS[3][R, 0:256], ["ps3", "sA8"], ["sOC"])
        self.tt("dve", QS[R, :], OC[R, :], OC[R, :], ALU.mult, ["sOC", "sQS"], ["sQS"])
        ST2 = CL
        self.B.op("dve", lambda: nc.vector.tensor_reduce(out=ST2[R, 0:4], in_=QS[R, :].rearrange("p (h d) -> p h d", h=4),
                                                         axis=AX.X, op=ALU.add), ["sQS", "sCL"], ["sCL"])
        self.act(ST2[R, 0:4], ST2[R, 0:4], AF.Sqrt, ["sCL"], ["sCL"], scale=1.0 / 64, bias=EPS)
        self.B.op("dve", lambda: nc.vector.reciprocal(out=ST2[R, 0:4], in_=ST2[R, 0:4]), ["sCL"], ["sCL"])
        self.tt("dve", OC[R, :].rearrange("p (h d) -> p h d", h=4), OC[R, :].rearrange("p (h d) -> p h d", h=4),
                ST2[R, 0:4].unsqueeze(2).to_broadcast([NS, 4, 64]), ALU.mult, ["sOC", "sCL"], ["sOC"])
        self.tt("dve", Y[R, :].rearrange("p (h d) -> p h d", h=4), OC[R, :].rearrange("p (h d) -> p h d", h=4),
                P["subgrow"][R, :].unsqueeze(1).to_broadcast([NS, 4, 64]), ALU.mult, ["sOC", "subgrow", "sY"], ["sY"])
        for c in range(2):
            self.tok2fm(Y[R, c * 128:(c + 1) * 128], ["sY"], self.cats[:, 4 + c, :], ["cats"])
        self.B.barrier()
        ar.release(m)
        self.B.barrier()
        ar.release(m0)

    def sample_consts(self):
        ar = self.ar
        self.ones1 = ar.alloc([128])
        self.memset("dve", self.ones1, 1.0, ["ones1"])
        self.c8 = ar.alloc([2])
        self.bm8 = ar.alloc([256])
        self.oh8 = ar.alloc([16, 16])
        self.ld(self.c8[0:8, :], self.i_c8[:, 0:2], writes=["c8"])
        self.ld(self.bm8[0:8, :], self.i_c8[:, 2:258], writes=["bm8"])
        self.ld(self.oh8[0:8, :, :].rearrange("p n c -> p (n c)"), self.i_c8[:, 258:514], writes=["oh8"])
        self.BS8 = ar.alloc([16, 8])
        self.B0 = ar.alloc([8])
        for h in range(4):
            for mm_ in range(2):
                j = h * 2 + mm_
                self.cp("dve", self.BS8[:, :, j], self.BSh[:, h, :], ["BSh", "BS8"], ["BS8"])
                self.cp("dve", self.B0[:, j:j + 1], self.RB[:, h:h + 1], ["RB", "B0"], ["B0"])
        self.IDXF = ar.alloc([NS])
        m = ar.mark()
        PTI = ar.alloc([NS], I32)
        SUB = ar.alloc([1])
        for p in range(NPG):
            self.ld(PTI[p * 8:(p + 1) * 8, :], self.i_pt[:, p:p + 1].rearrange("n o -> o n").partition_broadcast(8),
                    writes=["PTI"], allow_slow_non_contiguous=True)
        self.ld(SUB, self.i_misc[:, 0:1], writes=["SUB"], allow_slow_non_contiguous=True)
        self.cp("dve", self.IDXF, PTI, ["PTI"], ["IDXF"])
        self.ts("dve", self.IDXF, self.IDXF, 8.0, ALU.mult, ["IDXF", "SUB"], ["IDXF"], s2=SUB[:, 0:1], op1=ALU.add)
        self.B.barrier()
        ar.release(m)

    def proj_fm(self, slab, wk, ti, ps, pk):
        t0, tn = self.ptiles[ti]
        for k in range(KC):
            self.mm(ps[:, :tn], slab[:, k, :], self.xbf[:, k, t0:t0 + tn], k == 0, k == KC - 1,
                    [wk, "bx.%d.%d" % (k, ti)], [pk])

    def proj_tm(self, slab, wk, t0, M, ps_ap, pk):
        ti = min(t0 // 512, self.NT)
        for k in range(KC):
            self.mm(ps_ap, self.xbf[:, k, t0:t0 + M], slab[:, k, :], k == 0, k == KC - 1,
                    [wk, "bx.%d.%d" % (k, ti)], [pk])

    def slabv(self, buf):
        return buf[:, :1024].rearrange("p (k c) -> p k c", k=KC)

    def kstop(self, n):
        import os
        if int(os.environ.get('KSTOP', 99)) == n:
            raise StopBuild()

    def mixer(self, l):
        ar = self.ar
        m0 = ar.mark()
        P = self.layer_params(l)
        self.cats = ar.alloc([KC, NS], BF16)
        self.memset("pool", self.cats, 0.0, ["cats"])
        if self.stages >= 4:
            self.sample_mixer(l, P)
        self.cat = {n: ar.alloc([2, self.TT], BF16) for n in ("a", "b", "d")}
        self.prompt_proj(l, P)
        self.kstop(9)
        self.B.barrier()
        self.prompt_attn(l, P)
        self.kstop(10)
        self.wout(l, P)
        self.B.barrier()
        ar.release(m0)
        self.layernorm(l, 1)

    def sgu_prompt(self, l, P):
        ar = self.ar
        m = ar.mark()
        b18, k18 = self.ws_next(("win", l, 18))
        b19, k19 = self.ws_next(("win", l, 19), hold=True)
        slabs = [(self.slabv(b18), k18), (self.slabv(b19), k19)]
        VN = [ar.alloc([256]) for _ in range(2)]
        VNB = [ar.alloc([256], BF16) for _ in range(2)]
        ST = ar.alloc([8])
        T1 = ar.alloc([128])
        nc = self.nc
        catd = self.cat["d"]
        for blk in range(self.NB):
            b = blk % 2
            t0 = blk * 128
            ps = self.PS[b]
            pk = "ps%d" % b
            for e in range(2):
                self.proj_tm(slabs[e][0], slabs[e][1], t0, 128, ps[:, e * 128:(e + 1) * 128], pk)
            self.B.op("dve", lambda: nc.vector.bn_stats(out=ST[:, 0:6], in_=ps[:, 0:256]), [pk], ["sgST"])
            self.B.op("dve", lambda: nc.vector.bn_aggr(out=ST[:, 6:8], in_=ST[:, 0:6]), ["sgST"], ["sgST"])
            self.act(ST[:, 7:8], ST[:, 7:8], AF.Sqrt, ["sgST"], ["sgST"], bias=EPS)
            self.B.op("dve", lambda: nc.vector.reciprocal(out=ST[:, 7:8], in_=ST[:, 7:8]), ["sgST"], ["sgST"])
            vn = VN[b]
            vk = "sgVN%d" % b
            self.ts("dve", vn, ps[:, 0:256], ST[:, 6:7], ALU.subtract, [pk, "sgST"], [vk], s2=ST[:, 7:8], op1=ALU.mult)
            self.tt("dve", vn, vn, P["sglg"], ALU.mult, [vk, "sglg"], [vk])
            self.tt("dve", vn, vn, P["sglb"], ALU.add, [vk, "sglb"], [vk])
            self.cp("act", VNB[b], vn, [vk], ["sgVB%d" % b])
            if blk == self.NB - 1:
                self.st(self.o_sgp[l], vn, reads=[vk])
            for c in range(2):
                for hh in range(2):
                    g = 2 * c + hh
                    pg = self.PS[2 + hh]
                    pgk = "ps%d" % (2 + hh)
                    self.mm(pg[:, 0:128], VNB[b][:, c * 128:(c + 1) * 128], P["wt"][:, g, :], True, True,
                            ["sgVB%d" % b, "wt"], [pgk])
                    hp = slice(hh * 64, (hh + 1) * 64)
                    self.tt("dve", catd[hp, c, t0:t0 + 128], pg[hp, 0:128], P["sgb"][hp, c, :], ALU.add,
                            [pgk, "sgb"], ["catd.%d" % c])
        for c in range(2):
            buf, wk = self.ws_next(("win", l, 16 + c))
            slab = self.slabv(buf)
            for ti, (t0, tn) in enumerate(self.ptiles):
                ps = self.PS[4 + ti % 2]
                pk = "ps%d" % (4 + ti % 2)
                self.proj_fm(slab, wk, ti, ps, pk)
                self.tt("dve", catd[:, c, t0:t0 + tn], catd[:, c, t0:t0 + tn], ps[:, :tn], ALU.mult,
                        ["catd.%d" % c, pk], ["catd.%d" % c])
        self.B.barrier()
        ar.release(m)

    def pool_prompt(self, l, P, GS):
        ar = self.ar
        m = ar.mark()
        TL = self.TL
        W = 16 + 512
        A = ar.alloc([W])
        PA = ar.alloc([W])
        PB = ar.alloc([W])
        Dbf = ar.alloc([512], BF16)
        PT = ar.alloc([128])
        cata = self.cat["a"]
        for c in range(2):
            buf, wk = self.ws_next(("win", l, c))
            slab = self.slabv(buf)
            self.memset("pool", A[:, 0:16], 0.0, ["plA"])
            for ti, (t0, tn) in enumerate(self.ptiles):
                ps = self.PS[ti % 2]
                pk = "ps%d" % (ti % 2)
                self.proj_fm(slab, wk, ti, ps, pk)
                self.act(A[:, 16:16 + tn], ps[:, :tn], AF.Copy, [pk], ["plA"])
                if ti == 0:
                    self.cp("pool", self.A16s[:, c, :], A[:, 16:32], ["plA"], ["A16s"])
                if ti == self.NT - 1:
                    self.cp("pool", GS[:, 129 + c * 16:129 + c * 16 + 16], A[:, tn:tn + 16], ["plA"], ["GS"])
                    self.cp("dve", PT, A[:, tn + 16 - 128:tn + 16], ["plA"], ["plPT"])
                    self.tr(self.PS[6][:, 0:128], PT, self.identF, ["plPT", "identF"], ["ps6"])
                    PTo = PA[:, 0:128]
                    self.cp("act", PTo, self.PS[6][:, 0:128], ["ps6"], ["plPA"])
                    self.st(self.o_poolp[l][:, c * 128:(c + 1) * 128], PTo[113:128, :], reads=["plPA"])
                self.tt("pool", PA[:, 2:W], A[:, 2:W], A[:, 1:W - 1], ALU.add, ["plA"], ["plPA"])
                self.tt("pool", PB[:, 4:W], PA[:, 4:W], PA[:, 2:W - 2], ALU.add, ["plPA"], ["plPB"])
                wins = {(0, 0): (PA, "plPA", 2), (0, 1): (PB, "plPB", 4)}
                if c == 1:
                    self.tt("pool", PA[:, 8:W], PB[:, 8:W], PB[:, 4:W - 4], ALU.add, ["plPB", "plPA"], ["plPA"])
                    self.tt("pool", PB[:, 16:W], PA[:, 16:W], PA[:, 8:W - 8], ALU.add, ["plPA", "plPB"], ["plPB"])
                    wins = {(1, 0): (PA, "plPA", 8), (1, 1): (PB, "plPB", 16)}
                for hh in range(2):
                    src, sk, win = wins[(c, hh)]
                    hp = slice(hh * 64, (hh + 1) * 64)
                    self.stt(Dbf[hp, :tn], src[hp, 16:16 + tn], 1.0 / win, A[hp, 16:16 + tn], ALU.mult, ALU.subtract,
                             [sk, "plA"], ["plD"])
                pl = self.PS[2 + ti % 2]
                plk = "ps%d" % (2 + ti % 2)
                self.mm(pl[:, :tn], P["pw"][:, c, :], Dbf[:, :tn], True, True, ["pw", "plD"], [plk])
                self.ts("dve", cata[:, c, t0:t0 + tn], pl[:, :tn], P["psc"][:, c:c + 1], ALU.mult, [plk, "psc"],
                        ["cata.%d" % c])
                self.cp("pool", A[:, 0:16], A[:, tn:tn + 16], ["plA"], ["plA"])
        self.B.barrier()
        ar.release(m)

    def pool_fix(self, l, P, GO):
        ar = self.ar
        m = ar.mark()
        A = ar.alloc([2, 32])
        PA = ar.alloc([2, 32])
        PB = ar.alloc([2, 32])
        Dbf = ar.alloc([2, 16], BF16)
        INV = ar.alloc([2, 16])
        self.ld(INV.rearrange("p c t -> p (c t)"), self.i_invc[:, :], writes=["pfINV"])
        self.memset("dve", A, 0.0, ["pfA"])
        for i in range(3):
            tail = GO[:, i, 129:161].rearrange("p (c t) -> p c t", c=2)
            self.stt(A[:, :, 0:16], tail, self.CC[:, 4 + i:5 + i], A[:, :, 0:16], ALU.mult, ALU.add,
                     ["GO", "CC", "pfA"], ["pfA"])
        self.cp("dve", A[:, :, 16:32], self.A16s, ["A16s", "pfA"], ["pfA"])
        self.tt("dve", PA[:, :, 2:32], A[:, :, 2:32], A[:, :, 1:31], ALU.add, ["pfA"], ["pfPA"])
        self.tt("dve", PB[:, :, 4:32], PA[:, :, 4:32], PA[:, :, 2:30], ALU.add, ["pfPA"], ["pfPB"])
        PC = ar.alloc([2, 32])
        PD = ar.alloc([2, 32])
        self.tt("dve", PC[:, :, 8:32], PB[:, :, 8:32], PB[:, :, 4:28], ALU.add, ["pfPB"], ["pfPC"])
        self.tt("dve", PD[:, :, 16:32], PC[:, :, 16:32], PC[:, :, 8:24], ALU.add, ["pfPC"], ["pfPD"])
        D32 = ar.alloc([2, 16])
        for c, hh, src, sk in ((0, 0, PA, "pfPA"), (0, 1, PB, "pfPB"), (1, 0, PC, "pfPC"), (1, 1, PD, "pfPD")):
            hp = slice(hh * 64, (hh + 1) * 64)
            self.tt("dve", D32[hp, c, :], src[hp, c, 16:32], INV[hp, c, :], ALU.mult, [sk, "pfINV", "pfD32"], ["pfD32"])
        self.tt("dve", D32, D32, A[:, :, 16:32], ALU.subtract, ["pfD32", "pfA"], ["pfD32"])
        self.cp("dve", Dbf, D32, ["pfD32"], ["pfD"])
        for c in range(2):
            self.mm(self.PS[6][:, 0:16], P["pw"][:, c, :], Dbf[:, c, :], True, True, ["pw", "pfD"], ["ps6"])
            self.ts("dve", self.cat["a"][:, c, 0:16], self.PS[6][:, 0:16], P["psc"][:, c:c + 1], ALU.mult,
                    ["ps6", "psc", "cata.%d" % c], ["cata.%d" % c])
        self.B.barrier()
        ar.release(m)

    def hgrn_prep(self, l, P, c, H):
        ar = self.ar
        m = ar.mark()
        TL = self.TL
        bf_, kf_ = self.ws_next(("win", l, 4 + c))
        bq_, kq_ = self.ws_next(("win", l, 2 + c), hold=True)
        bg_, kg_ = self.ws_next(("win", l, 8 + c), hold=True)
        bv_, kv_ = self.ws_next(("win", l, 6 + c), hold=True)
        sf, sq, sg_, sv = self.slabv(bf_), self.slabv(bq_), self.slabv(bg_), self.slabv(bv_)
        S = [ar.alloc([512]) for _ in range(4)]
        KHf = ar.alloc([512], BF16)
        lbc = self.LBC[:, l, c:c + 1]
        omc = self.OMC[:, l, c:c + 1]
        nc = self.nc
        catb = self.cat["b"]
        for ti, (t0, tn) in enumerate(self.ptiles):
            pf, pq, pg = self.PS[0], self.PS[1], self.PS[2]
            self.proj_fm(sf, kf_, ti, pf, "ps0")
            self.proj_fm(sq, kq_, ti, pq, "ps1")
            self.proj_fm(sg_, kg_, ti, pg, "ps2")
            self.act(catb[:, c, t0:t0 + tn], pg[:, :tn], AF.Silu, ["ps2"], ["catb.%d" % c])
            Fv, LG, BCv, E = S
            self.act(Fv, pf[:, :tn], AF.Sigmoid, ["ps0"], ["hF"])
            self.ts("dve", Fv, Fv, omc, ALU.mult, ["hF", "OMC"], ["hF"], s2=lbc, op1=ALU.add)
            self.act(LG, Fv, AF.Ln, ["hF"], ["hLG"])
            self.ts("dve", Fv, Fv, -1.0, ALU.mult, ["hF"], ["hF"], s2=1.0, op1=ALU.add)
            self.B.op("dve", lambda: nc.vector.tensor_tensor_scan(out=BCv, data0=self.segm, data1=LG, initial=0.0,
                                                                  op0=ALU.mult, op1=ALU.add),
                      ["segm", "hLG"], ["hBC"])
            bl = BCv.rearrange("p (n s) -> p n s", s=64)[:, :, 63]
            n8 = tn // 64
            self.cp("dve", H["BL"][:, ti * 8:ti * 8 + n8], bl, ["hBC"], ["hBL"])
            self.act(E, BCv, AF.Exp, ["hBC"], ["hE"])
            self.tt("dve", H["QT"][:, t0:t0 + tn], pq[:, :tn], E, ALU.mult, ["ps1", "hE"], ["hQT"])
            self.act(E, BCv, AF.Exp, ["hBC", "hE"], ["hE"], scale=-1.0)
            self.tt("dve", H["KT"][:, t0:t0 + tn], Fv, E, ALU.mult, ["hF", "hE"], ["hKT"])
            blb = H["BL"][:, ti * 8:ti * 8 + n8].unsqueeze(2).to_broadcast([128, n8, 64])
            self.tt("dve", LG.rearrange("p (n s) -> p n s", s=64), blb, BCv.rearrange("p (n s) -> p n s", s=64),
                    ALU.subtract, ["hBL", "hBC", "hLG"], ["hLG"])
            self.act(LG, LG, AF.Exp, ["hLG"], ["hLG"])
            self.tt("dve", KHf, Fv, LG, ALU.mult, ["hF", "hLG"], ["hKHf"])
            for bi in range(tn // 128):
                blk = ti * 4 + bi
                pt = self.PS[3 + bi % 2]
                ptk = "ps%d" % (3 + bi % 2)
                self.mm(pt[:, 0:128], KHf[:, bi * 128:(bi + 1) * 128], self.identB, True, True, ["hKHf", "identB"], [ptk])
                self.proj_tm(sv, kv_, t0 + bi * 128, 128, pt[:, 128:256], ptk)
                self.cp("act", H["KHt"][:, blk, :], pt[:, 0:128], [ptk], ["hKHt"])
                self.cp("dve", H["VHt"][:, blk, :], pt[:, 128:256], [ptk], ["hVHt"])
        self.act(H["EBL"], H["BL"], AF.Exp, ["hBL"], ["hEBL"])
        self.B.barrier()
        ar.release(m)

    def hgrn_state_step(self, H, ch, S32):
        psS = self.PS[5]
        rows = slice((ch % 2) * 64, (ch % 2) * 64 + 64)
        self.mm(psS[:, 0:128], H["KHt"][rows, ch // 2, :], H["VHt"][rows, ch // 2, :], True, True, ["hKHt", "hVHt"], ["ps5"])
        for hh in range(2):
            hp = slice(hh * 64, (hh + 1) * 64)
            self.stt(S32[hp, hp], S32[hp, hp], H["EBL"][hp, ch:ch + 1], psS[hp, hp], ALU.mult, ALU.add,
                     ["hS32", "hEBL", "ps5"], ["hS32"])

    def hgrn_pass1(self, l, c, H, GS, gi):
        nc = self.nc
        S32 = H["S32"]
        self.memset("dve", S32, 0.0, ["hS32"])
        for ch in range(self.NCH):
            self.hgrn_state_step(H, ch, S32)
        self.cp("dve", GS[:, 0:128], S32, ["hS32"], ["GS"])
        self.B.op("dve", lambda: nc.vector.reduce_sum(out=GS[:, 128:129], in_=H["BL"], axis=AX.X), ["hBL", "GS"], ["GS"])
        self.act(GS[:, 128:129], GS[:, 128:129], AF.Exp, ["GS"], ["GS"])
        gin, gout = self.g_sm_in[l][gi], self.g_sm_out[l][gi]
        self.st(gin[:, :], GS, reads=["GS"], writes=["gsmi%d" % gi])
        self.B.cc(lambda: nc.gpsimd.collective_compute("AllGather", ALU.bypass, replica_groups=[[0, 1, 2, 3], [4, 5, 6, 7]],
                                                       ins=[gin[:, :]], outs=[gout[:, :]]),
                  reads=["gsmi%d" % gi], writes=["gsmo%d" % gi])

    def hgrn_pass2(self, l, P, c, H, GO, gi):
        ar = self.ar
        m = ar.mark()
        nc = self.nc
        gout = self.g_sm_out[l][gi]
        self.ld(GO, gout.ap().rearrange("(i p) c -> p i c", p=128), writes=["GO"], reads=["gsmo%d" % gi])
        S32, Sbf = H["S32"], H["Sbf"]
        Pst = ar.alloc([128])
        self.memset("dve", Pst, 0.0, ["hP"])
        self.memset("dve", S32, 0.0, ["hS32"])
        for i in range(4):
            self.stt(S32, Pst, self.CC[:, i:i + 1], S32, ALU.mult, ALU.add, ["hP", "CC", "hS32"], ["hS32"])
            self.stt(Pst, Pst, GO[:, i, 128:129], GO[:, i, 0:128], ALU.mult, ALU.add, ["hP", "GO"], ["hP"])
        for hh in range(2):
            hp = slice(hh * 64, (hh + 1) * 64)
            self.st(self.o_hgp[l, 2 * c + hh], Pst[hp, hp], reads=["hP"])
        self.cp("act", Sbf, S32, ["hS32"], ["hSbf"])
        import os
        P2 = int(os.environ.get("P2STOP", 99))
        if P2 == 1:
            self.B.barrier()
            ar.release(m)
            return
        AT = [ar.alloc([256], BF16) for _ in range(2)]
        OB = ar.alloc([512])
        SQ = ar.alloc([512])
        catb = self.cat["b"]
        for ch in range(self.NCH):
            c0 = ch * 64
            blk, half = ch // 2, ch % 2
            rows = slice(half * 64, half * 64 + 64)
            at = AT[blk % 2]
            atk = "hAT%d" % (blk % 2)
            if half == 0:
                b0 = blk * 128
                banks = (0, 1) if blk % 2 == 0 else (6, 7)
                for hh in range(2):
                    hp = slice(hh * 64, (hh + 1) * 64)
                    psA = self.PS[banks[hh]]
                    pak = "ps%d" % banks[hh]
                    self.mm(psA[:, 0:128], H["KT"][hp, b0:b0 + 128], H["QT"][hp, b0:b0 + 128], True, True,
                            ["hKT", "hQT"], [pak])
                    self.tt("dve", at[:, hh * 128:(hh + 1) * 128], psA[:, 0:128], self.mask2, ALU.mult,
                            [pak, "mask2"], [atk])
            if P2 == 2:
                continue
            psO = self.PS[2 + ch % 2]
            pok = "ps%d" % (2 + ch % 2)
            rhs = at.rearrange("p (h t) -> p h t", h=2)[rows, :, half * 64:(half + 1) * 64]
            self.mm(psO[:, 0:128].rearrange("p (h t) -> p h t", h=2), H["VHt"][rows, blk, :], rhs, True, False, ["hVHt", atk], [pok])
            if P2 == 3:
                continue
            for hh in range(2):
                self.mm(psO[:, hh * 64:(hh + 1) * 64], Sbf, H["QT"][:, c0:c0 + 64], False, hh == 1, ["hSbf", "hQT"], [pok])
            if P2 == 4:
                continue
            o0 = (ch % 8) * 64
            for hh in range(2):
                hp = slice(hh * 64, (hh + 1) * 64)
                self.cp("act", OB[hp, o0:o0 + 64], psO[hp, hh * 64:(hh + 1) * 64], [pok], ["hOB"])
            self.hgrn_state_step(H, ch, S32)
            self.cp("act", Sbf, S32, ["hS32"], ["hSbf"])
            if ch % 8 == 7 or ch == self.NCH - 1:
                t0 = (ch // 8) * 512
                tn = o0 + 64
                self.act(SQ[:, :tn], OB[:, :tn], AF.Square, ["hOB"], ["hSQ"])
                self.mm(self.PS[4][:, :tn], self.blk1, SQ[:, :tn], True, True, ["blk1", "hSQ"], ["ps4"])
                self.act(SQ[:, :tn], self.PS[4][:, :tn], AF.Sqrt, ["ps4", "hSQ"], ["hSQ"], bias=EPS)
                self.B.op("dve", lambda: nc.vector.reciprocal(out=SQ[:, :tn], in_=SQ[:, :tn]), ["hSQ"], ["hSQ"])
                self.stt(OB[:, :tn], OB[:, :tn], P["hgg"][:, 0:1], SQ[:, :tn], ALU.mult, ALU.mult, ["hOB", "hgg", "hSQ"], ["hOB"])
                self.tt("dve", catb[:, c, t0:t0 + tn], catb[:, c, t0:t0 + tn], OB[:, :tn], ALU.mult,
                        ["catb.%d" % c, "hOB"], ["catb.%d" % c])
        self.B.barrier()
        ar.release(m)

    def qkv_prompt(self, l, P):
        ar = self.ar
        m = ar.mark()
        TL = self.TL
        hb = self.NB // 2
        gKp = [self.g_kv_in[l][c].ap().rearrange("r c -> (r c)").rearrange("(p t) -> p t", p=128) for c in range(2)]
        gVp = [self.g_kv_in[l][2 + q].ap().rearrange("r c -> (r c)").rearrange("(b p c) -> p b c", p=128, c=256) for q in range(2)]
        sl = []
        for e in (12, 13, 14, 15):
            b_, k_ = self.ws_next(("win", l, e), hold=(e != 12))
            sl.append((self.slabv(b_), k_))
        KB = [ar.alloc([512], BF16) for _ in range(2)]
        for c in range(2):
            for ti, (t0, tn) in enumerate(self.ptiles):
                ps = self.PS[ti % 2]
                pk = "ps%d" % (ti % 2)
                self.proj_fm(sl[c][0], sl[c][1], ti, ps, pk)
                kb = KB[ti % 2]
                self.cp("act", kb[:, :tn], ps[:, :tn], [pk], ["qkKB%d" % (ti % 2)])
                self.st(gKp[c][:, t0:t0 + tn], kb[:, :tn], reads=["qkKB%d" % (ti % 2)], writes=["gkvi%d" % c])
        KV32 = [ar.alloc([512]) for _ in range(2)]
        VB = [ar.alloc([256], BF16) for _ in range(2)]
        for blk in range(self.NB):
            b = blk % 2
            t0 = blk * 128
            ps = self.PS[2 + b]
            pk = "ps%d" % (2 + b)
            for e in range(4):
                self.proj_tm(sl[e][0], sl[e][1], t0, 128, ps[:, e * 128:(e + 1) * 128], pk)
            self.cp("act", KV32[b], ps[:, :], [pk], ["qkKV%d" % b])
            self.cp("dve", VB[b], ps[:, 256:512], [pk], ["qkVB%d" % b])
            self.st(self.o_kp[l, t0:t0 + 128, :], KV32[b][:, 0:256], reads=["qkKV%d" % b])
            self.st(self.o_vp[l, t0:t0 + 128, :], KV32[b][:, 256:512], reads=["qkKV%d" % b])
            self.st(gVp[blk // hb][:, blk % hb, :], VB[b], reads=["qkVB%d" % b], writes=["gkvi%d" % (2 + blk // hb)])
        self.B.barrier()
        ar.release(m)

    def q_prompt(self, l, P):
        for c in range(2):
            buf, wk = self.ws_next(("win", l, 10 + c))
            slab = self.slabv(buf)
            for ti, (t0, tn) in enumerate(self.ptiles):
                ps = self.PS[ti % 2]
                pk = "ps%d" % (ti % 2)
                self.proj_fm(slab, wk, ti, ps, pk)
                for mm_ in range(2):
                    eng = "dve" if mm_ == 0 else "dve"
                    self.ts(eng, self.QZ[:, c, mm_, t0:t0 + tn], ps[:, :tn], self.mapm[:, mm_:mm_ + 1], ALU.mult,
                            [pk, "mapm"], ["QZ.%d" % c])

    def prompt_proj(self, l, P):
        ar = self.ar
        nc = self.nc
        TL = self.TL
        self.A16s = ar.alloc([2, 16])
        GS = ar.alloc([161])
        GO = ar.alloc([4, 161])
        self.memset("dve", GS, 0.0, ["GS"])

        def new_H():
            H = {}
            H["QT"] = ar.alloc([TL], BF16)
            H["KT"] = ar.alloc([TL], BF16)
            H["KHt"] = ar.alloc([self.NB, 128], BF16)
            H["VHt"] = ar.alloc([self.NB, 128], BF16)
            H["BL"] = ar.alloc([self.NCH])
            H["EBL"] = ar.alloc([self.NCH])
            H["S32"] = ar.alloc([128])
            H["Sbf"] = ar.alloc([128], BF16)
            return H

        m = ar.mark()
        H = new_H()
        self.kstop(1)
        self.hgrn_prep(l, P, 0, H)
        self.kstop(2)
        self.hgrn_pass1(l, 0, H, GS, 0)
        self.kstop(3)
        self.sgu_prompt(l, P)
        self.kstop(4)
        self.hgrn_pass2(l, P, 0, H, GO, 0)
        self.kstop(5)
        self.B.barrier()
        self.B.barrier()
        ar.release(m)
        self.pool_prompt(l, P, GS)
        self.kstop(6)
        m = ar.mark()
        H = new_H()
        self.hgrn_prep(l, P, 1, H)
        self.hgrn_pass1(l, 1, H, GS, 1)
        self.qkv_prompt(l, P)
        self.kstop(7)
        self.hgrn_pass2(l, P, 1, H, GO, 1)
        self.pool_fix(l, P, GO)
        self.kstop(8)
        self.B.barrier()
        self.B.barrier()
        ar.release(m)
        self.QZ = ar.alloc([2, 2, TL], BF16)
        self.q_prompt(l, P)
        for q in range(4):
            gin, gout = self.g_kv_in[l][q], self.g_kv_out[l][q]
            self.B.cc(lambda: nc.gpsimd.collective_compute("AllGather", ALU.bypass, replica_groups=[[0, 1, 2, 3], [4, 5, 6, 7]],
                                                           ins=[gin[:, :]], outs=[gout[:, :]]),
                      reads=["gkvi%d" % q], writes=["gkvo%d" % q])

    def prompt_attn(self, l, P):
        ar, nc, TL, NB = self.ar, self.nc, self.TL, self.NB
        m = ar.mark()
        a2 = self.xbf_words
        ar2 = Arena.__new__(Arena)
        ar2.t, ar2.words, ar2.top, ar2.peak = self.ar.t, a2[1], a2[0], a2[0]
        def a2(shape, dt=F32):
            n = 1
            for s_ in shape:
                n *= s_
            w = n if dt == F32 else (n + 1) // 2
            return ar2.alloc(shape, dt) if ar2.top + w <= ar2.words else ar.alloc(shape, dt)
        self.catc = a2([2, self.TT], BF16)
        KR = [a2([128], BF16) for _ in range(8)]
        VR = [a2([2, 128], BF16) for _ in range(8)]
        PR = [a2([512], BF16) for _ in range(4)]
        YH = [a2([512], BF16) for _ in range(2)]
        TMPn = [a2([128]) for _ in range(2)]
        self.BND = a2([3, 4, 128])
        for i in range(3):
            for h in range(4):
                self.ts("dve", self.BND[:, i, h, :], self.T01[:, h, 128:256], self.CC[:, 4 + i:5 + i], ALU.mult,
                        ["T01", "CC", "COLV"], ["BND"], s2=self.COLV[:, i, h:h + 1], op1=ALU.add)
        for v in VR:
            self.memset("pool", v, 0.0, ["vr_init"])
            self.memset("pool", v[:, :, 64:65], 1.0, ["vr_init"])
        self.B.barrier()
        OS = [ar.alloc([512]) for _ in range(2)]
        RL = ar.alloc([512])
        SQ = ar.alloc([512])
        nq = self.nq
        hb = NB // 2

        def kview(t, base):
            return t.ap().rearrange("r c -> (r c)")[base:base + nq].rearrange("(p t) -> p t", p=128)

        def vview(t, base):
            return t.ap().rearrange("r c -> (r c)")[base:base + nq].rearrange("(b p c) -> p b c", p=128, c=256)

        remote = [([kview(self.g_kv_out[l][c_], i * nq) for c_ in range(2)],
                   [vview(self.g_kv_out[l][2 + q], i * nq) for q in range(2)]) for i in range(3)]
        local = ([kview(self.g_kv_in[l][c_], 0) for c_ in range(2)], [vview(self.g_kv_in[l][2 + q], 0) for q in range(2)])
        ring = 0
        pr = 0
        sc = 0
        for c in range(2):
            for qt in range(self.NT):
                q0 = qt * 512
                blocks = [("r", i, kb) for i in range(3) for kb in range(NB)] + [("l", 0, kb) for kb in range(4 * qt + 4)]
                nblk = len(blocks)
                for bi, (kind, i, kb) in enumerate(blocks):
                    gk, gv = remote[i] if kind == "r" else local
                    rk = ring % 8
                    ring += 1
                    pre = "gkvo%d" if kind == "r" else "gkvi%d"
                    self.ld(KR[rk], gk[c][:, kb * 128:(kb + 1) * 128], writes=["KR%d" % rk], reads=[pre % c])
                    self.ld(VR[rk][:, :, 0:64], gv[kb // hb][:, kb % hb, c * 128:(c + 1) * 128].rearrange("p (h d) -> p h d", h=2),
                            writes=["VR%d" % rk], reads=[pre % (2 + kb // hb)])
                    for hh in range(2):
                        h = 2 * c + hh
                        hp = slice(hh * 64, (hh + 1) * 64)
                        for mm_ in range(2):
                            ps = self.PS[sc % 4]
                            pk = "ps%d" % (sc % 4)
                            sc += 1
                            self.mm(ps[:, :512], KR[rk][hp, :], self.QZ[hp, c, mm_, q0:q0 + 512], True, True,
                                    ["KR%d" % rk, "QZ.%d" % c], [pk])
                            pt = PR[pr % 4]
                            ptk = "PR%d" % (pr % 4)
                            pr += 1
                            if kind == "r":
                                lo = 0
                                if qt == 0 and kb == NB - 1:
                                    lo = 128
                                    tmp = TMPn[pr % 2]
                                    self.stt(tmp, ps[:, 0:128], QSCALE, self.BND[:, i, h, :], ALU.mult, ALU.add,
                                             [pk, "BND"], ["TMPn%d" % (pr % 2)])
                                    self.act(pt[:, 0:128], tmp, AF.Exp, ["TMPn%d" % (pr % 2)], [ptk])
                                self.act(pt[:, lo:512], ps[:, lo:512], AF.Exp, [pk, "BC"], [ptk], scale=QSCALE,
                                         bias=self.BC[:, i, h:h + 1])
                            else:
                                r0 = kb - 4 * qt
                                if r0 > 0:
                                    self.memset("pool", pt[:, 0:r0 * 128], 0.0, [ptk])
                                for (r, off) in ((r0, 0), (r0 + 1, 128)):
                                    if 0 <= r <= 3:
                                        tmp = TMPn[(pr + r) % 2]
                                        tk = "TMPn%d" % ((pr + r) % 2)
                                        self.stt(tmp, ps[:, r * 128:(r + 1) * 128], QSCALE, self.T01[:, h, off:off + 128],
                                                 ALU.mult, ALU.add, [pk, "T01"], [tk])
                                        self.act(pt[:, r * 128:(r + 1) * 128], tmp, AF.Exp, [tk], [ptk])
                                lo = max(r0 + 2, 0) * 128
                                if lo < 512:
                                    self.act(pt[:, lo:512], ps[:, lo:512], AF.Exp, [pk, "RB"], [ptk], scale=QSCALE,
                                             bias=self.CH[:, h:h + 1])
                            ob = 4 + hh * 2 + mm_
                            self.mm(self.PS[ob][:, :512], VR[rk][:, hh, :], pt, bi == 0, bi == nblk - 1,
                                    ["VR%d" % rk, ptk], ["ps%d" % ob])
                for hh in range(2):
                    for mm_ in range(2):
                        ob = 4 + hh * 2 + mm_
                        self.cp("act", OS[mm_][0:65, :], self.PS[ob][0:65, :512], ["ps%d" % ob], ["OS%d" % mm_])
                        self.mm(self.PS[0][0:64, :512], self.e65[0:65, :], OS[mm_][0:65, :], True, True,
                                ["e65", "OS%d" % mm_], ["ps0"])
                        self.B.op("dve", lambda: nc.vector.reciprocal(out=RL[0:64, :], in_=self.PS[0][0:64, :512]),
                                  ["ps0"], ["RL"])
                        self.tt("dve", OS[mm_][0:64, :], OS[mm_][0:64, :], RL[0:64, :], ALU.mult, ["OS%d" % mm_, "RL"],
                                ["OS%d" % mm_])
                    A_ = OS[0]
                    self.stt(A_[0:64, :], OS[1][0:64, :], P["nlam"][0:64, 0:1], OS[0][0:64, :], ALU.mult, ALU.add,
                             ["OS0", "OS1", "nlam"], ["OS0"])
                    self.act(SQ[0:64, :], A_[0:64, :], AF.Square, ["OS0"], ["aSQ"])
                    self.mm(self.PS[1][0:64, :512], self.ones64[0:64, :], SQ[0:64, :], True, True, ["ones64", "aSQ"], ["ps1"])
                    self.act(SQ[0:64, :], self.PS[1][0:64, :512], AF.Sqrt, ["ps1", "aSQ"], ["aSQ"], bias=EPS)
                    self.B.op("dve", lambda: nc.vector.reciprocal(out=SQ[0:64, :], in_=SQ[0:64, :]), ["aSQ"], ["aSQ"])
                    self.stt(YH[hh][0:64, :], A_[0:64, :], P["subg"][0:64, 0:1], SQ[0:64, :], ALU.mult, ALU.mult,
                             ["OS0", "subg", "aSQ"], ["YH%d" % hh])
                for hh in range(2):
                    self.mm(self.PS[2][:, :512], self.sh64[0:64, hh, :], YH[hh][0:64, :], hh == 0, hh == 1,
                            ["sh64", "YH%d" % hh], ["ps2"])
                self.cp("act", self.catc[:, c, q0:q0 + 512], self.PS[2][:, :512], ["ps2"], ["catc.%d" % c])
        self.B.barrier()
        self.B.barrier()
        ar.release(m)

    def wout(self, l, P):
        srcs = [(self.cat["a"], "cata.%d"), (self.cat["b"], "catb.%d"), (self.catc, "catc.%d"), (self.cat["d"], "catd.%d")]
        cy = 0
        for dc in range(KC):
            buf, wk = self.ws_next(("wout", l, dc))
            slab = self.slabv(buf)
            for ti, (t0, tn) in enumerate(self.tiles):
                ps = self.PS[4 + cy % 2]
                pk = "ps%d" % (4 + cy % 2)
                cy += 1
                for e in range(KC):
                    if ti < self.NT:
                        src, kf = srcs[e // 2]
                        rhs = src[:, e % 2, t0:t0 + tn]
                        rk = kf % (e % 2)
                    else:
                        rhs = self.cats[:, e, :]
                        rk = "cats"
                    self.mm(ps[:, :tn], slab[:, e, :], rhs, e == 0, e == KC - 1, [wk, rk], [pk])
                xr = self.xres[:, dc, t0:t0 + tn]
                self.tt("dve", xr, ps[:, :tn], xr, ALU.add, [pk, "rx.%d.%d" % (dc, ti)], ["rx.%d.%d" % (dc, ti)])


_PROG_CACHE = {}


def get_prog(TL, DEPTH, n_pool_pages, stages=99, dbg=False):
    key = (TL, DEPTH, n_pool_pages, stages, dbg)
    if key not in _PROG_CACHE:
        _PROG_CACHE[key] = Prog(TL, DEPTH, n_pool_pages, stages, dbg)
    return _PROG_CACHE[key]


def host_consts(TL, core):
    j = core % 4
    c = {}
    c["c_ident"] = np.eye(128, dtype=np.float32)
    k = np.arange(128)[:, None]
    q = np.arange(128)[None, :]
    c["c_maskc"] = (k <= q).astype(np.float32)
    idx0 = np.where(q >= k, rel_bucket_np(q - k), -1).astype(np.float32)
    idx1 = rel_bucket_np(128 + q - k).astype(np.float32)
    c["c_idx01"] = np.concatenate([idx0, idx1], axis=1)
    p = np.arange(128)
    pos = (p // 8)[:, None] * 128 + (p % 8)[:, None] * 16 + np.arange(16)[None, :]
    c["c_idxs"] = rel_bucket_np(PAST - pos).astype(np.float32)
    segm = np.ones((128, 512), np.float32)
    segm[:, 0::64] = 0.0
    c["c_mask2"] = ((k <= q) & ((k // 64) == (q // 64))).astype(np.float32)
    c["c_segm"] = segm
    cc = np.zeros((1, 32), np.float32)
    cc[0, 0:4] = [1.0 if i == j else 0.0 for i in range(4)]
    cc[0, 4:8] = [1.0 if i == j - 1 else 0.0 for i in range(4)]
    cc[0, 8:12] = [1.0 if i < j else 0.0 for i in range(4)]
    cc[0, 12:16] = [1.0 if i < j - 1 else 0.0 for i in range(4)]
    c["c_core"] = cc
    invc = np.zeros((128, 2, 16), np.float32)
    for ch in range(2):
        for half in range(2):
            win = 2 ** (ch * 2 + half + 1)
            for t in range(16):
                cntv = min(t + 1, win) if j == 0 else win
                invc[half * 64:(half + 1) * 64, ch, t] = 1.0 / cntv
    c["c_invc"] = invc.reshape(128, 32)
    sel = np.zeros((16, 16, 128), np.float32)
    for n in range(16):
        sel[n, n, :] = 1.0
    c["c_sel16"] = sel.reshape(16, 16 * 128)
    misc = np.zeros((128, 64), np.float32)
    misc[:, 0] = np.arange(128) % 8
    c["c_misc"] = misc
    c8 = np.zeros((8, 514), np.float32)
    for jj in range(8):
        c8[jj, 0] = 1.0 if jj % 2 == 0 else 0.0
        c8[jj, 1] = 1.0 if jj % 2 == 1 else 0.0
        hh_ = jj // 2
        c8[jj, 2 + hh_ * 64:2 + (hh_ + 1) * 64] = 1.0
    oh = np.zeros((8, 16, 16), np.float32)
    for n_ in range(16):
        oh[:, n_, n_] = 1.0
    c8[:, 258:514] = oh.reshape(8, 256)
    c["c_c8"] = c8
    return c


def kernel(x_prompt, x_sample, state_pool, state_hgrn, cache_k, cache_v, page_table,
           rel_bias, ln_g, ln_b, ffn1_w_gu, ffn1_w_dn, ffn2_w_gu, ffn2_w_dn, w_in, w_out,
           pool_w, pool_scale, hgrn_lb, hgrn_norm_g, diff_lam_q1, diff_lam_k1,
           diff_lam_q2, diff_lam_k2, diff_subln_g, sgu_ln_g, sgu_ln_b, sgu_w, sgu_b,
           _stages=99, _dbg=False):
    f = lambda a: np.ascontiguousarray(np.asarray(a))
    x_prompt = f(x_prompt)
    Bp, T, _ = x_prompt.shape
    L = ln_g.shape[0]
    TL = T // 4
    n_pool = cache_k.shape[1]
    prog = get_prog(TL, L, n_pool, _stages, _dbg)
    shared = {

        "rel_bias": f(rel_bias).reshape(1, 128),
        "ln_g": f(ln_g).reshape(L * 3 * KC, 128), "ln_b": f(ln_b).reshape(L * 3 * KC, 128),
        "ffn1_w_gu": f(ffn1_w_gu), "ffn1_w_dn": f(ffn1_w_dn), "ffn2_w_gu": f(ffn2_w_gu), "ffn2_w_dn": f(ffn2_w_dn),
        "w_in": f(w_in), "w_out": f(w_out), "pool_w": f(pool_w), "pool_scale": f(pool_scale),
        "hgrn_lb": f(hgrn_lb).reshape(1, L * 256), "hgrn_norm_g": f(hgrn_norm_g),
        "diff_lam_q1": f(diff_lam_q1), "diff_lam_k1": f(diff_lam_k1), "diff_lam_q2": f(diff_lam_q2),
        "diff_lam_k2": f(diff_lam_k2), "diff_subln_g": f(diff_subln_g), "sgu_ln_g": f(sgu_ln_g),
        "sgu_ln_b": f(sgu_ln_b), "sgu_w": f(sgu_w), "sgu_b": f(sgu_b),
    }
    ck = f(cache_k).reshape(L, n_pool * 8, 4096)
    cv = f(cache_v).reshape(L, n_pool * 8, 4096)
    for l in range(L):
        shared["cache_k%d" % l] = ck[l]
        shared["cache_v%d" % l] = cv[l]
    xs = f(x_sample).reshape(8, NS, D_MODEL)
    sp = f(state_pool).reshape(L, 8, NS, 15 * 256)
    sh = f(state_hgrn).reshape(L, 8, NS, 4, 4096)
    pt = f(page_table).astype(np.int32).reshape(8, NS, NPG)
    in_maps = []
    for c in range(8):
        b, j = c // 4, c % 4
        m = dict(shared)
        m["xp"] = x_prompt[b, j * TL:(j + 1) * TL]
        m["xs"] = xs[c]
        m["spool"] = np.ascontiguousarray(sp[:, c])
        m["shgrn"] = np.ascontiguousarray(sh[:, c])
        m["pt"] = pt[c]
        m.update(host_consts(TL, c))
        in_maps.append(m)
    res = run_bass_kernel_spmd(prog.nc, in_maps, core_ids=list(range(8)))
    R = res.results
    y = np.stack([np.concatenate([R[b * 4 + j]["o_y"] for j in range(4)], 0) for b in range(2)], 0)
    ys = np.concatenate([R[c]["o_ys"] for c in range(8)], 0).reshape(8 * NS, 1, D_MODEL)
    pool_p = np.stack([R[b * 4 + 3]["o_poolp"] for b in range(2)], 1)
    pool_s = np.concatenate([R[c]["o_pools"].reshape(L, NS, 15, 256) for c in range(8)], 1)
    hg_p = np.stack([R[b * 4 + 3]["o_hgp"] for b in range(2)], 1)
    hg_s = np.concatenate([R[c]["o_hgs"].reshape(L, NS, 4, 64, 64) for c in range(8)], 1)
    k_p = np.stack([np.concatenate([R[b * 4 + j]["o_kp"] for j in range(4)], 1) for b in range(2)], 1).reshape(L, 2, T, 4, 64)
    k_s = np.concatenate([R[c]["o_ks"] for c in range(8)], 1).reshape(L, 8 * NS, 1, 4, 64)
    v_p = np.stack([np.concatenate([R[b * 4 + j]["o_vp"] for j in range(4)], 1) for b in range(2)], 1).reshape(L, 2, T, 4, 64)
    v_s = np.concatenate([R[c]["o_vs"] for c in range(8)], 1).reshape(L, 8 * NS, 1, 4, 64)
    sg_p = np.stack([R[b * 4 + 3]["o_sgp"] for b in range(2)], 1)
    sg_s = np.concatenate([R[c]["o_sgs"] for c in range(8)], 1).reshape(L, 8 * NS, 1, 256)
    return (y, ys, pool_p, pool_s, hg_p, hg_s, k_p, k_s, v_p, v_s, sg_p, sg_s)
```
